# Optimizing a Trainium2 kernel written in Bass

```python
import math
import jax, jax.numpy as jnp
from jax import lax
import numpy as np

D_MODEL = 1024
BATCH = 8
SEQ = 2048
DEPTH = 2

HEAD_DIM = 64
SWA_HEADS = (D_MODEL // 2) // HEAD_DIM
SWA_KV_HEADS = SWA_HEADS // 4
SWA_GROUP = SWA_HEADS // SWA_KV_HEADS
WINDOW = 128
SWA_BLOCK = 128
SB_HEADS = (D_MODEL // 2) // HEAD_DIM
SB_BLOCK = 128
HG_DK = 128
HG_WIDTH = D_MODEL // 2
HG_HEADS = HG_WIDTH // HG_DK
HG_DV = HG_WIDTH // HG_HEADS
HG_CHUNK = 64
CONV_WIDTH = D_MODEL // 2
CONV_K = 3
D_FF = 2816
EPS = 1e-6

N_EVEN = (DEPTH + 1) // 2
N_ODD = DEPTH // 2
SWA_Q = SWA_HEADS * HEAD_DIM
SWA_KV = SWA_KV_HEADS * HEAD_DIM
SB_W = SB_HEADS * HEAD_DIM
ATTN_IN = SWA_Q + 2 * SWA_KV + 3 * SB_W
ATTN_OUT = SWA_Q + SB_W
REC_IN = 4 * HG_WIDTH + 3 * CONV_WIDTH
REC_OUT = HG_HEADS * HG_DV + CONV_WIDTH

kernel_name = "hybrid_swa_stickbreak_hgrn2_shortconv_macaron"

F32 = jnp.float32


def split_cols(t, sizes):
    return jnp.split(t, np.cumsum(sizes)[:-1].tolist(), axis=-1)


def rms_norm(x, g):
    x32 = x.astype(F32)
    y = x32 * lax.rsqrt(jnp.mean(x32 * x32, axis=-1, keepdims=True) + EPS)
    return (y * g.astype(F32)).astype(x.dtype)


def swiglu(x, w_in, w_out):
    gate, up = jnp.split(x @ w_in, 2, axis=-1)
    return (jax.nn.silu(gate) * up) @ w_out


def alibi_slopes(n_heads):
    return jnp.exp2(-8.0 * jnp.arange(1, n_heads + 1, dtype=F32) / n_heads)


def sliding_window_attention(q, k, v, sinks):
    b, s, _, dh = q.shape
    nb = s // SWA_BLOCK
    qb = q.reshape(b, nb, SWA_BLOCK, SWA_KV_HEADS, SWA_GROUP, dh)

    def band(t):
        tb = t.reshape(b, nb, SWA_BLOCK, SWA_KV_HEADS, dh)
        prev = jnp.concatenate([jnp.zeros_like(tb[:, :1]), tb[:, :-1]], axis=1)
        return jnp.concatenate([prev, tb], axis=2)

    kb, vb = band(k), band(v)
    scores = jnp.einsum('bnqhgd,bnkhd->bnhgqk', qb, kb).astype(F32) * (dh ** -0.5)
    qi = jnp.arange(SWA_BLOCK)[:, None]
    kj = jnp.arange(2 * SWA_BLOCK)[None, :]
    dist = qi + SWA_BLOCK - kj
    key_pos = jnp.arange(nb)[:, None, None] * SWA_BLOCK - SWA_BLOCK + kj
    valid = (dist >= 0) & (dist < WINDOW) & (key_pos >= 0)
    slopes = alibi_slopes(SWA_HEADS).reshape(SWA_KV_HEADS, SWA_GROUP)
    scores = scores - slopes[:, :, None, None] * dist.astype(F32)
    scores = jnp.where(valid[None, :, None, None], scores, -jnp.inf)
    sink = sinks.astype(F32).reshape(SWA_KV_HEADS, SWA_GROUP)[None, None, :, :, None, None]
    m = jnp.maximum(jnp.max(scores, axis=-1, keepdims=True), sink)
    p = jnp.exp(scores - m)
    probs = p / (jnp.sum(p, axis=-1, keepdims=True) + jnp.exp(sink - m))
    o = jnp.einsum('bnhgqk,bnkhd->bnqhgd', probs.astype(v.dtype), vb)
    return o.reshape(b, s, SWA_HEADS * dh)


def stick_breaking_attention(q, k, v):
    b, s, h, dh = q.shape
    outs = []
    for n in range(s // SB_BLOCK):
        start, end = n * SB_BLOCK, (n + 1) * SB_BLOCK
        z = jnp.einsum('bqhd,bkhd->bhqk', q[:, start:end], k[:, :end]).astype(F32) * (dh ** -0.5)
        t_pos = start + jnp.arange(SB_BLOCK)[:, None]
        s_pos = jnp.arange(end)[None, :]
        causal = s_pos < t_pos
        log_keep = jnp.where(causal, -jax.nn.softplus(z), 0.0)
        suffix = lax.cumsum(log_keep, axis=3, reverse=True)
        between = jnp.concatenate([suffix[..., 1:], jnp.zeros_like(suffix[..., :1])], axis=-1)
        w = jnp.where(causal, jnp.exp(jax.nn.log_sigmoid(z) + between), 0.0)
        outs.append(jnp.einsum('bhqk,bkhd->bqhd', w.astype(v.dtype), v[:, :end]))
    return jnp.concatenate(outs, axis=1).reshape(b, s, h * dh)


def hgrn_lower_bounds(logits):
    c = jnp.cumsum(jax.nn.softmax(logits.astype(F32), axis=0), axis=0)
    return c - c[0:1]


def hgrn2(q, f_logit, i, lb):
    b, s, h, dk = q.shape
    nc = s // HG_CHUNK
    z = f_logit.astype(F32)
    lb = lb.astype(F32)
    log_f = jnp.log(lb + (1.0 - lb) * jax.nn.sigmoid(z))
    key = (1.0 - lb) * jax.nn.sigmoid(-z)
    qf = jax.nn.silu(q.astype(F32))
    vf = i.astype(F32)

    def chunks(t):
        return t.reshape(b, nc, HG_CHUNK, h, t.shape[-1]).transpose(1, 0, 3, 2, 4)

    qc, kc, vc = chunks(qf), chunks(key), chunks(vf)
    gc = lax.cumsum(chunks(log_f), axis=3)
    causal = jnp.tril(jnp.ones((HG_CHUNK, HG_CHUNK), dtype=bool))[:, :, None]

    def step(state, inp):
        qt, kt, vt, gt = inp
        o_inter = jnp.einsum('bhtd,bhde->bhte', qt * jnp.exp(gt), state)
        diff = gt[:, :, :, None, :] - gt[:, :, None, :, :]
        decay = jnp.exp(jnp.where(causal, diff, -jnp.inf))
        scores = jnp.einsum('bhtd,bhsd,bhtsd->bhts', qt, kt, decay)
        o_intra = jnp.einsum('bhts,bhse->bhte', scores, vt)
        g_last = gt[:, :, -1:, :]
        state = (jnp.exp(g_last)[:, :, 0, :, None] * state
                 + jnp.einsum('bhsd,bhse->bhde', kt * jnp.exp(g_last - gt), vt))
        return state, o_inter + o_intra

    init = jnp.zeros((b, h, dk, vf.shape[-1]), F32)
    _, o = lax.scan(step, init, (qc, kc, vc, gc))
    return o.transpose(1, 0, 3, 2, 4).reshape(b, s, h, vf.shape[-1])


def short_conv(u, w):
    return lax.conv_general_dilated(u, w[:, None, :].astype(u.dtype), window_strides=(1,),
                                    padding=[(CONV_K - 1, 0)],
                                    dimension_numbers=('NWC', 'WIO', 'NWC'),
                                    feature_group_count=u.shape[-1])


def attention_mixer(h, w_in, sinks, w_out):
    b, s, _ = h.shape
    q_a, k_a, v_a, q_b, k_b, v_b = split_cols(h @ w_in, (SWA_Q, SWA_KV, SWA_KV, SB_W, SB_W, SB_W))
    o_a = sliding_window_attention(q_a.reshape(b, s, SWA_HEADS, HEAD_DIM),
                                   k_a.reshape(b, s, SWA_KV_HEADS, HEAD_DIM),
                                   v_a.reshape(b, s, SWA_KV_HEADS, HEAD_DIM), sinks)
    o_b = stick_breaking_attention(q_b.reshape(b, s, SB_HEADS, HEAD_DIM),
                                   k_b.reshape(b, s, SB_HEADS, HEAD_DIM),
                                   v_b.reshape(b, s, SB_HEADS, HEAD_DIM))
    return jnp.concatenate([o_a.astype(h.dtype), o_b.astype(h.dtype)], axis=-1) @ w_out


def recurrent_conv_mixer(h, w_in, lb, out_norm_g, conv_w, w_out):
    b, s, _ = h.shape
    q, f, i, g, gate_b, gate_c, u = split_cols(
        h @ w_in, (HG_WIDTH, HG_WIDTH, HG_WIDTH, HG_WIDTH, CONV_WIDTH, CONV_WIDTH, CONV_WIDTH))
    hs = (b, s, HG_HEADS, HG_DK)
    o_c = hgrn2(q.reshape(hs), f.reshape(hs), i.reshape(b, s, HG_HEADS, HG_DV),
                lb.reshape(HG_HEADS, HG_DK))
    o_c = rms_norm(o_c, out_norm_g).reshape(b, s, HG_HEADS * HG_DV) * jax.nn.silu(g.astype(F32))
    o_d = gate_b * short_conv(gate_c * u, conv_w)
    return jnp.concatenate([o_c.astype(h.dtype), o_d.astype(h.dtype)], axis=-1) @ w_out


def setup_inputs(seed: int = 0) -> dict:
    key = jax.random.key(seed)
    ks = jax.random.split(key, 13)

    def w(k, shape, fan_in):
        return jax.random.normal(k, shape, F32) * (fan_in ** -0.5)

    def gain(k, shape):
        return 1.0 + 0.02 * jax.random.normal(k, shape, F32)

    return {
        "x": jax.random.normal(ks[0], (BATCH, SEQ, D_MODEL), F32),
        "norm_g": gain(ks[1], (DEPTH, 3, D_MODEL)),
        "ffn_w_in": w(ks[2], (DEPTH, 2, D_MODEL, 2 * D_FF), D_MODEL),
        "ffn_w_out": w(ks[3], (DEPTH, 2, D_FF, D_MODEL), D_FF),
        "attn_w_in": w(ks[4], (N_EVEN, D_MODEL, ATTN_IN), D_MODEL),
        "attn_sinks": 0.5 * jax.random.normal(ks[5], (N_EVEN, SWA_HEADS), F32),
        "attn_w_out": w(ks[6], (N_EVEN, ATTN_OUT, D_MODEL), ATTN_OUT),
        "rec_w_in": w(ks[7], (N_ODD, D_MODEL, REC_IN), D_MODEL),
        "hgrn_lb_logits": jax.random.normal(ks[8], (DEPTH, HG_WIDTH), F32),
        "hgrn_norm_g": gain(ks[9], (N_ODD, HG_DV)),
        "conv_w": w(ks[10], (N_ODD, CONV_K, CONV_WIDTH), CONV_K),
        "rec_w_out": w(ks[11], (N_ODD, REC_OUT, D_MODEL), REC_OUT),
        "final_g": gain(ks[12], (D_MODEL,)),
    }


def reference(x, norm_g, ffn_w_in, ffn_w_out, attn_w_in, attn_sinks, attn_w_out,
              rec_w_in, hgrn_lb_logits, hgrn_norm_g, conv_w, rec_w_out, final_g):
    lower_bounds = hgrn_lower_bounds(hgrn_lb_logits)
    h = x
    for layer in range(DEPTH):
        h = h + 0.5 * swiglu(rms_norm(h, norm_g[layer, 0]), ffn_w_in[layer, 0], ffn_w_out[layer, 0])
        hn = rms_norm(h, norm_g[layer, 1])
        if layer % 2 == 0:
            e = layer // 2
            mix = attention_mixer(hn, attn_w_in[e], attn_sinks[e], attn_w_out[e])
        else:
            o = layer // 2
            mix = recurrent_conv_mixer(hn, rec_w_in[o], lower_bounds[layer], hgrn_norm_g[o],
                                       conv_w[o], rec_w_out[o])
        h = h + mix
        h = h + 0.5 * swiglu(rms_norm(h, norm_g[layer, 2]), ffn_w_in[layer, 1], ffn_w_out[layer, 1])
    return rms_norm(h, final_g)
```

```python
import os
import numpy as np
import ml_dtypes
import concourse.bass as bass
import concourse.mybir as mybir
from concourse.bass_utils import run_bass_kernel_spmd
from contextlib import ExitStack

F32 = mybir.dt.float32
BF16 = mybir.dt.bfloat16
ALU = mybir.AluOpType
AF = mybir.ActivationFunctionType

S = 2048
D = 1024
KT = 8
NT = 4
NB = 16
DFF = 2816
NJ = 22
EPS = 1e-6


class View:
    __slots__ = ("ap", "tid", "rects")

    def __init__(self, ap, tid, rects):
        self.ap = ap
        self.tid = tid
        self.rects = rects

    def with_ap(self, ap):
        return View(ap, self.tid, self.rects)


class Tile:
    def __init__(self, handle, shape, tid, base=0, isz=4, p0=0, gran=None):
        self.h = handle
        self.shape = tuple(shape)
        self.tid = tid
        self.base = base
        self.isz = isz
        self.p0 = p0
        self.gran = gran

    def __getitem__(self, idx):
        if not isinstance(idx, tuple):
            idx = (idx,)
        idx = idx + (slice(None),) * (len(self.shape) - len(idx))
        ap = self.h[idx]
        if self.tid is None:
            return View(ap, None, [])
        rng = []
        for ix, n in zip(idx, self.shape):
            if isinstance(ix, int):
                rng.append((ix, ix + 1))
            else:
                s, e, st = ix.indices(n)
                if st != 1:
                    e = s + ((e - s + st - 1) // st - 1) * st + 1
                rng.append((s, e))
        ivs = [(0, 1)]
        size = 1
        for (s, e), n in zip(reversed(rng[1:]), reversed(self.shape[1:])):
            if len(ivs) == 1 and ivs[0] == (0, size):
                ivs = [(s * size, e * size)]
            else:
                ivs = [(a * size + lo, a * size + hi) for a in range(s, e) for (lo, hi) in ivs]
            size *= n
        b, z = self.base, self.isz
        rects = [(self.p0 + rng[0][0], self.p0 + rng[0][1], b + lo * z, b + hi * z) for lo, hi in ivs]
        if self.gran is not None:
            gb, gp = self.gran
            rr = set()
            for (pa, pb, lo, hi) in rects:
                rr.add((pa // gp * gp, (pb + gp - 1) // gp * gp, lo // gb * gb, (hi + gb - 1) // gb * gb))
            rects = sorted(rr)
        return View(ap, self.tid, rects)


class Arena:
    def __init__(self, k, handle, nbytes, gran=None):
        self.h = handle
        self.nbytes = nbytes
        self.tid = k._new_tid()
        self.top = 0
        self.peak = 0
        self.gran = gran

    def mark(self):
        return self.top

    def release(self, m):
        self.top = m

    def tile(self, shape, dtype, off=None):
        isz = 4 if dtype == F32 else 2
        n = 1
        for d in shape[1:]:
            n *= d
        nb = (n * isz + 31) // 32 * 32
        if off is None:
            off = self.top
            self.top += nb
            self.peak = max(self.peak, self.top)
            assert self.top <= self.nbytes, ("arena overflow", self.top, self.nbytes)
        ap = self.h[:, off // 4:(off + nb) // 4]
        if dtype != F32:
            ap = ap.bitcast(dtype)
        ap = ap[0:shape[0], 0:n]
        if len(shape) == 3:
            ap = ap.rearrange("p (a b) -> p a b", a=shape[1], b=shape[2])
        elif len(shape) == 4:
            ap = ap.rearrange("p (a b c) -> p a b c", a=shape[1], b=shape[2], c=shape[3])
        return Tile(ap, shape, self.tid, base=off, isz=isz, gran=self.gran)


class Ins:
    __slots__ = ("eng", "fn", "deps", "signal", "count", "is_dma", "sem", "_barred", "epoch", "idx")

    def __init__(self, eng, fn, is_dma=False):
        self.epoch = 0
        self.idx = 0
        self.eng = eng
        self.fn = fn
        self.deps = []
        self.signal = False
        self.count = 0
        self.is_dma = is_dma
        self.sem = None
        self._barred = False


ENGS = ("pe", "act", "dve", "pool", "sp")


class K:
    def __init__(self, nc):
        self.nc = nc
        self.streams = {e: [] for e in ENGS}
        self.acc = {}
        self._tid = 0
        self.dma_keys = {}
        self.n_ins = 0
        self.psum_tid = None
        self.epoch = 0

    def _new_tid(self):
        self._tid += 1
        return self._tid

    @staticmethod
    def _ov(a, b):
        return a[0] < b[1] and b[0] < a[1] and a[2] < b[3] and b[2] < a[3]

    @staticmethod
    def _cover(a, b):
        return a[0] <= b[0] and a[1] >= b[1] and a[2] <= b[2] and a[3] >= b[3]

    def _track(self, ins, reads, writes):
        deps = {}
        ov = self._ov
        for v in reads:
            if v is None or v.tid is None:
                continue
            lst = self.acc.setdefault(v.tid, [])
            for r in v.rects:
                for (rect, j, kind) in lst:
                    if kind == "w" and ov(rect, r):
                        deps[id(j)] = j
        for v in writes:
            if v is None or v.tid is None:
                continue
            lst = self.acc.setdefault(v.tid, [])
            for r in v.rects:
                for (rect, j, kind) in lst:
                    if ov(rect, r):
                        deps[id(j)] = j
        for v in reads:
            if v is None or v.tid is None:
                continue
            lst = self.acc[v.tid]
            for r in v.rects:
                if not ins.is_dma:
                    lst[:] = [e for e in lst if not (e[2] == "r" and e[0] == r and e[1].eng == ins.eng and not e[1].is_dma)]
                lst.append((r, ins, "r"))
        for v in writes:
            if v is None or v.tid is None:
                continue
            lst = self.acc[v.tid]
            for r in v.rects:
                lst[:] = [e for e in lst if not self._cover(r, e[0])]
                lst.append((r, ins, "w"))
        deps.pop(id(ins), None)
        for j in deps.values():
            if j.eng == "pe" and ins.eng == "pe" and not j.is_dma and not ins.is_dma:
                continue
            ins.deps.append(j)
            j.signal = True

    def new_epoch(self):
        self.epoch += 1

    def op(self, eng, fn, reads, writes):
        ins = Ins(eng, fn)
        ins.epoch = self.epoch
        self.n_ins += 1
        if self.psum_tid is not None:
            pr = [v for v in reads if v is not None and v.tid == self.psum_tid]
            if pr:
                reads = [v for v in reads if v is None or v.tid != self.psum_tid]
                writes = list(writes) + pr
        self._track(ins, reads, writes)
        self.streams[eng].append(ins)
        return ins

    def dma(self, queue, out, in_, key, **kw):
        ins = Ins(queue, None, is_dma=True)
        self.n_ins += 1
        kk = self.dma_keys.setdefault(key, [0])
        kk[0] += 16
        ins.sem = key
        ins.count = kk[0]
        oa, ia = out.ap, in_.ap
        ins.fn = lambda e: e.dma_start(out=oa, in_=ia, **kw)
        self._track(ins, [in_], [out])
        self.streams[queue].append(ins)
        return ins

    def barrier(self):
        lasts = []
        for e in ENGS:
            comp = [i for i in self.streams[e] if not i.is_dma and i.fn is not None]
            if comp:
                lasts.append(comp[-1])
            for i in self.streams[e]:
                if i.is_dma and not i._barred:
                    i._barred = True
                    lasts.append(i)
        for e in ENGS:
            ins = Ins(e, None)
            ins.epoch = self.epoch
            for j in lasts:
                if j.eng == e and not j.is_dma and e == "pe":
                    continue
                ins.deps.append(j)
                j.signal = True
            self.streams[e].append(ins)
        self.acc = {}

    def mm(self, out, lhsT, rhs, start=True, stop=True, **kw):
        o, l, r = out.ap, lhsT.ap, rhs.ap
        return self.op("pe", lambda e: e.matmul(o, l, r, start=start, stop=stop, **kw), [lhsT, rhs], [out])

    def transpose(self, out, in_, ident):
        o, i, d = out.ap, in_.ap, ident.ap
        return self.op("pe", lambda e: e.transpose(o, i, d), [in_, ident], [out])

    def act(self, out, in_, func, bias=None, scale=1.0, accum_out=None):
        o, i = out.ap, in_.ap
        kw = {}
        rd = [in_]
        wr = [out]
        if bias is not None:
            if isinstance(bias, View):
                kw["bias"] = bias.ap
                rd.append(bias)
            else:
                kw["bias"] = bias
        if isinstance(scale, View):
            kw["scale"] = scale.ap
            rd.append(scale)
        else:
            kw["scale"] = scale
        if accum_out is not None:
            kw["accum_out"] = accum_out.ap
            wr.append(accum_out)
        return self.op("act", lambda e: e.activation(o, i, func, **kw), rd, wr)

    def tt(self, eng, out, in0, in1, op):
        o, a, b = out.ap, in0.ap, in1.ap
        return self.op(eng, lambda e: e.tensor_tensor(o, a, b, op), [in0, in1], [out])

    def ts(self, eng, out, in0, s1, s2=None, op0=ALU.mult, op1=None):
        o, a = out.ap, in0.ap
        rd = [in0]
        if isinstance(s1, View):
            rd.append(s1)
            s1 = s1.ap
        if isinstance(s2, View):
            rd.append(s2)
            s2 = s2.ap
        kw = {}
        if op1 is not None:
            kw["op1"] = op1
        return self.op(eng, lambda e: e.tensor_scalar(o, a, s1, s2, op0, **kw), rd, [out])

    def stt(self, out, in0, scalar, in1, op0, op1):
        o, a, b = out.ap, in0.ap, in1.ap
        rd = [in0, in1]
        if isinstance(scalar, View):
            rd.append(scalar)
            scalar = scalar.ap
        return self.op("dve", lambda e: e.scalar_tensor_tensor(o, a, scalar, b, op0, op1), rd, [out])

    def copy(self, eng, out, in_):
        o, i = out.ap, in_.ap
        if eng == "act":
            return self.op("act", lambda e: e.copy(o, i), [in_], [out])
        return self.op(eng, lambda e: e.tensor_copy(o, i), [in_], [out])

    def scan(self, out, d0, d1, initial, op0, op1):
        o, a, b = out.ap, d0.ap, d1.ap
        rd = [d0, d1]
        if isinstance(initial, View):
            rd.append(initial)
            initial = initial.ap
        return self.op("dve", lambda e: e.tensor_tensor_scan(o, a, b, initial, op0, op1), rd, [out])

    def recip(self, out, in_):
        o, i = out.ap, in_.ap
        return self.op("dve", lambda e: e.reciprocal(o, i), [in_], [out])

    def memset(self, eng, out, val):
        o = out.ap
        return self.op(eng, lambda e: e.memset(o, val), [], [out])

    def emit(self):
        nc = self.nc
        with ExitStack() as st:
            esem = {}
            dsem = {}
            for key in self.dma_keys:
                dsem[key] = st.enter_context(nc.semaphore("d_%d" % len(dsem)))
            self.max_count = 0
            for e in ENGS:
                for n_, ins in enumerate(self.streams[e]):
                    ins.idx = n_
                    ins.signal = False
            for e in ENGS:
                for ins in self.streams[e]:
                    best = {}
                    keep = []
                    for j in ins.deps:
                        if j.is_dma:
                            keep.append(j)
                            continue
                        kk_ = (j.eng, j.epoch)
                        if kk_ not in best or best[kk_].idx < j.idx:
                            best[kk_] = j
                    for j in best.values():
                        j.signal = True
                        keep.append(j)
                    ins.deps = keep
            for e in ENGS:
                c = {}
                for ins in self.streams[e]:
                    if ins.is_dma:
                        continue
                    if ins.signal:
                        ek = (e, ins.epoch)
                        if ek not in esem:
                            esem[ek] = st.enter_context(nc.semaphore("s_%s_%d" % ek))
                        c[ek] = c.get(ek, 0) + 1
                        ins.count = c[ek]
                        self.max_count = max(self.max_count, ins.count)
            block = st.enter_context(nc.Block())
            engobj = {"pe": block.tensor, "act": block.scalar, "dve": block.vector, "pool": block.gpsimd, "sp": block.sync}

            def mk(ename):
                def body(eng):
                    waited = {}
                    for ins in self.streams[ename]:
                        need = {}
                        for j in ins.deps:
                            if j.is_dma:
                                s, c, wk = dsem[j.sem], j.count, ("d", j.sem)
                            else:
                                s, c, wk = esem[(j.eng, j.epoch)], j.count, ("e", j.eng, j.epoch)
                            if c > need.get(wk, (None, 0))[1]:
                                need[wk] = (s, c)
                        for wk, (s, c) in need.items():
                            if waited.get(wk, 0) >= c:
                                continue
                            waited[wk] = c
                            eng.wait_ge(s, c)
                        if ins.fn is None:
                            continue
                        r = ins.fn(eng)
                        if ins.is_dma:
                            r.then_inc(dsem[ins.sem], 16)
                        elif ins.signal:
                            r.then_inc(esem[(ename, ins.epoch)], 1)
                return body

            for e in ENGS:
                engobj[e](mk(e))


CF_E = 0
CF_SBM = 2048
CF_HGM = 2176
CF_RST = 2240
CF_N = 2752


def host_consts():
    cf = np.zeros((128, CF_N), np.float32)
    k = np.arange(128)[:, None].astype(np.float64)
    q = np.arange(128)[None, :].astype(np.float64)
    for h in range(8):
        slope = 2.0 ** (-(h + 1))
        cur = np.where(q >= k, np.exp(-slope * (q - k)), 0.0)
        prev = np.where(k > q, np.exp(-slope * (q + 128 - k)), 0.0)
        cf[:, CF_E + h * 256:CF_E + h * 256 + 128] = cur
        cf[:, CF_E + h * 256 + 128:CF_E + h * 256 + 256] = prev
    t = np.arange(128)[:, None]
    j = np.arange(128)[None, :]
    cf[:, CF_SBM:CF_SBM + 128] = (j < t)
    s = (np.arange(128) % 64)[:, None]
    tt = np.arange(64)[None, :]
    cf[:, CF_HGM:CF_HGM + 64] = (tt >= s)
    cf[:, CF_RST:CF_RST + 512] = (np.arange(512) % 64 != 0)[None, :]
    cb = np.zeros((128, 256), np.float32)
    cb[:, 0:128] = np.eye(128)
    cb[:, 128:256] = 1.0
    return cf, cb.astype(ml_dtypes.bfloat16)


VC_NG = 0
VC_FG = 48
VC_HN = 56
VC_CW = 57
VC_LB = 69
VC_SK = 77
VC_N = 88


def host_vecs(norm_g, final_g, hgrn_norm_g, conv_w, hgrn_lb_logits, attn_sinks):
    v = np.zeros((128, VC_N), np.float32)
    v[:, VC_NG:VC_NG + 48] = norm_g.reshape(6, 8, 128).transpose(2, 0, 1).reshape(128, 48)
    v[:, VC_FG:VC_FG + 8] = final_g.reshape(8, 128).T
    v[:, VC_HN] = hgrn_norm_g.reshape(128)
    v[:, VC_CW:VC_CW + 12] = conv_w.reshape(3, 4, 128).transpose(2, 1, 0).reshape(128, 12)
    v[:, VC_LB:VC_LB + 8] = hgrn_lb_logits.reshape(2, 4, 128).transpose(2, 1, 0).reshape(128, 8)
    v[:, VC_SK:VC_SK + 8] = attn_sinks.reshape(1, 8)
    return v


ALL_STAGES = ("ffn00", "attn", "ffn01", "ffn10", "rec", "ffn11", "final")
W_SHAPES = {"attn_w_in": [D, 2304], "attn_w_out": [D, D], "rec_w_in": [D, 3584], "rec_w_out": [D, D]}
for _l in range(2):
    for _f in range(2):
        W_SHAPES["ffn_w_in_%d%d" % (_l, _f)] = [D, 2 * DFF]
        W_SHAPES["ffn_w_out_%d%d" % (_l, _f)] = [DFF, D]


class Prog:
    def __init__(self, stages=ALL_STAGES):
        self.stages = tuple(stages)
        nc = self.nc = bass.Bass("TRN2", target_bir_lowering=False)
        k = self.k = K(nc)

        def din(name, shape, dt=F32):
            return Tile(nc.dram_tensor(name, list(shape), dt, kind="ExternalInput"), shape, None)

        self.xT = din("xT", [128, 8, S])
        self.cf = din("cf", [128, CF_N])
        self.cb = din("cb", [128, 256], BF16)
        self.vecs = din("vecs", [128, VC_N])
        self.wd = {}
        outh = nc.dram_tensor("outT", [128, 8, S], F32, kind="ExternalOutput")
        self.outT = Tile(outh, [128, 8, S], k._new_tid())

    def w(self, name):
        if name not in self.wd:
            self.wd[name] = self.nc.dram_tensor(name, W_SHAPES[name], F32, kind="ExternalInput")
        return self.wd[name]

    def build(self):
        nc, k = self.nc, self.k
        with ExitStack() as st:
            SB_BYTES = 212480
            ah = st.enter_context(nc.sbuf_tensor("arena", [128, SB_BYTES // 4], F32))
            ph = st.enter_context(nc.psum_tensor("psum", [128, 4096], F32))
            self.A = A = Arena(k, ah, SB_BYTES)
            self.PS = Arena(k, ph, 16384, gran=(2048, 32))
            k.psum_tid = self.PS.tid
            self.hT = A.tile([128, 8, S], F32)
            self.cfT = A.tile([128, CF_N], F32)
            self.cbT = A.tile([128, 256], BF16)
            self.vc = A.tile([128, VC_N], F32)
            self.epsT = A.tile([128, 2], F32)
            self.wring = [(A.tile([128, 8, 256], BF16), A.tile([128, 8, 256], BF16)) for _ in range(3)]
            self.ident = Tile(self.cbT.h[:, 0:128], [128, 128], self.cbT.tid, self.cbT.base, 2)
            self.ones = Tile(self.cbT.h[:, 128:256], [128, 128], self.cbT.tid, self.cbT.base + 256, 2)
            self.wslab_issued = 0

            for t in range(NT):
                k.dma("sp", self.hT[:, :, t * 512:(t + 1) * 512], self.xT[:, :, t * 512:(t + 1) * 512], ("x", t))
            k.dma("sp", self.cfT[:], self.cf[:], "const_cf")
            k.dma("sp", self.cbT[:], self.cb[:], "const_cb")
            k.dma("sp", self.vc[:], self.vecs[:], "const_vc")
            k.memset("pool", self.epsT[:, 0:1], EPS)
            k.memset("pool", self.epsT[:, 1:2], 1.0)

            self.body()

            k.barrier()
            k.emit()
        return nc

    def psb(self, b, shape=(128, 512), dtype=F32):
        return self.PS.tile(list(shape), dtype, off=b * 2048)

    def body(self):
        st = self.stages
        self.ffn_seq = [x for x in st if x.startswith("ffn")]
        for i, name in enumerate(st):
            self.k.new_epoch()
            if name.startswith("ffn"):
                fidx = self.ffn_seq.index(name)
                nxt_is_ffn = (i + 1 < len(st) and st[i + 1].startswith("ffn"))
                self.ffn_issue(limit=fidx * 11 + 3)
                self.ffn(int(name[3]), int(name[4]), fidx, next_limit=fidx * 11 + (14 if nxt_is_ffn else 11))
            elif name == "attn":
                self.attn()
            elif name == "rec":
                self.rec()
            elif name == "final":
                self.final_norm()
        if "final" not in st:
            self.dump_h()

    def dump_h(self):
        k = self.k
        for t in range(NT):
            k.dma("sp", self.outT[:, :, t * 512:(t + 1) * 512], self.hT[:, :, t * 512:(t + 1) * 512], ("out", t))

    def rms_norm_T(self, gcol0, xnT):
        k, A = self.k, self.A
        m = A.mark()
        sq = A.tile([128, 8, 512], BF16)
        rs = [A.tile([128, 512], F32) for _ in range(2)]
        ssq = self.psb(6)
        for t in range(NT):
            ts_ = slice(t * 512, (t + 1) * 512)
            k.act(sq[:], self.hT[:, :, ts_], AF.Square)
            for kt in range(KT):
                k.mm(ssq[:], self.ones[:], sq[:, kt, :], start=(kt == 0), stop=(kt == KT - 1))
            r = rs[t % 2]
            k.act(r[:], ssq[:], AF.Sqrt, bias=self.epsT[:, 0:1], scale=1.0 / D)
            k.recip(r[:], r[:])
            for kt in range(KT):
                k.stt(xnT[:, kt, ts_], self.hT[:, kt, ts_], self.vc[:, gcol0 + kt:gcol0 + kt + 1], r[:], ALU.mult, ALU.mult)
        A.release(m)

    def ffn_issue(self, limit):
        k = self.k
        while self.wslab_issued < limit and self.wslab_issued < 11 * len(self.ffn_seq):
            gi = self.wslab_issued
            fi, s = divmod(gi, 11)
            slot = self.wring[gi % 3]
            w = self.w("ffn_w_in_" + self.ffn_seq[fi][3:5]).ap().rearrange("(kt p) c -> p kt c", p=128)
            k.dma("pool", slot[0][:], View(w[:, :, s * 256:(s + 1) * 256], None, []), ("wr", gi % 3, 0))
            k.dma("pool", slot[1][:], View(w[:, :, DFF + s * 256:DFF + (s + 1) * 256], None, []), ("wr", gi % 3, 1))
            self.wslab_issued += 1

    def ffn(self, l, f, fi, next_limit):
        k, A = self.k, self.A
        m = A.mark()
        xnT = A.tile([128, 8, S], BF16)
        parts = [6, 6, 6, 4]
        actT = A.tile([128, 6, S], BF16)
        wout = A.tile([128, 6, D], BF16)
        sg = [A.tile([128, 512], F32) for _ in range(3)]
        self.rms_norm_T(VC_NG + (l * 3 + (0 if f == 0 else 2)) * 8, xnT)
        wo_dram = self.w("ffn_w_out_%d%d" % (l, f)).ap().rearrange("(j p) c -> p j c", p=128)
        pg = [self.psb(0), self.psb(1), self.psb(2)]
        pu = [self.psb(3), self.psb(4), self.psb(5)]
        po = [self.psb(0), self.psb(1), self.psb(2), self.psb(3)]
        j0 = 0
        cnt = 0
        cnt2 = 0
        for nj in parts:
            k.dma("pool", wout[:, 0:nj, :], View(wo_dram[:, j0:j0 + nj, :], None, []), "wout")
            for sl in range(nj // 2):
                s = j0 // 2 + sl
                gi = fi * 11 + s
                slot = self.wring[gi % 3]
                for jj in range(2):
                    jl = sl * 2 + jj
                    for t in range(NT):
                        ts_ = slice(t * 512, (t + 1) * 512)
                        g_ps, u_ps, sgt = pg[cnt % 3], pu[cnt % 3], sg[cnt % 3]
                        cnt += 1
                        for kt in range(KT):
                            k.mm(g_ps[:], slot[0][:, kt, jj * 128:(jj + 1) * 128], xnT[:, kt, ts_], start=(kt == 0), stop=(kt == KT - 1))
                        for kt in range(KT):
                            k.mm(u_ps[:], slot[1][:, kt, jj * 128:(jj + 1) * 128], xnT[:, kt, ts_], start=(kt == 0), stop=(kt == KT - 1))
                        k.act(sgt[:], g_ps[:], AF.Silu)
                        k.tt("dve", actT[:, jl, ts_], sgt[:], u_ps[:], ALU.mult)
                self.ffn_issue(min(gi + 4, next_limit))
            for mo in range(KT):
                for t in range(NT):
                    ts_ = slice(t * 512, (t + 1) * 512)
                    o_ps = po[cnt2 % 4]
                    cnt2 += 1
                    for jl in range(nj):
                        k.mm(o_ps[:], wout[:, jl, mo * 128:(mo + 1) * 128], actT[:, jl, ts_], start=(jl == 0), stop=(jl == nj - 1))
                    k.stt(self.hT[:, mo, ts_], o_ps[:], 0.5, self.hT[:, mo, ts_], ALU.mult, ALU.add)
            j0 += nj
        A.release(m)

    def final_norm(self):
        k, A = self.k, self.A
        m = A.mark()
        sq = A.tile([128, 8, 512], BF16)
        rs = [A.tile([128, 512], F32) for _ in range(2)]
        yo = [A.tile([128, 8, 512], F32) for _ in range(2)]
        ssq = self.psb(6)
        for t in range(NT):
            ts_ = slice(t * 512, (t + 1) * 512)
            k.act(sq[:], self.hT[:, :, ts_], AF.Square)
            for kt in range(KT):
                k.mm(ssq[:], self.ones[:], sq[:, kt, :], start=(kt == 0), stop=(kt == KT - 1))
            r = rs[t % 2]
            k.act(r[:], ssq[:], AF.Sqrt, bias=self.epsT[:, 0:1], scale=1.0 / D)
            k.recip(r[:], r[:])
            y = yo[t % 2]
            for kt in range(KT):
                k.stt(y[:, kt, :], self.hT[:, kt, ts_], self.vc[:, VC_FG + kt:VC_FG + kt + 1], r[:], ALU.mult, ALU.mult)
            k.dma("sp", self.outT[:, :, ts_], y[:], ("out", t % 2))
        A.release(m)

    def sub(self, t, c0, c1, shape):
        ap = t.h[:, c0:c1]
        if len(shape) == 3:
            ap = ap.rearrange("p (a b) -> p a b", a=shape[1], b=shape[2])
        return Tile(ap, shape, t.tid, base=t.base + c0 * t.isz, isz=t.isz)

    def proj_fm(self, wt, c0, xnT, dest, ci, scale=None):
        k = self.k
        for t in range(NT):
            ts_ = slice(t * 512, (t + 1) * 512)
            ps = self.psb(self.pcnt % 4)
            self.pcnt += 1
            for kt in range(KT):
                k.mm(ps[:], wt[:, kt, c0:c0 + 128], xnT[:, kt, ts_], start=(kt == 0), stop=(kt == KT - 1))
            if self.pcnt % 2 == 0:
                k.act(dest[:, ci, ts_], ps[:], AF.Copy, scale=(1.0 if scale is None else scale))
            else:
                k.ts("dve", dest[:, ci, ts_], ps[:], (1.0 if scale is None else scale), None, op0=ALU.mult)

    def proj_tm(self, wt, c0, n, xnT, dest, d0):
        k = self.k
        for blk in range(NB):
            ps = self.PS.tile([128, n], F32, off=(self.pcnt % 4) * 2048)
            self.pcnt += 1
            for kt in range(KT):
                k.mm(ps[:], xnT[:, kt, blk * 128:(blk + 1) * 128], wt[:, kt, c0:c0 + n], start=(kt == 0), stop=(kt == KT - 1))
            if self.pcnt % 2 == 0:
                k.copy("act", dest[:, blk, d0:d0 + n], ps[:])
            else:
                k.copy("dve", dest[:, blk, d0:d0 + n], ps[:])

    def wslab(self, wname, ncols_total, c0, hs, keyid):
        w = self.w(wname).ap().rearrange("(kt p) c -> p kt c", p=128)
        self.k.dma("pool", hs[:], View(w[:, :, c0:c0 + 256], None, []), ("wr", keyid // 2, keyid % 2))

    def attn(self):
        k, A = self.k, self.A
        base = A.mark()
        self.pcnt = 0
        hs = [self.wring[i // 2][i % 2] for i in range(6)]
        W0 = self.wring[0][0].base
        xnT = A.tile([128, 8, S], BF16, off=base)
        oTa = A.tile([128, 4, S], BF16, off=base + 32768)
        oTb = A.tile([128, 4, S], BF16, off=base + 0)
        R1 = base + 49152
        A.top = R1 + 53312
        A.peak = max(A.peak, A.top)
        assert A.top <= A.nbytes, A.top
        qaT = A.tile([128, 4, S], BF16, off=R1)
        kkT = A.tile([128, 2, S], BF16, off=R1 + 16384)
        va = A.tile([128, NB, 128], BF16, off=R1 + 24576)
        PT = [A.tile([128, 8, 256], BF16, off=R1 + 28672 + i * 4096) for i in range(3)]
        eS = [A.tile([128, 4, 256], F32, off=R1 + 40960 + i * 4096) for i in range(2)]
        den = [A.tile([128, 4, 128], F32, off=R1 + 49152 + i * 2048) for i in range(2)]
        es = A.tile([128, 8], F32, off=R1 + 53248)
        esk2 = A.tile([128, 4], F32, off=R1 + 53280)
        qbT = A.tile([128, 4, S], BF16, off=R1)
        kbT = A.tile([128, 4, S], BF16, off=R1 + 16384)
        vb = A.tile([128, NB, 512], BF16, off=R1 + 32768)
        Etab = self.sub(self.cfT, CF_E, CF_E + 2048, [128, 8, 256])
        sbm = self.sub(self.cfT, CF_SBM, CF_SBM + 128, [128, 128])

        top_save = A.top
        A.top = R1 + 28672
        self.rms_norm_T(VC_NG + 1 * 8, xnT)
        A.top = top_save

        wname = "attn_w_in"
        wv = self.w(wname).ap().rearrange("(kt p) c -> p kt c", p=128)
        self.wslab(wname, 2304, 0, hs[0], 0)
        self.wslab(wname, 2304, 256, hs[1], 1)
        self.wslab(wname, 2304, 512, hs[2], 2)
        for i, (src, dst) in enumerate([(512, 0), (512, 64), (576, 128), (576, 192)]):
            k.dma("pool", hs[3][:, :, dst:dst + 64], View(wv[:, :, src:src + 64], None, []), ("kk", i))
        for ci in range(4):
            self.proj_fm(hs[ci // 2], (ci % 2) * 128, xnT, qaT, ci, scale=0.125)
        self.wslab(wname, 2304, 768, hs[4], 4)
        self.wslab(wname, 2304, 1024, hs[5], 5)
        self.proj_fm(hs[3], 0, xnT, kkT, 0)
        self.proj_fm(hs[3], 128, xnT, kkT, 1)
        self.proj_tm(hs[2], 128, 128, xnT, va, 0)
        self.wslab(wname, 2304, 1280, hs[0], 0)
        self.wslab(wname, 2304, 1536, hs[1], 1)
        self.wslab(wname, 2304, 1792, hs[2], 2)
        self.wslab(wname, 2304, 2048, hs[3], 3)

        cut = int(os.environ.get('ATTN_CUT', '99'))
        if cut <= 1:
            A.release(base)
            return
        k.act(es[:], self.vc[:, VC_SK:VC_SK + 8], AF.Exp)
        k.copy("dve", esk2[0:64, :], es[0:64, 0:8:2])
        k.copy("dve", esk2[64:128, :], es[64:128, 1:8:2])
        esb = View(esk2.h[:, :].unsqueeze(2).to_broadcast([128, 4, 128]), esk2.tid, esk2[:].rects)
        def swa_scores(kb):
            nq = 256 if kb < NB - 1 else 128
            pt = PT[kb % 3]
            for hg in range(2):
                Sps = self.PS.tile([128, 4, 256], F32, off=(hg * 2) * 2048)
                for i in range(4):
                    h = 4 * hg + i
                    c, po = h // 2, (h % 2) * 64
                    j = (i % 2) * 2 + (i // 2)
                    k.mm(Sps[:, j, 0:nq], kkT[po:po + 64, hg, kb * 128:(kb + 1) * 128],
                         qaT[po:po + 64, c, kb * 128:kb * 128 + nq])
                e = eS[hg]
                k.act(e[:, :, 0:nq], Sps[:, :, 0:nq], AF.Exp)
                for p in range(2):
                    hsel = slice(4 * hg + p, 4 * hg + p + 3, 2)
                    k.tt("dve", pt[:, hsel, 0:nq], e[:, 2 * p:2 * p + 2, 0:nq], Etab[:, hsel, 0:nq], ALU.mult)

        def swa_out(kb):
            pt = PT[kb % 3]
            O = self.PS.tile([128, 4, 128], F32, off=(4 + 2 * (kb % 2)) * 2048)
            Dn = self.PS.tile([128, 4, 128], F32, off=(5 + 2 * (kb % 2)) * 2048)
            ptp = PT[(kb - 1) % 3]
            for h in range(8):
                g, c, po = h // 4, h // 2, (h % 2) * 64
                k.mm(O[po:po + 64, c, :], va[:, kb, g * 64:(g + 1) * 64], pt[:, h, 0:128], start=True, stop=(kb == 0), tile_position=(0, po))
                if kb > 0:
                    k.mm(O[po:po + 64, c, :], va[:, kb - 1, g * 64:(g + 1) * 64], ptp[:, h, 128:256], start=False, stop=True, tile_position=(0, po))
                k.mm(Dn[po:po + 64, c, :], self.ones[:, 0:64], pt[:, h, 0:128], start=True, stop=(kb == 0), tile_position=(0, po))
                if kb > 0:
                    k.mm(Dn[po:po + 64, c, :], self.ones[:, 0:64], ptp[:, h, 128:256], start=False, stop=True, tile_position=(0, po))
            dn = den[kb % 2]
            k.tt("dve", dn[:], Dn[:], esb, ALU.add)
            k.recip(dn[:], dn[:])
            k.tt("dve", oTa[:, :, kb * 128:(kb + 1) * 128], O[:], dn[:], ALU.mult)

        swa_scores(0)
        for kb in range(NB):
            if kb + 1 < NB:
                swa_scores(kb + 1)
            swa_out(kb)

        if cut <= 2:
            A.release(base)
            return
        self.proj_fm(hs[4], 0, xnT, qbT, 0, scale=0.125)
        self.proj_fm(hs[4], 128, xnT, qbT, 1, scale=0.125)
        self.proj_fm(hs[5], 0, xnT, qbT, 2, scale=0.125)
        self.proj_fm(hs[5], 128, xnT, qbT, 3, scale=0.125)
        self.proj_fm(hs[0], 0, xnT, kbT, 0)
        self.proj_fm(hs[0], 128, xnT, kbT, 1)
        self.proj_fm(hs[1], 0, xnT, kbT, 2)
        self.proj_fm(hs[1], 128, xnT, kbT, 3)
        self.proj_tm(hs[2], 0, 256, xnT, vb, 0)
        self.proj_tm(hs[3], 0, 256, xnT, vb, 256)

        if cut <= 3:
            A.release(base)
            return
        k.new_epoch()
        X1 = base + 16384

        class Bf:
            pass
        bA, bB = Bf(), Bf()
        bB.e32 = A.tile([128, 2048], F32, off=W0)
        bB.spP = A.tile([128, 2049], F32, off=W0 + 8192)
        bB.xw = A.tile([128, 2048], BF16, off=W0 + 16416)
        bA.xw = A.tile([128, 1024], BF16, off=W0 + 20512)
        bA.npt = A.tile([128, 1], F32, off=W0 + 22560)
        bB.npt = A.tile([128, 1], F32, off=W0 + 22592)
        bB.WT = A.tile([128, 16, 128], BF16, off=X1)
        bA.e32 = A.tile([128, 1024], F32, off=X1 + 4096)
        bA.spP = A.tile([128, 1025], F32, off=X1 + 8192)
        bA.WT = A.tile([128, 8, 128], BF16, off=X1 + 12320)
        k.memset("pool", bA.spP[:, 0:1], 0.0)
        k.memset("pool", bB.spP[:, 0:1], 0.0)
        Zt = self.PS.tile([128, 2048], F32, off=0)
        TPb = [self.PS.tile([128, 8, 128], BF16, off=(4 + i) * 2048) for i in range(2)]
        Ob = [self.PS.tile([128, 128], F32, off=(6 + i) * 2048) for i in range(2)]
        order = []
        for i in range(8):
            order += [i, 15 - i]
        blocks = [(h, qb) for h in range(8) for qb in order]
        nblk = len(blocks)
        st = {"o": 0}

        def geo(i):
            h, qb = blocks[i]
            return h, qb, h // 2, (h % 2) * 64, (bA if qb < 8 else bB), (qb + 1) * 128, slice(qb * 128, (qb + 1) * 128)

        def s_qk(i):
            h, qb, c, po, B, Lk, qs = geo(i)
            for kc in range((Lk + 511) // 512):
                w = min(512, Lk - kc * 512)
                k.mm(Zt[:, kc * 512:kc * 512 + w], qbT[po:po + 64, c, qs], kbT[po:po + 64, c, kc * 512:kc * 512 + w])

        def s_exp(i):
            h, qb, c, po, B, Lk, qs = geo(i)
            k.act(B.e32[:, 0:Lk], Zt[:, 0:Lk], AF.Exp)
            k.tt("pool", B.e32[:, qb * 128:Lk], B.e32[:, qb * 128:Lk], sbm[:], ALU.mult)

        def s_ln(i):
            h, qb, c, po, B, Lk, qs = geo(i)
            k.act(B.spP[:, 1:Lk + 1], B.e32[:, 0:Lk], AF.Ln, bias=self.epsT[:, 1:2])

        def s_scan(i):
            h, qb, c, po, B, Lk, qs = geo(i)
            k.scan(B.spP[:, 1:Lk + 1], B.spP[:, 1:Lk + 1], B.spP[:, 1:Lk + 1], 0.0, ALU.add, ALU.max)
            k.ts("dve", B.npt[:], B.spP[:, Lk:Lk + 1], -1.0, None, op0=ALU.mult)

        def s_exp2(i):
            h, qb, c, po, B, Lk, qs = geo(i)
            k.act(B.xw[:, 0:Lk], B.spP[:, 0:Lk], AF.Exp, bias=B.npt[:])

        def s_mult(i):
            h, qb, c, po, B, Lk, qs = geo(i)
            k.tt("pool", B.xw[:, 0:Lk], B.e32[:, 0:Lk], B.xw[:, 0:Lk], ALU.mult)

        def s_tr(i):
            h, qb, c, po, B, Lk, qs = geo(i)
            for g, sb0 in enumerate(range(0, qb + 1, 8)):
                n = min(8, qb + 1 - sb0)
                for ii in range(n):
                    k.transpose(TPb[g][:, ii, :], B.xw[:, (sb0 + ii) * 128:(sb0 + ii + 1) * 128], self.ident[:])

        def s_evac(i, eng_sel):
            h, qb, c, po, B, Lk, qs = geo(i)
            for g, sb0 in enumerate(range(0, qb + 1, 8)):
                n = min(8, qb + 1 - sb0)
                eng = "dve"
                if eng == eng_sel:
                    k.copy(eng, B.WT[:, sb0:sb0 + n, :], TPb[g][:, 0:n, :])

        def s_pv(i):
            h, qb, c, po, B, Lk, qs = geo(i)
            o = Ob[st["o"] % 2]
            st["o"] += 1
            for sb in range(qb + 1):
                k.mm(o[po:po + 64, :], vb[:, sb, c * 128 + po:c * 128 + po + 64], B.WT[:, sb, :],
                     start=(sb == 0), stop=(sb == qb), tile_position=(0, po))
            k.copy("act", oTb[po:po + 64, c, qs], o[po:po + 64, :])

        ok = lambda j: 0 <= j < nblk
        s_qk(0)
        for i in range(-1, nblk + 1):
            if ok(i + 1):
                s_exp(i + 1)
            if ok(i + 2):
                s_qk(i + 2)
            if ok(i):
                s_scan(i)
            if ok(i - 1):
                s_evac(i - 1, "act")
            if ok(i + 1):
                s_ln(i + 1)
            if ok(i):
                s_exp2(i)
            if ok(i - 1):
                s_evac(i - 1, "dve")
                s_pv(i - 1)
            if ok(i):
                s_mult(i)
                s_tr(i)

        if cut <= 4:
            A.release(base)
            return
        self.out_proj("attn_w_out", [(oTa, c) for c in range(4)] + [(oTb, c) for c in range(4)])
        A.release(base)

    def out_proj(self, wname, srcs):
        k = self.k
        hs = [self.wring[i // 2][i % 2] for i in range(6)]
        wo = self.w(wname).ap().rearrange("(c p) m -> p c m", p=128)
        for i in range(4):
            k.dma("pool", hs[i][:], View(wo[:, :, i * 256:(i + 1) * 256], None, []), ("wr", i // 2, i % 2))
        for mo in range(KT):
            hsl, c0 = hs[mo // 2], (mo % 2) * 128
            for t in range(NT):
                ts_ = slice(t * 512, (t + 1) * 512)
                ps = self.psb(self.pcnt % 4)
                self.pcnt += 1
                for c in range(8):
                    src, ci = srcs[c]
                    k.mm(ps[:], hsl[:, c, c0:c0 + 128], src[:, ci, ts_], start=(c == 0), stop=(c == 7))
                k.tt("dve", self.hT[:, mo, ts_], ps[:], self.hT[:, mo, ts_], ALU.add)

    def rec(self):
        k, A = self.k, self.A
        base = A.mark()
        self.pcnt = 0
        xnT = A.tile([128, 8, S], BF16)
        oT = A.tile([128, 8, S], BF16)
        self.rms_norm_T(VC_NG + (3 + 1) * 8, xnT)
        W0 = self.wring[0][0].base
        slots = [A.tile([128, 8, 512], BF16, off=W0 + i * 8192) for i in range(3)]
        slots.append(A.tile([128, 8, 512], BF16))
        wv = self.w("rec_w_in").ap().rearrange("(kt p) c -> p kt c", p=128)

        def load(slot_i, j, c0):
            k.dma("pool", slots[slot_i][:, :, j * 128:(j + 1) * 128], View(wv[:, :, c0:c0 + 128], None, []), ("wq", slot_i, j))

        sm = A.tile([128, 48], F32)
        L0 = self.vc[:, VC_LB:VC_LB + 8:2]
        L1 = self.vc[:, VC_LB + 1:VC_LB + 8:2]
        mx, d0, d1, ssum, lb, oml = [sm[:, i * 4:(i + 1) * 4] for i in range(6)]
        k.tt("dve", mx, L0, L1, ALU.max)
        k.tt("dve", d0, L0, mx, ALU.subtract)
        k.tt("dve", d1, L1, mx, ALU.subtract)
        k.act(d0, d0, AF.Exp)
        k.act(d1, d1, AF.Exp)
        k.tt("dve", ssum, d0, d1, ALU.add)
        k.recip(ssum, ssum)
        k.tt("dve", d0, d0, ssum, ALU.mult)
        k.tt("dve", d1, d1, ssum, ALU.mult)
        k.tt("dve", d1, d0, d1, ALU.add)
        k.tt("dve", lb, d1, d0, ALU.subtract)
        k.ts("dve", oml, lb, -1.0, 1.0, op0=ALU.mult, op1=ALU.add)

        mconv = A.mark()
        xbuf = A.tile([128, S + 2], F32)
        gbs = A.tile([128, S], F32)
        ybuf = A.tile([128, S], F32)
        usb = [A.tile([128, 512], F32) for _ in range(2)]
        k.memset("pool", xbuf[:, 0:2], 0.0)
        for c in range(3):
            for j in range(3):
                load(c, j, 2048 + j * 512 + c * 128)
        bcnt = 0
        for c in range(4):
            sl = slots[c % 4 if c < 3 else 3]
            if c == 3:
                for j in range(3):
                    load(3, j, 2048 + j * 512 + c * 128)
            for t in range(NT):
                ts_ = slice(t * 512, (t + 1) * 512)
                pss = []
                for j in range(3):
                    ps = self.psb(bcnt % 8)
                    bcnt += 1
                    for kt in range(KT):
                        k.mm(ps[:], sl[:, kt, j * 128:(j + 1) * 128], xnT[:, kt, ts_], start=(kt == 0), stop=(kt == KT - 1))
                    pss.append(ps)
                u = usb[t % 2]
                k.copy("act", u[:], pss[2][:])
                k.tt("dve", xbuf[:, 2 + t * 512:2 + (t + 1) * 512], pss[1][:], u[:], ALU.mult)
                k.copy("act", gbs[:, ts_], pss[0][:])
            cw = VC_CW + c * 3
            k.ts("dve", ybuf[:], xbuf[:, 2:S + 2], self.vc[:, cw + 2:cw + 3], None, op0=ALU.mult)
            k.stt(ybuf[:], xbuf[:, 1:S + 1], self.vc[:, cw + 1:cw + 2], ybuf[:], ALU.mult, ALU.add)
            k.stt(ybuf[:], xbuf[:, 0:S], self.vc[:, cw:cw + 1], ybuf[:], ALU.mult, ALU.add)
            k.tt("dve", oT[:, 4 + c, :], gbs[:], ybuf[:], ALU.mult)
        A.release(mconv)

        for h in range(4):
            for j in range(4):
                load(h, j, j * 512 + h * 128)
        f32t = lambda: A.tile([128, 512], F32)
        T_sq, T_b1, T_b2, T_b3 = f32t(), f32t(), f32t(), f32t()
        rs2 = T_sq
        T_kh = A.tile([128, 512], BF16)
        T_sq2 = T_kh

        class Hd:
            pass
        hd = []
        for h in range(4):
            H = Hd()
            H.qt = A.tile([128, 512], BF16)
            H.kt = A.tile([128, 512], BF16)
            H.khT = A.tile([128, 4, 128], BF16)
            H.vt = A.tile([128, 4, 128], BF16)
            H.AT = A.tile([128, 4, 64], BF16)
            H.sgl = A.tile([128, 512], BF16)
            H.eGl = A.tile([128, 8], F32)
            H.sf = A.tile([128, 128], F32)
            H.sb = [A.tile([128, 128], BF16) for _ in range(2)]
            k.memset("pool", H.sf[:], 0.0)
            hd.append(H)
        rst = self.sub(self.cfT, CF_RST, CF_RST + 512, [128, 512])
        hgm = self.sub(self.cfT, CF_HGM, CF_HGM + 64, [128, 64])
        hgmb = View(hgm.h[:, :].unsqueeze(1).to_broadcast([128, 4, 64]), hgm.tid, hgm[:].rects)
        ops = [self.psb(h) for h in range(4)]
        ups = [self.PS.tile([128, 128], F32, off=(4 + i) * 2048) for i in range(2)]
        pw = 0
        ucnt = 0

        def pwbank(shape=(128, 512), dtype=F32):
            nonlocal pw
            pw += 1
            return self.PS.tile(list(shape), dtype, off=(6 + pw % 2) * 2048)

        for t in range(NT):
            ts_ = slice(t * 512, (t + 1) * 512)
            for h in range(4):
                H, sl = hd[h], slots[h]

                def proj(j):
                    ps = pwbank()
                    for kt in range(KT):
                        k.mm(ps[:], sl[:, kt, j * 128:(j + 1) * 128], xnT[:, kt, ts_], start=(kt == 0), stop=(kt == KT - 1))
                    return ps
                psq = proj(0)
                k.act(T_sq[:], psq[:], AF.Silu)
                psg = proj(3)
                k.act(H.sgl[:], psg[:], AF.Silu)
                psf = proj(1)
                k.act(T_b1[:], psf[:], AF.Sigmoid)
                k.ts("dve", T_b1[:], T_b1[:], oml[:, h:h + 1] if False else sm[:, 20 + h:21 + h], sm[:, 16 + h:17 + h], op0=ALU.mult, op1=ALU.add)
                k.ts("dve", T_b2[:], T_b1[:], -1.0, 1.0, op0=ALU.mult, op1=ALU.add)
                k.act(T_b1[:], T_b1[:], AF.Ln)
                k.scan(T_b3[:], rst[:], T_b1[:], 0.0, ALU.mult, ALU.add)
                k.act(T_b1[:], T_b3[:], AF.Exp)
                k.act(T_b3[:], T_b3[:], AF.Exp, scale=-1.0)
                k.copy("dve", H.eGl[:], T_b1[:, 63:512:64])
                k.tt("dve", H.qt[:], T_sq[:], T_b1[:], ALU.mult)
                k.tt("dve", T_b2[:], T_b2[:], T_b3[:], ALU.mult)
                k.copy("pool", H.kt[:], T_b2[:])
                eb = View(H.eGl.h[:, :].unsqueeze(2).to_broadcast([128, 8, 64]), H.eGl.tid, H.eGl[:].rects)
                khv = View(T_kh.h[:, :].rearrange("p (a b) -> p a b", a=8, b=64), T_kh.tid, T_kh[:].rects)
                b2v = View(T_b2.h[:, :].rearrange("p (a b) -> p a b", a=8, b=64), T_b2.tid, T_b2[:].rects)
                k.tt("dve", khv, b2v, eb, ALU.mult)
                for blk in range(4):
                    ps = pwbank((128, 128))
                    tok = slice(t * 512 + blk * 128, t * 512 + (blk + 1) * 128)
                    for kt in range(KT):
                        k.mm(ps[:], xnT[:, kt, tok], sl[:, kt, 256:384], start=(kt == 0), stop=(kt == KT - 1))
                    k.copy("act", H.vt[:, blk, :], ps[:])
                tp = pwbank((128, 4, 128), BF16)
                for blk in range(4):
                    k.transpose(tp[:, blk, :], T_kh[:, blk * 128:(blk + 1) * 128], self.ident[:])
                k.copy("dve", H.khT[:], tp[:])
                ap_ = pwbank((128, 4, 64))
                for blk in range(4):
                    for half in range(2):
                        t0 = (2 * blk + half) * 64
                        k.mm(ap_[half * 64:(half + 1) * 64, blk, :], H.kt[:, t0:t0 + 64], H.qt[:, t0:t0 + 64], tile_position=(0, half * 64))
                k.tt("dve", H.AT[:], ap_[:], hgmb, ALU.mult)
            for cc in range(8):
                blk, half = cc // 2, cc % 2
                p0 = half * 64
                t0 = cc * 64
                gc = t * 8 + cc
                for h in range(4):
                    H = hd[h]
                    k.mm(ops[h][:, t0:t0 + 64], H.vt[p0:p0 + 64, blk, :], H.AT[p0:p0 + 64, blk, :], start=True, stop=(gc == 0))
                    if gc > 0:
                        k.mm(ops[h][:, t0:t0 + 64], H.sb[gc % 2][:], H.qt[:, t0:t0 + 64], start=False, stop=True)
                    if gc < 31:
                        u = ups[ucnt % 2]
                        ucnt += 1
                        k.mm(u[:], H.khT[p0:p0 + 64, blk, :], H.vt[p0:p0 + 64, blk, :])
                        k.stt(H.sf[:], H.sf[:], H.eGl[:, cc:cc + 1], u[:], ALU.mult, ALU.add)
                        k.copy("act", H.sb[(gc + 1) % 2][:], H.sf[:])
            for h in range(4):
                H = hd[h]
                k.act(T_sq2[:], ops[h][:], AF.Square)
                ssq = pwbank()
                k.mm(ssq[:], self.ones[:], T_sq2[:])
                k.act(rs2[:], ssq[:], AF.Sqrt, bias=self.epsT[:, 0:1], scale=1.0 / 128)
                k.recip(rs2[:], rs2[:])
                k.stt(rs2[:], ops[h][:], self.vc[:, VC_HN:VC_HN + 1], rs2[:], ALU.mult, ALU.mult)
                k.tt("dve", oT[:, h, ts_], rs2[:], H.sgl[:], ALU.mult)

        self.out_proj("rec_w_out", [(oT, c) for c in range(8)])
        A.release(base)


_CONSTS = None


def kernel(x, norm_g, ffn_w_in, ffn_w_out, attn_w_in, attn_sinks, attn_w_out, rec_w_in,
           hgrn_lb_logits, hgrn_norm_g, conv_w, rec_w_out, final_g, _stages=ALL_STAGES, _xT=None):
    global _CONSTS
    if _CONSTS is None:
        _CONSTS = host_consts()
    cf, cb = _CONSTS
    f32 = lambda a: np.ascontiguousarray(np.asarray(a, dtype=np.float32))
    x = f32(x)
    vecs = host_vecs(f32(norm_g), f32(final_g), f32(hgrn_norm_g), f32(conv_w), f32(hgrn_lb_logits), f32(attn_sinks))
    prog = Prog(stages=_stages)
    nc = prog.build()
    allw = {"attn_w_in": lambda: f32(attn_w_in)[0], "attn_w_out": lambda: f32(attn_w_out)[0],
            "rec_w_in": lambda: f32(rec_w_in)[0], "rec_w_out": lambda: f32(rec_w_out)[0]}
    for l in range(2):
        for f in range(2):
            allw["ffn_w_in_%d%d" % (l, f)] = (lambda l=l, f=f: f32(np.asarray(ffn_w_in)[l, f]))
            allw["ffn_w_out_%d%d" % (l, f)] = (lambda l=l, f=f: f32(np.asarray(ffn_w_out)[l, f]))
    shared = {"cf": cf, "cb": cb, "vecs": vecs}
    for name in prog.wd:
        shared[name] = allw[name]()
    in_maps = []
    for b in range(8):
        src = x[b] if _xT is None else f32(_xT[b])
        xT = np.ascontiguousarray(src.T.reshape(8, 128, S).transpose(1, 0, 2))
        in_maps.append({"xT": xT, **shared})
    res = run_bass_kernel_spmd(nc, in_maps, core_ids=list(range(8)))
    out = np.empty((8, S, D), np.float32)
    for b in range(8):
        oT = np.asarray(res.results[b]["outT"])
        out[b] = oT.transpose(2, 1, 0).reshape(S, D)
    return out
```

```python
import os
import numpy as np
import ml_dtypes
import concourse.bass as bass
import concourse.mybir as mybir
from concourse.bass_utils import run_bass_kernel_spmd
from contextlib import ExitStack

F32 = mybir.dt.float32
BF16 = mybir.dt.bfloat16
ALU = mybir.AluOpType
AF = mybir.ActivationFunctionType

S = 2048
D = 1024
KT = 8
NT = 4
NB = 16
DFF = 2816
NJ = 22
EPS = 1e-6


class View:
    __slots__ = ("ap", "tid", "rects")

    def __init__(self, ap, tid, rects):
        self.ap = ap
        self.tid = tid
        self.rects = rects

    def with_ap(self, ap):
        return View(ap, self.tid, self.rects)


class Tile:
    def __init__(self, handle, shape, tid, base=0, isz=4, p0=0, gran=None):
        self.h = handle
        self.shape = tuple(shape)
        self.tid = tid
        self.base = base
        self.isz = isz
        self.p0 = p0
        self.gran = gran

    def __getitem__(self, idx):
        if not isinstance(idx, tuple):
            idx = (idx,)
        idx = idx + (slice(None),) * (len(self.shape) - len(idx))
        ap = self.h[idx]
        if self.tid is None:
            return View(ap, None, [])
        rng = []
        for ix, n in zip(idx, self.shape):
            if isinstance(ix, int):
                rng.append((ix, ix + 1))
            else:
                s, e, st = ix.indices(n)
                if st != 1:
                    e = s + ((e - s + st - 1) // st - 1) * st + 1
                rng.append((s, e))
        ivs = [(0, 1)]
        size = 1
        for (s, e), n in zip(reversed(rng[1:]), reversed(self.shape[1:])):
            if len(ivs) == 1 and ivs[0] == (0, size):
                ivs = [(s * size, e * size)]
            else:
                ivs = [(a * size + lo, a * size + hi) for a in range(s, e) for (lo, hi) in ivs]
            size *= n
        b, z = self.base, self.isz
        rects = [(self.p0 + rng[0][0], self.p0 + rng[0][1], b + lo * z, b + hi * z) for lo, hi in ivs]
        if self.gran is not None:
            gb, gp = self.gran
            rr = set()
            for (pa, pb, lo, hi) in rects:
                rr.add((pa // gp * gp, (pb + gp - 1) // gp * gp, lo // gb * gb, (hi + gb - 1) // gb * gb))
            rects = sorted(rr)
        return View(ap, self.tid, rects)


class Arena:
    def __init__(self, k, handle, nbytes, gran=None):
        self.h = handle
        self.nbytes = nbytes
        self.tid = k._new_tid()
        self.top = 0
        self.peak = 0
        self.gran = gran

    def mark(self):
        return self.top

    def release(self, m):
        self.top = m

    def tile(self, shape, dtype, off=None):
        isz = 4 if dtype == F32 else 2
        n = 1
        for d in shape[1:]:
            n *= d
        nb = (n * isz + 31) // 32 * 32
        if off is None:
            off = self.top
            self.top += nb
            self.peak = max(self.peak, self.top)
            assert self.top <= self.nbytes, ("arena overflow", self.top, self.nbytes)
        ap = self.h[:, off // 4:(off + nb) // 4]
        if dtype != F32:
            ap = ap.bitcast(dtype)
        ap = ap[0:shape[0], 0:n]
        if len(shape) == 3:
            ap = ap.rearrange("p (a b) -> p a b", a=shape[1], b=shape[2])
        elif len(shape) == 4:
            ap = ap.rearrange("p (a b c) -> p a b c", a=shape[1], b=shape[2], c=shape[3])
        return Tile(ap, shape, self.tid, base=off, isz=isz, gran=self.gran)


class Ins:
    __slots__ = ("eng", "fn", "deps", "signal", "count", "is_dma", "sem", "_barred", "epoch", "idx")

    def __init__(self, eng, fn, is_dma=False):
        self.epoch = 0
        self.idx = 0
        self.eng = eng
        self.fn = fn
        self.deps = []
        self.signal = False
        self.count = 0
        self.is_dma = is_dma
        self.sem = None
        self._barred = False


ENGS = ("pe", "act", "dve", "pool", "sp")


class K:
    def __init__(self, nc):
        self.nc = nc
        self.streams = {e: [] for e in ENGS}
        self.acc = {}
        self._tid = 0
        self.dma_keys = {}
        self.n_ins = 0
        self.psum_tid = None
        self.epoch = 0

    def _new_tid(self):
        self._tid += 1
        return self._tid

    @staticmethod
    def _ov(a, b):
        return a[0] < b[1] and b[0] < a[1] and a[2] < b[3] and b[2] < a[3]

    @staticmethod
    def _cover(a, b):
        return a[0] <= b[0] and a[1] >= b[1] and a[2] <= b[2] and a[3] >= b[3]

    def _track(self, ins, reads, writes):
        deps = {}
        ov = self._ov
        for v in reads:
            if v is None or v.tid is None:
                continue
            lst = self.acc.setdefault(v.tid, [])
            for r in v.rects:
                for (rect, j, kind) in lst:
                    if kind == "w" and ov(rect, r):
                        deps[id(j)] = j
        for v in writes:
            if v is None or v.tid is None:
                continue
            lst = self.acc.setdefault(v.tid, [])
            for r in v.rects:
                for (rect, j, kind) in lst:
                    if ov(rect, r):
                        deps[id(j)] = j
        for v in reads:
            if v is None or v.tid is None:
                continue
            lst = self.acc[v.tid]
            for r in v.rects:
                if not ins.is_dma:
                    lst[:] = [e for e in lst if not (e[2] == "r" and e[0] == r and e[1].eng == ins.eng and not e[1].is_dma)]
                lst.append((r, ins, "r"))
        for v in writes:
            if v is None or v.tid is None:
                continue
            lst = self.acc[v.tid]
            for r in v.rects:
                lst[:] = [e for e in lst if not self._cover(r, e[0])]
                lst.append((r, ins, "w"))
        deps.pop(id(ins), None)
        for j in deps.values():
            if j.eng == "pe" and ins.eng == "pe" and not j.is_dma and not ins.is_dma:
                continue
            ins.deps.append(j)
            j.signal = True

    def new_epoch(self):
        self.epoch += 1

    def op(self, eng, fn, reads, writes):
        ins = Ins(eng, fn)
        ins.epoch = self.epoch
        self.n_ins += 1
        if self.psum_tid is not None:
            pr = [v for v in reads if v is not None and v.tid == self.psum_tid]
            if pr:
                reads = [v for v in reads if v is None or v.tid != self.psum_tid]
                writes = list(writes) + pr
        self._track(ins, reads, writes)
        self.streams[eng].append(ins)
        return ins

    def dma(self, queue, out, in_, key, **kw):
        ins = Ins(queue, None, is_dma=True)
        self.n_ins += 1
        kk = self.dma_keys.setdefault(key, [0])
        kk[0] += 16
        ins.sem = key
        ins.count = kk[0]
        oa, ia = out.ap, in_.ap
        ins.fn = lambda e: e.dma_start(out=oa, in_=ia, **kw)
        self._track(ins, [in_], [out])
        self.streams[queue].append(ins)
        return ins

    def barrier(self):
        lasts = []
        for e in ENGS:
            comp = [i for i in self.streams[e] if not i.is_dma and i.fn is not None]
            if comp:
                lasts.append(comp[-1])
            for i in self.streams[e]:
                if i.is_dma and not i._barred:
                    i._barred = True
                    lasts.append(i)
        for e in ENGS:
            ins = Ins(e, None)
            ins.epoch = self.epoch
            for j in lasts:
                if j.eng == e and not j.is_dma and e == "pe":
                    continue
                ins.deps.append(j)
                j.signal = True
            self.streams[e].append(ins)
        self.acc = {}

    def mm(self, out, lhsT, rhs, start=True, stop=True, **kw):
        o, l, r = out.ap, lhsT.ap, rhs.ap
        return self.op("pe", lambda e: e.matmul(o, l, r, start=start, stop=stop, **kw), [lhsT, rhs], [out])

    def transpose(self, out, in_, ident):
        o, i, d = out.ap, in_.ap, ident.ap
        return self.op("pe", lambda e: e.transpose(o, i, d), [in_, ident], [out])

    def act(self, out, in_, func, bias=None, scale=1.0, accum_out=None):
        o, i = out.ap, in_.ap
        kw = {}
        rd = [in_]
        wr = [out]
        if bias is not None:
            if isinstance(bias, View):
                kw["bias"] = bias.ap
                rd.append(bias)
            else:
                kw["bias"] = bias
        if isinstance(scale, View):
            kw["scale"] = scale.ap
            rd.append(scale)
        else:
            kw["scale"] = scale
        if accum_out is not None:
            kw["accum_out"] = accum_out.ap
            wr.append(accum_out)
        return self.op("act", lambda e: e.activation(o, i, func, **kw), rd, wr)

    def tt(self, eng, out, in0, in1, op):
        o, a, b = out.ap, in0.ap, in1.ap
        return self.op(eng, lambda e: e.tensor_tensor(o, a, b, op), [in0, in1], [out])

    def ts(self, eng, out, in0, s1, s2=None, op0=ALU.mult, op1=None):
        o, a = out.ap, in0.ap
        rd = [in0]
        if isinstance(s1, View):
            rd.append(s1)
            s1 = s1.ap
        if isinstance(s2, View):
            rd.append(s2)
            s2 = s2.ap
        kw = {}
        if op1 is not None:
            kw["op1"] = op1
        return self.op(eng, lambda e: e.tensor_scalar(o, a, s1, s2, op0, **kw), rd, [out])

    def stt(self, out, in0, scalar, in1, op0, op1):
        o, a, b = out.ap, in0.ap, in1.ap
        rd = [in0, in1]
        if isinstance(scalar, View):
            rd.append(scalar)
            scalar = scalar.ap
        return self.op("dve", lambda e: e.scalar_tensor_tensor(o, a, scalar, b, op0, op1), rd, [out])

    def copy(self, eng, out, in_):
        o, i = out.ap, in_.ap
        if eng == "act":
            return self.op("act", lambda e: e.copy(o, i), [in_], [out])
        return self.op(eng, lambda e: e.tensor_copy(o, i), [in_], [out])

    def scan(self, out, d0, d1, initial, op0, op1):
        o, a, b = out.ap, d0.ap, d1.ap
        rd = [d0, d1]
        if isinstance(initial, View):
            rd.append(initial)
            initial = initial.ap
        return self.op("dve", lambda e: e.tensor_tensor_scan(o, a, b, initial, op0, op1), rd, [out])

    def recip(self, out, in_):
        o, i = out.ap, in_.ap
        return self.op("dve", lambda e: e.reciprocal(o, i), [in_], [out])

    def memset(self, eng, out, val):
        o = out.ap
        return self.op(eng, lambda e: e.memset(o, val), [], [out])

    def emit(self):
        nc = self.nc
        with ExitStack() as st:
            esem = {}
            dsem = {}
            for key in self.dma_keys:
                dsem[key] = st.enter_context(nc.semaphore("d_%d" % len(dsem)))
            self.max_count = 0
            for e in ENGS:
                for n_, ins in enumerate(self.streams[e]):
                    ins.idx = n_
                    ins.signal = False
            for e in ENGS:
                for ins in self.streams[e]:
                    best = {}
                    keep = []
                    for j in ins.deps:
                        if j.is_dma:
                            keep.append(j)
                            continue
                        kk_ = (j.eng, j.epoch)
                        if kk_ not in best or best[kk_].idx < j.idx:
                            best[kk_] = j
                    for j in best.values():
                        j.signal = True
                        keep.append(j)
                    ins.deps = keep
            for e in ENGS:
                c = {}
                for ins in self.streams[e]:
                    if ins.is_dma:
                        continue
                    if ins.signal:
                        ek = (e, ins.epoch)
                        if ek not in esem:
                            esem[ek] = st.enter_context(nc.semaphore("s_%s_%d" % ek))
                        c[ek] = c.get(ek, 0) + 1
                        ins.count = c[ek]
                        self.max_count = max(self.max_count, ins.count)
            block = st.enter_context(nc.Block())
            engobj = {"pe": block.tensor, "act": block.scalar, "dve": block.vector, "pool": block.gpsimd, "sp": block.sync}

            def mk(ename):
                def body(eng):
                    waited = {}
                    for ins in self.streams[ename]:
                        need = {}
                        for j in ins.deps:
                            if j.is_dma:
                                s, c, wk = dsem[j.sem], j.count, ("d", j.sem)
                            else:
                                s, c, wk = esem[(j.eng, j.epoch)], j.count, ("e", j.eng, j.epoch)
                            if c > need.get(wk, (None, 0))[1]:
                                need[wk] = (s, c)
                        for wk, (s, c) in need.items():
                            if waited.get(wk, 0) >= c:
                                continue
                            waited[wk] = c
                            eng.wait_ge(s, c)
                        if ins.fn is None:
                            continue
                        r = ins.fn(eng)
                        if ins.is_dma:
                            r.then_inc(dsem[ins.sem], 16)
                        elif ins.signal:
                            r.then_inc(esem[(ename, ins.epoch)], 1)
                return body

            for e in ENGS:
                engobj[e](mk(e))


CF_E = 0
CF_SBM = 2048
CF_HGM = 2176
CF_RST = 2240
CF_N = 2752


def host_consts():
    cf = np.zeros((128, CF_N), np.float32)
    k = np.arange(128)[:, None].astype(np.float64)
    q = np.arange(128)[None, :].astype(np.float64)
    for h in range(8):
        slope = 2.0 ** (-(h + 1))
        cur = np.where(q >= k, np.exp(-slope * (q - k)), 0.0)
        prev = np.where(k > q, np.exp(-slope * (q + 128 - k)), 0.0)
        cf[:, CF_E + h * 256:CF_E + h * 256 + 128] = cur
        cf[:, CF_E + h * 256 + 128:CF_E + h * 256 + 256] = prev
    t = np.arange(128)[:, None]
    j = np.arange(128)[None, :]
    cf[:, CF_SBM:CF_SBM + 128] = (j < t)
    s = (np.arange(128) % 64)[:, None]
    tt = np.arange(64)[None, :]
    cf[:, CF_HGM:CF_HGM + 64] = (tt >= s)
    cf[:, CF_RST:CF_RST + 512] = (np.arange(512) % 64 != 0)[None, :]
    cb = np.zeros((128, 384), np.float32)
    cb[:, 0:128] = np.eye(128)
    cb[:, 128:256] = 1.0
    cb[:, 256:384] = np.where(j < t, 0.0, -30000.0)
    return cf, cb.astype(ml_dtypes.bfloat16)


VC_NG = 0
VC_FG = 48
VC_HN = 56
VC_CW = 57
VC_LB = 69
VC_SK = 77
VC_N = 88


def host_vecs(norm_g, final_g, hgrn_norm_g, conv_w, hgrn_lb_logits, attn_sinks):
    v = np.zeros((128, VC_N), np.float32)
    v[:, VC_NG:VC_NG + 48] = norm_g.reshape(6, 8, 128).transpose(2, 0, 1).reshape(128, 48)
    v[:, VC_FG:VC_FG + 8] = final_g.reshape(8, 128).T
    v[:, VC_HN] = hgrn_norm_g.reshape(128)
    v[:, VC_CW:VC_CW + 12] = conv_w.reshape(3, 4, 128).transpose(2, 1, 0).reshape(128, 12)
    v[:, VC_LB:VC_LB + 8] = hgrn_lb_logits.reshape(2, 4, 128).transpose(2, 1, 0).reshape(128, 8)
    v[:, VC_SK:VC_SK + 8] = attn_sinks.reshape(1, 8)
    return v


ALL_STAGES = ("ffn00", "attn", "ffn01", "ffn10", "rec", "ffn11", "final")
W_SHAPES = {"attn_w_in": [D, 2304], "attn_w_out": [D, D], "rec_w_in": [D, 3584], "rec_w_out": [D, D]}
for _l in range(2):
    for _f in range(2):
        W_SHAPES["ffn_w_in_%d%d" % (_l, _f)] = [D, 2 * DFF]
        W_SHAPES["ffn_w_out_%d%d" % (_l, _f)] = [DFF, D]


class Prog:
    def __init__(self, stages=ALL_STAGES):
        self.stages = tuple(stages)
        nc = self.nc = bass.Bass("TRN2", target_bir_lowering=False)
        k = self.k = K(nc)

        def din(name, shape, dt=F32):
            return Tile(nc.dram_tensor(name, list(shape), dt, kind="ExternalInput"), shape, None)

        self.xT = din("xT", [128, 8, S])
        self.cf = din("cf", [128, CF_N])
        self.cb = din("cb", [128, 384], BF16)
        self.vecs = din("vecs", [128, VC_N])
        self.wd = {}
        outh = nc.dram_tensor("outT", [128, 8, S], F32, kind="ExternalOutput")
        self.outT = Tile(outh, [128, 8, S], k._new_tid())

    def w(self, name):
        if name not in self.wd:
            self.wd[name] = self.nc.dram_tensor(name, W_SHAPES[name], F32, kind="ExternalInput")
        return self.wd[name]

    def build(self):
        nc, k = self.nc, self.k
        with ExitStack() as st:
            SB_BYTES = 212480
            ah = st.enter_context(nc.sbuf_tensor("arena", [128, SB_BYTES // 4], F32))
            ph = st.enter_context(nc.psum_tensor("psum", [128, 4096], F32))
            self.A = A = Arena(k, ah, SB_BYTES)
            self.PS = Arena(k, ph, 16384, gran=(2048, 32))
            k.psum_tid = self.PS.tid
            self.hT = A.tile([128, 8, S], F32)
            self.cfT = A.tile([128, CF_N], F32)
            self.cbT = A.tile([128, 384], BF16)
            self.vc = A.tile([128, VC_N], F32)
            self.epsT = A.tile([128, 2], F32)
            self.wring = [(A.tile([128, 8, 256], BF16), A.tile([128, 8, 256], BF16)) for _ in range(3)]
            self.ident = Tile(self.cbT.h[:, 0:128], [128, 128], self.cbT.tid, self.cbT.base, 2)
            self.ones = Tile(self.cbT.h[:, 128:256], [128, 128], self.cbT.tid, self.cbT.base + 256, 2)
            self.mbT = Tile(self.cbT.h[:, 256:384], [128, 128], self.cbT.tid, self.cbT.base + 512, 2)
            self.wslab_issued = 0

            for t in range(NT):
                k.dma("sp", self.hT[:, :, t * 512:(t + 1) * 512], self.xT[:, :, t * 512:(t + 1) * 512], ("x", t))
            k.dma("sp", self.cfT[:], self.cf[:], "const_cf")
            k.dma("sp", self.cbT[:], self.cb[:], "const_cb")
            k.dma("sp", self.vc[:], self.vecs[:], "const_vc")
            k.memset("pool", self.epsT[:, 0:1], EPS)
            k.memset("pool", self.epsT[:, 1:2], 1.0)

            self.body()

            k.barrier()
            k.emit()
        return nc

    def psb(self, b, shape=(128, 512), dtype=F32):
        return self.PS.tile(list(shape), dtype, off=b * 2048)

    def body(self):
        st = self.stages
        self.ffn_seq = [x for x in st if x.startswith("ffn")]
        for i, name in enumerate(st):
            self.k.new_epoch()
            if name.startswith("ffn"):
                fidx = self.ffn_seq.index(name)
                nxt_is_ffn = (i + 1 < len(st) and st[i + 1].startswith("ffn"))
                self.ffn_issue(limit=fidx * 11 + 3)
                self.ffn(int(name[3]), int(name[4]), fidx, next_limit=fidx * 11 + (14 if nxt_is_ffn else 11))
            elif name == "attn":
                self.attn()
            elif name == "rec":
                self.rec()
            elif name == "final":
                self.final_norm()
        if "final" not in st:
            self.dump_h()

    def dump_h(self):
        k = self.k
        for t in range(NT):
            k.dma("sp", self.outT[:, :, t * 512:(t + 1) * 512], self.hT[:, :, t * 512:(t + 1) * 512], ("out", t))

    def rms_norm_T(self, gcol0, xnT):
        k, A = self.k, self.A
        m = A.mark()
        sq = A.tile([128, 8, 512], BF16)
        rs = [A.tile([128, 512], F32) for _ in range(2)]
        ssq = self.psb(6)
        for t in range(NT):
            ts_ = slice(t * 512, (t + 1) * 512)
            k.act(sq[:], self.hT[:, :, ts_], AF.Square)
            for kt in range(KT):
                k.mm(ssq[:], self.ones[:], sq[:, kt, :], start=(kt == 0), stop=(kt == KT - 1))
            r = rs[t % 2]
            k.act(r[:], ssq[:], AF.Sqrt, bias=self.epsT[:, 0:1], scale=1.0 / D)
            k.recip(r[:], r[:])
            for kt in range(KT):
                k.stt(xnT[:, kt, ts_], self.hT[:, kt, ts_], self.vc[:, gcol0 + kt:gcol0 + kt + 1], r[:], ALU.mult, ALU.mult)
        A.release(m)

    def ffn_issue(self, limit):
        k = self.k
        while self.wslab_issued < limit and self.wslab_issued < 11 * len(self.ffn_seq):
            gi = self.wslab_issued
            fi, s = divmod(gi, 11)
            slot = self.wring[gi % 3]
            w = self.w("ffn_w_in_" + self.ffn_seq[fi][3:5]).ap().rearrange("(kt p) c -> p kt c", p=128)
            k.dma("pool", slot[0][:], View(w[:, :, s * 256:(s + 1) * 256], None, []), ("wr", gi % 3, 0))
            k.dma("pool", slot[1][:], View(w[:, :, DFF + s * 256:DFF + (s + 1) * 256], None, []), ("wr", gi % 3, 1))
            self.wslab_issued += 1

    def ffn(self, l, f, fi, next_limit):
        k, A = self.k, self.A
        m = A.mark()
        xnT = A.tile([128, 8, S], BF16)
        parts = [6, 6, 6, 4]
        actT = A.tile([128, 6, S], BF16)
        wout = A.tile([128, 6, D], BF16)
        sg = [A.tile([128, 512], F32) for _ in range(3)]
        self.rms_norm_T(VC_NG + (l * 3 + (0 if f == 0 else 2)) * 8, xnT)
        wo_dram = self.w("ffn_w_out_%d%d" % (l, f)).ap().rearrange("(j p) c -> p j c", p=128)
        pg = [self.psb(0), self.psb(1), self.psb(2)]
        pu = [self.psb(3), self.psb(4), self.psb(5)]
        po = [self.psb(0), self.psb(1), self.psb(2), self.psb(3)]
        j0 = 0
        cnt = 0
        cnt2 = 0
        for nj in parts:
            k.dma("pool", wout[:, 0:nj, :], View(wo_dram[:, j0:j0 + nj, :], None, []), "wout")
            for sl in range(nj // 2):
                s = j0 // 2 + sl
                gi = fi * 11 + s
                slot = self.wring[gi % 3]
                for jj in range(2):
                    jl = sl * 2 + jj
                    for t in range(NT):
                        ts_ = slice(t * 512, (t + 1) * 512)
                        g_ps, u_ps, sgt = pg[cnt % 3], pu[cnt % 3], sg[cnt % 3]
                        cnt += 1
                        for kt in range(KT):
                            k.mm(g_ps[:], slot[0][:, kt, jj * 128:(jj + 1) * 128], xnT[:, kt, ts_], start=(kt == 0), stop=(kt == KT - 1))
                        for kt in range(KT):
                            k.mm(u_ps[:], slot[1][:, kt, jj * 128:(jj + 1) * 128], xnT[:, kt, ts_], start=(kt == 0), stop=(kt == KT - 1))
                        k.act(sgt[:], g_ps[:], AF.Silu)
                        k.tt("dve", actT[:, jl, ts_], sgt[:], u_ps[:], ALU.mult)
                self.ffn_issue(min(gi + 4, next_limit))
            for mo in range(KT):
                for t in range(NT):
                    ts_ = slice(t * 512, (t + 1) * 512)
                    o_ps = po[cnt2 % 4]
                    cnt2 += 1
                    for jl in range(nj):
                        k.mm(o_ps[:], wout[:, jl, mo * 128:(mo + 1) * 128], actT[:, jl, ts_], start=(jl == 0), stop=(jl == nj - 1))
                    k.stt(self.hT[:, mo, ts_], o_ps[:], 0.5, self.hT[:, mo, ts_], ALU.mult, ALU.add)
            j0 += nj
        A.release(m)

    def final_norm(self):
        k, A = self.k, self.A
        m = A.mark()
        sq = A.tile([128, 8, 512], BF16)
        rs = [A.tile([128, 512], F32) for _ in range(2)]
        yo = [A.tile([128, 8, 512], F32) for _ in range(2)]
        ssq = self.psb(6)
        for t in range(NT):
            ts_ = slice(t * 512, (t + 1) * 512)
            k.act(sq[:], self.hT[:, :, ts_], AF.Square)
            for kt in range(KT):
                k.mm(ssq[:], self.ones[:], sq[:, kt, :], start=(kt == 0), stop=(kt == KT - 1))
            r = rs[t % 2]
            k.act(r[:], ssq[:], AF.Sqrt, bias=self.epsT[:, 0:1], scale=1.0 / D)
            k.recip(r[:], r[:])
            y = yo[t % 2]
            for kt in range(KT):
                k.stt(y[:, kt, :], self.hT[:, kt, ts_], self.vc[:, VC_FG + kt:VC_FG + kt + 1], r[:], ALU.mult, ALU.mult)
            k.dma("sp", self.outT[:, :, ts_], y[:], ("out", t % 2))
        A.release(m)

    def sub(self, t, c0, c1, shape):
        ap = t.h[:, c0:c1]
        if len(shape) == 3:
            ap = ap.rearrange("p (a b) -> p a b", a=shape[1], b=shape[2])
        return Tile(ap, shape, t.tid, base=t.base + c0 * t.isz, isz=t.isz)

    def proj_fm(self, wt, c0, xnT, dest, ci, scale=None):
        k = self.k
        for t in range(NT):
            ts_ = slice(t * 512, (t + 1) * 512)
            ps = self.psb(self.pcnt % 4)
            self.pcnt += 1
            for kt in range(KT):
                k.mm(ps[:], wt[:, kt, c0:c0 + 128], xnT[:, kt, ts_], start=(kt == 0), stop=(kt == KT - 1))
            if self.pcnt % 2 == 0:
                k.act(dest[:, ci, ts_], ps[:], AF.Copy, scale=(1.0 if scale is None else scale))
            else:
                k.ts("dve", dest[:, ci, ts_], ps[:], (1.0 if scale is None else scale), None, op0=ALU.mult)

    def proj_tm(self, wt, c0, n, xnT, dest, d0):
        k = self.k
        for blk in range(NB):
            ps = self.PS.tile([128, n], F32, off=(self.pcnt % 4) * 2048)
            self.pcnt += 1
            for kt in range(KT):
                k.mm(ps[:], xnT[:, kt, blk * 128:(blk + 1) * 128], wt[:, kt, c0:c0 + n], start=(kt == 0), stop=(kt == KT - 1))
            if self.pcnt % 2 == 0:
                k.copy("act", dest[:, blk, d0:d0 + n], ps[:])
            else:
                k.copy("dve", dest[:, blk, d0:d0 + n], ps[:])

    def wslab(self, wname, ncols_total, c0, hs, keyid):
        w = self.w(wname).ap().rearrange("(kt p) c -> p kt c", p=128)
        self.k.dma("pool", hs[:], View(w[:, :, c0:c0 + 256], None, []), ("wr", keyid // 2, keyid % 2))

    def attn(self):
        k, A = self.k, self.A
        base = A.mark()
        self.pcnt = 0
        hs = [self.wring[i // 2][i % 2] for i in range(6)]
        W0 = self.wring[0][0].base
        xnT = A.tile([128, 8, S], BF16, off=base)
        oTa = A.tile([128, 4, S], BF16, off=base + 32768)
        oTb = A.tile([128, 4, S], BF16, off=base + 0)
        R1 = base + 49152
        A.top = R1 + 53312
        A.peak = max(A.peak, A.top)
        assert A.top <= A.nbytes, A.top
        qaT = A.tile([128, 4, S], BF16, off=R1)
        kkT = A.tile([128, 2, S], BF16, off=R1 + 16384)
        va = A.tile([128, NB, 128], BF16, off=R1 + 24576)
        PT = [A.tile([128, 8, 256], BF16, off=R1 + 28672 + i * 4096) for i in range(3)]
        eS = [A.tile([128, 4, 256], F32, off=R1 + 40960 + i * 4096) for i in range(2)]
        den = [A.tile([128, 4, 128], F32, off=R1 + 49152 + i * 2048) for i in range(2)]
        es = A.tile([128, 8], F32, off=R1 + 53248)
        esk2 = A.tile([128, 4], F32, off=R1 + 53280)
        qbT = A.tile([128, 4, S], BF16, off=R1)
        kbT = A.tile([128, 4, S], BF16, off=R1 + 16384)
        vb = A.tile([128, NB, 512], BF16, off=R1 + 32768)
        Etab = self.sub(self.cfT, CF_E, CF_E + 2048, [128, 8, 256])
        sbm = self.sub(self.cfT, CF_SBM, CF_SBM + 128, [128, 128])

        top_save = A.top
        A.top = R1 + 28672
        self.rms_norm_T(VC_NG + 1 * 8, xnT)
        A.top = top_save

        wname = "attn_w_in"
        wv = self.w(wname).ap().rearrange("(kt p) c -> p kt c", p=128)
        self.wslab(wname, 2304, 0, hs[0], 0)
        self.wslab(wname, 2304, 256, hs[1], 1)
        self.wslab(wname, 2304, 512, hs[2], 2)
        for i, (src, dst) in enumerate([(512, 0), (512, 64), (576, 128), (576, 192)]):
            k.dma("pool", hs[3][:, :, dst:dst + 64], View(wv[:, :, src:src + 64], None, []), ("kk", i))
        for ci in range(4):
            self.proj_fm(hs[ci // 2], (ci % 2) * 128, xnT, qaT, ci, scale=0.125)
        self.wslab(wname, 2304, 768, hs[4], 4)
        self.wslab(wname, 2304, 1024, hs[5], 5)
        self.proj_fm(hs[3], 0, xnT, kkT, 0)
        self.proj_fm(hs[3], 128, xnT, kkT, 1)
        self.proj_tm(hs[2], 128, 128, xnT, va, 0)
        self.wslab(wname, 2304, 1280, hs[0], 0)
        self.wslab(wname, 2304, 1536, hs[1], 1)
        self.wslab(wname, 2304, 1792, hs[2], 2)
        self.wslab(wname, 2304, 2048, hs[3], 3)

        cut = int(os.environ.get('ATTN_CUT', '99'))
        if cut <= 1:
            A.release(base)
            return
        k.act(es[:], self.vc[:, VC_SK:VC_SK + 8], AF.Exp)
        k.copy("dve", esk2[0:64, :], es[0:64, 0:8:2])
        k.copy("dve", esk2[64:128, :], es[64:128, 1:8:2])
        esb = View(esk2.h[:, :].unsqueeze(2).to_broadcast([128, 4, 128]), esk2.tid, esk2[:].rects)
        def swa_scores(kb):
            nq = 256 if kb < NB - 1 else 128
            pt = PT[kb % 3]
            for hg in range(2):
                Sps = self.PS.tile([128, 4, 256], F32, off=(hg * 2) * 2048)
                for i in range(4):
                    h = 4 * hg + i
                    c, po = h // 2, (h % 2) * 64
                    j = (i % 2) * 2 + (i // 2)
                    k.mm(Sps[:, j, 0:nq], kkT[po:po + 64, hg, kb * 128:(kb + 1) * 128],
                         qaT[po:po + 64, c, kb * 128:kb * 128 + nq])
                e = eS[hg]
                k.act(e[:, :, 0:nq], Sps[:, :, 0:nq], AF.Exp)
                for p in range(2):
                    hsel = slice(4 * hg + p, 4 * hg + p + 3, 2)
                    k.tt("dve", pt[:, hsel, 0:nq], e[:, 2 * p:2 * p + 2, 0:nq], Etab[:, hsel, 0:nq], ALU.mult)

        def swa_out(kb):
            pt = PT[kb % 3]
            O = self.PS.tile([128, 4, 128], F32, off=(4 + 2 * (kb % 2)) * 2048)
            Dn = self.PS.tile([128, 4, 128], F32, off=(5 + 2 * (kb % 2)) * 2048)
            ptp = PT[(kb - 1) % 3]
            for h in range(8):
                g, c, po = h // 4, h // 2, (h % 2) * 64
                k.mm(O[po:po + 64, c, :], va[:, kb, g * 64:(g + 1) * 64], pt[:, h, 0:128], start=True, stop=(kb == 0), tile_position=(0, po))
                if kb > 0:
                    k.mm(O[po:po + 64, c, :], va[:, kb - 1, g * 64:(g + 1) * 64], ptp[:, h, 128:256], start=False, stop=True, tile_position=(0, po))
                k.mm(Dn[po:po + 64, c, :], self.ones[:, 0:64], pt[:, h, 0:128], start=True, stop=(kb == 0), tile_position=(0, po))
                if kb > 0:
                    k.mm(Dn[po:po + 64, c, :], self.ones[:, 0:64], ptp[:, h, 128:256], start=False, stop=True, tile_position=(0, po))
            dn = den[kb % 2]
            k.tt("dve", dn[:], Dn[:], esb, ALU.add)
            k.recip(dn[:], dn[:])
            k.tt("dve", oTa[:, :, kb * 128:(kb + 1) * 128], O[:], dn[:], ALU.mult)

        swa_scores(0)
        for kb in range(NB):
            if kb + 1 < NB:
                swa_scores(kb + 1)
            swa_out(kb)

        if cut <= 2:
            A.release(base)
            return
        self.proj_fm(hs[4], 0, xnT, qbT, 0, scale=0.125)
        self.proj_fm(hs[4], 128, xnT, qbT, 1, scale=0.125)
        self.proj_fm(hs[5], 0, xnT, qbT, 2, scale=0.125)
        self.proj_fm(hs[5], 128, xnT, qbT, 3, scale=0.125)
        self.proj_fm(hs[0], 0, xnT, kbT, 0)
        self.proj_fm(hs[0], 128, xnT, kbT, 1)
        self.proj_fm(hs[1], 0, xnT, kbT, 2)
        self.proj_fm(hs[1], 128, xnT, kbT, 3)
        self.proj_tm(hs[2], 0, 256, xnT, vb, 0)
        self.proj_tm(hs[3], 0, 256, xnT, vb, 256)

        if cut <= 3:
            A.release(base)
            return
        k.new_epoch()
        X1 = base + 16384

        class Bf:
            pass
        bA, bB = Bf(), Bf()
        bB.e32 = A.tile([128, 2048], F32, off=W0)
        bB.spP = A.tile([128, 2049], F32, off=W0 + 8192)
        bB.xw = A.tile([128, 2048], BF16, off=W0 + 16416)
        bA.xw = A.tile([128, 1024], BF16, off=W0 + 20512)
        bA.npt = A.tile([128, 1], F32, off=W0 + 22560)
        bB.npt = A.tile([128, 1], F32, off=W0 + 22592)
        bB.WT = A.tile([128, 16, 128], BF16, off=X1)
        bA.e32 = A.tile([128, 1024], F32, off=X1 + 4096)
        bA.spP = A.tile([128, 1025], F32, off=X1 + 8192)
        bA.WT = A.tile([128, 8, 128], BF16, off=X1 + 12320)
        k.memset("pool", bA.spP[:, 0:1], 0.0)
        k.memset("pool", bB.spP[:, 0:1], 0.0)
        Zt = self.PS.tile([128, 2048], F32, off=0)
        TPb = [self.PS.tile([128, 8, 128], BF16, off=(4 + i) * 2048) for i in range(2)]
        Ob = [self.PS.tile([128, 128], F32, off=(6 + i) * 2048) for i in range(2)]
        order = []
        for i in range(8):
            order += [i, 15 - i]
        blocks = [(h, qb) for h in range(8) for qb in order]
        nblk = len(blocks)
        st = {"o": 0}

        def geo(i):
            h, qb = blocks[i]
            return h, qb, h // 2, (h % 2) * 64, (bA if qb < 8 else bB), (qb + 1) * 128, slice(qb * 128, (qb + 1) * 128)

        def s_qk(i):
            h, qb, c, po, B, Lk, qs = geo(i)
            nkc = (Lk + 511) // 512
            for kc in range(nkc):
                w = min(512, Lk - kc * 512)
                k.mm(Zt[:, kc * 512:kc * 512 + w], qbT[po:po + 64, c, qs], kbT[po:po + 64, c, kc * 512:kc * 512 + w],
                     start=True, stop=(kc < nkc - 1))
            k.mm(Zt[:, qb * 128:Lk], self.ident[:], self.mbT[:], start=False, stop=True)

        def s_exp(i):
            h, qb, c, po, B, Lk, qs = geo(i)
            k.act(B.e32[:, 0:Lk], Zt[:, 0:Lk], AF.Exp)

        def s_ln(i):
            h, qb, c, po, B, Lk, qs = geo(i)
            k.act(B.spP[:, 1:Lk + 1], B.e32[:, 0:Lk], AF.Ln, bias=self.epsT[:, 1:2])

        def s_scan(i):
            h, qb, c, po, B, Lk, qs = geo(i)
            k.scan(B.spP[:, 1:Lk + 1], B.spP[:, 1:Lk + 1], B.spP[:, 1:Lk + 1], 0.0, ALU.add, ALU.max)
            k.ts("dve", B.npt[:], B.spP[:, Lk:Lk + 1], -1.0, None, op0=ALU.mult)

        def s_exp2(i):
            h, qb, c, po, B, Lk, qs = geo(i)
            k.act(B.xw[:, 0:Lk], B.spP[:, 0:Lk], AF.Exp, bias=B.npt[:])

        def s_mult(i):
            h, qb, c, po, B, Lk, qs = geo(i)
            k.tt("dve", B.xw[:, 0:Lk], B.e32[:, 0:Lk], B.xw[:, 0:Lk], ALU.mult)

        def s_tr(i):
            h, qb, c, po, B, Lk, qs = geo(i)
            for g, sb0 in enumerate(range(0, qb + 1, 8)):
                n = min(8, qb + 1 - sb0)
                for ii in range(n):
                    k.transpose(TPb[g][:, ii, :], B.xw[:, (sb0 + ii) * 128:(sb0 + ii + 1) * 128], self.ident[:])

        def s_evac(i, eng_sel):
            h, qb, c, po, B, Lk, qs = geo(i)
            for g, sb0 in enumerate(range(0, qb + 1, 8)):
                n = min(8, qb + 1 - sb0)
                eng = "dve" if g == 0 else "act"
                if eng == eng_sel:
                    k.copy(eng, B.WT[:, sb0:sb0 + n, :], TPb[g][:, 0:n, :])

        def s_pv(i):
            h, qb, c, po, B, Lk, qs = geo(i)
            o = Ob[st["o"] % 2]
            st["o"] += 1
            for sb in range(qb + 1):
                k.mm(o[po:po + 64, :], vb[:, sb, c * 128 + po:c * 128 + po + 64], B.WT[:, sb, :],
                     start=(sb == 0), stop=(sb == qb), tile_position=(0, po))
            k.copy("act", oTb[po:po + 64, c, qs], o[po:po + 64, :])

        ok = lambda j: 0 <= j < nblk
        s_qk(0)
        for i in range(-1, nblk + 1):
            if ok(i + 1):
                s_exp(i + 1)
            if ok(i + 2):
                s_qk(i + 2)
            if ok(i):
                s_scan(i)
            if ok(i - 1):
                s_evac(i - 1, "act")
            if ok(i + 1):
                s_ln(i + 1)
            if ok(i):
                s_exp2(i)
            if ok(i - 1):
                s_evac(i - 1, "dve")
                s_pv(i - 1)
            if ok(i):
                s_mult(i)
                s_tr(i)

        if cut <= 4:
            A.release(base)
            return
        self.out_proj("attn_w_out", [(oTa, c) for c in range(4)] + [(oTb, c) for c in range(4)])
        A.release(base)

    def out_proj(self, wname, srcs):
        k = self.k
        hs = [self.wring[i // 2][i % 2] for i in range(6)]
        wo = self.w(wname).ap().rearrange("(c p) m -> p c m", p=128)
        for i in range(4):
            k.dma("pool", hs[i][:], View(wo[:, :, i * 256:(i + 1) * 256], None, []), ("wr", i // 2, i % 2))
        for mo in range(KT):
            hsl, c0 = hs[mo // 2], (mo % 2) * 128
            for t in range(NT):
                ts_ = slice(t * 512, (t + 1) * 512)
                ps = self.psb(self.pcnt % 4)
                self.pcnt += 1
                for c in range(8):
                    src, ci = srcs[c]
                    k.mm(ps[:], hsl[:, c, c0:c0 + 128], src[:, ci, ts_], start=(c == 0), stop=(c == 7))
                k.tt("dve", self.hT[:, mo, ts_], ps[:], self.hT[:, mo, ts_], ALU.add)

    def rec(self):
        k, A = self.k, self.A
        base = A.mark()
        self.pcnt = 0
        xnT = A.tile([128, 8, S], BF16)
        oT = A.tile([128, 8, S], BF16)
        self.rms_norm_T(VC_NG + (3 + 1) * 8, xnT)
        W0 = self.wring[0][0].base
        slots = [A.tile([128, 8, 512], BF16, off=W0 + i * 8192) for i in range(3)]
        slots.append(A.tile([128, 8, 512], BF16))
        wv = self.w("rec_w_in").ap().rearrange("(kt p) c -> p kt c", p=128)

        def load(slot_i, j, c0):
            k.dma("pool", slots[slot_i][:, :, j * 128:(j + 1) * 128], View(wv[:, :, c0:c0 + 128], None, []), ("wq", slot_i, j))

        sm = A.tile([128, 48], F32)
        L0 = self.vc[:, VC_LB:VC_LB + 8:2]
        L1 = self.vc[:, VC_LB + 1:VC_LB + 8:2]
        mx, d0, d1, ssum, lb, oml = [sm[:, i * 4:(i + 1) * 4] for i in range(6)]
        k.tt("dve", mx, L0, L1, ALU.max)
        k.tt("dve", d0, L0, mx, ALU.subtract)
        k.tt("dve", d1, L1, mx, ALU.subtract)
        k.act(d0, d0, AF.Exp)
        k.act(d1, d1, AF.Exp)
        k.tt("dve", ssum, d0, d1, ALU.add)
        k.recip(ssum, ssum)
        k.tt("dve", d0, d0, ssum, ALU.mult)
        k.tt("dve", d1, d1, ssum, ALU.mult)
        k.tt("dve", d1, d0, d1, ALU.add)
        k.tt("dve", lb, d1, d0, ALU.subtract)
        k.ts("dve", oml, lb, -1.0, 1.0, op0=ALU.mult, op1=ALU.add)

        mconv = A.mark()
        xbuf = A.tile([128, S + 2], F32)
        gbs = A.tile([128, S], F32)
        ybuf = A.tile([128, S], F32)
        usb = [A.tile([128, 512], F32) for _ in range(2)]
        k.memset("pool", xbuf[:, 0:2], 0.0)
        for c in range(3):
            for j in range(3):
                load(c, j, 2048 + j * 512 + c * 128)
        bcnt = 0
        for c in range(4):
            sl = slots[c % 4 if c < 3 else 3]
            if c == 3:
                for j in range(3):
                    load(3, j, 2048 + j * 512 + c * 128)
            for t in range(NT):
                ts_ = slice(t * 512, (t + 1) * 512)
                pss = []
                for j in range(3):
                    ps = self.psb(bcnt % 8)
                    bcnt += 1
                    for kt in range(KT):
                        k.mm(ps[:], sl[:, kt, j * 128:(j + 1) * 128], xnT[:, kt, ts_], start=(kt == 0), stop=(kt == KT - 1))
                    pss.append(ps)
                u = usb[t % 2]
                k.copy("act", u[:], pss[2][:])
                k.tt("dve", xbuf[:, 2 + t * 512:2 + (t + 1) * 512], pss[1][:], u[:], ALU.mult)
                k.copy("act", gbs[:, ts_], pss[0][:])
            cw = VC_CW + c * 3
            k.ts("dve", ybuf[:], xbuf[:, 2:S + 2], self.vc[:, cw + 2:cw + 3], None, op0=ALU.mult)
            k.stt(ybuf[:], xbuf[:, 1:S + 1], self.vc[:, cw + 1:cw + 2], ybuf[:], ALU.mult, ALU.add)
            k.stt(ybuf[:], xbuf[:, 0:S], self.vc[:, cw:cw + 1], ybuf[:], ALU.mult, ALU.add)
            k.tt("dve", oT[:, 4 + c, :], gbs[:], ybuf[:], ALU.mult)
        A.release(mconv)

        for h in range(4):
            for j in range(4):
                load(h, j, j * 512 + h * 128)
        f32t = lambda: A.tile([128, 512], F32)
        T_sq, T_b1, T_b2, T_b3 = f32t(), f32t(), f32t(), f32t()
        rs2 = T_sq
        T_kh = A.tile([128, 512], BF16)
        T_sq2 = T_kh

        class Hd:
            pass
        hd = []
        for h in range(4):
            H = Hd()
            H.qt = A.tile([128, 512], BF16)
            H.kt = A.tile([128, 512], BF16)
            H.khT = A.tile([128, 4, 128], BF16)
            H.vt = A.tile([128, 4, 128], BF16)
            H.AT = A.tile([128, 4, 64], BF16)
            H.sgl = A.tile([128, 512], BF16)
            H.eGl = A.tile([128, 8], F32)
            H.sf = A.tile([128, 128], F32)
            H.sb = [A.tile([128, 128], BF16) for _ in range(2)]
            k.memset("pool", H.sf[:], 0.0)
            hd.append(H)
        rst = self.sub(self.cfT, CF_RST, CF_RST + 512, [128, 512])
        hgm = self.sub(self.cfT, CF_HGM, CF_HGM + 64, [128, 64])
        hgmb = View(hgm.h[:, :].unsqueeze(1).to_broadcast([128, 4, 64]), hgm.tid, hgm[:].rects)
        ops = [self.psb(h) for h in range(4)]
        ups = [self.PS.tile([128, 128], F32, off=(4 + i) * 2048) for i in range(2)]
        pw = 0
        ucnt = 0

        def pwbank(shape=(128, 512), dtype=F32):
            nonlocal pw
            pw += 1
            return self.PS.tile(list(shape), dtype, off=(6 + pw % 2) * 2048)

        for t in range(NT):
            ts_ = slice(t * 512, (t + 1) * 512)
            for h in range(4):
                H, sl = hd[h], slots[h]

                def proj(j):
                    ps = pwbank()
                    for kt in range(KT):
                        k.mm(ps[:], sl[:, kt, j * 128:(j + 1) * 128], xnT[:, kt, ts_], start=(kt == 0), stop=(kt == KT - 1))
                    return ps
                psq = proj(0)
                k.act(T_sq[:], psq[:], AF.Silu)
                psg = proj(3)
                k.act(H.sgl[:], psg[:], AF.Silu)
                psf = proj(1)
                k.act(T_b1[:], psf[:], AF.Sigmoid)
                k.ts("dve", T_b1[:], T_b1[:], oml[:, h:h + 1] if False else sm[:, 20 + h:21 + h], sm[:, 16 + h:17 + h], op0=ALU.mult, op1=ALU.add)
                k.ts("dve", T_b2[:], T_b1[:], -1.0, 1.0, op0=ALU.mult, op1=ALU.add)
                k.act(T_b1[:], T_b1[:], AF.Ln)
                k.scan(T_b3[:], rst[:], T_b1[:], 0.0, ALU.mult, ALU.add)
                k.act(T_b1[:], T_b3[:], AF.Exp)
                k.act(T_b3[:], T_b3[:], AF.Exp, scale=-1.0)
                k.copy("dve", H.eGl[:], T_b1[:, 63:512:64])
                k.tt("dve", H.qt[:], T_sq[:], T_b1[:], ALU.mult)
                k.tt("dve", T_b2[:], T_b2[:], T_b3[:], ALU.mult)
                k.copy("dve", H.kt[:], T_b2[:])
                eb = View(H.eGl.h[:, :].unsqueeze(2).to_broadcast([128, 8, 64]), H.eGl.tid, H.eGl[:].rects)
                khv = View(T_kh.h[:, :].rearrange("p (a b) -> p a b", a=8, b=64), T_kh.tid, T_kh[:].rects)
                b2v = View(T_b2.h[:, :].rearrange("p (a b) -> p a b", a=8, b=64), T_b2.tid, T_b2[:].rects)
                k.tt("dve", khv, b2v, eb, ALU.mult)
                for blk in range(4):
                    ps = pwbank((128, 128))
                    tok = slice(t * 512 + blk * 128, t * 512 + (blk + 1) * 128)
                    for kt in range(KT):
                        k.mm(ps[:], xnT[:, kt, tok], sl[:, kt, 256:384], start=(kt == 0), stop=(kt == KT - 1))
                    k.copy("act", H.vt[:, blk, :], ps[:])
                tp = pwbank((128, 4, 128), BF16)
                for blk in range(4):
                    k.transpose(tp[:, blk, :], T_kh[:, blk * 128:(blk + 1) * 128], self.ident[:])
                k.copy("dve", H.khT[:], tp[:])
                ap_ = pwbank((128, 4, 64))
                for blk in range(4):
                    for half in range(2):
                        t0 = (2 * blk + half) * 64
                        k.mm(ap_[half * 64:(half + 1) * 64, blk, :], H.kt[:, t0:t0 + 64], H.qt[:, t0:t0 + 64], tile_position=(0, half * 64))
                k.tt("dve", H.AT[:], ap_[:], hgmb, ALU.mult)
            for cc in range(8):
                blk, half = cc // 2, cc % 2
                p0 = half * 64
                t0 = cc * 64
                gc = t * 8 + cc
                for h in range(4):
                    H = hd[h]
                    k.mm(ops[h][:, t0:t0 + 64], H.vt[p0:p0 + 64, blk, :], H.AT[p0:p0 + 64, blk, :], start=True, stop=(gc == 0))
                    if gc > 0:
                        k.mm(ops[h][:, t0:t0 + 64], H.sb[gc % 2][:], H.qt[:, t0:t0 + 64], start=False, stop=True)
                    if gc < 31:
                        u = ups[ucnt % 2]
                        ucnt += 1
                        k.mm(u[:], H.khT[p0:p0 + 64, blk, :], H.vt[p0:p0 + 64, blk, :])
                        k.stt(H.sf[:], H.sf[:], H.eGl[:, cc:cc + 1], u[:], ALU.mult, ALU.add)
                        k.copy("act", H.sb[(gc + 1) % 2][:], H.sf[:])
            for h in range(4):
                H = hd[h]
                k.act(T_sq2[:], ops[h][:], AF.Square)
                ssq = pwbank()
                k.mm(ssq[:], self.ones[:], T_sq2[:])
                k.act(rs2[:], ssq[:], AF.Sqrt, bias=self.epsT[:, 0:1], scale=1.0 / 128)
                k.recip(rs2[:], rs2[:])
                k.stt(rs2[:], ops[h][:], self.vc[:, VC_HN:VC_HN + 1], rs2[:], ALU.mult, ALU.mult)
                k.tt("dve", oT[:, h, ts_], rs2[:], H.sgl[:], ALU.mult)

        self.out_proj("rec_w_out", [(oT, c) for c in range(8)])
        A.release(base)


_CONSTS = None


def kernel(x, norm_g, ffn_w_in, ffn_w_out, attn_w_in, attn_sinks, attn_w_out, rec_w_in,
           hgrn_lb_logits, hgrn_norm_g, conv_w, rec_w_out, final_g, _stages=ALL_STAGES, _xT=None):
    global _CONSTS
    if _CONSTS is None:
        _CONSTS = host_consts()
    cf, cb = _CONSTS
    f32 = lambda a: np.ascontiguousarray(np.asarray(a, dtype=np.float32))
    x = f32(x)
    vecs = host_vecs(f32(norm_g), f32(final_g), f32(hgrn_norm_g), f32(conv_w), f32(hgrn_lb_logits), f32(attn_sinks))
    prog = Prog(stages=_stages)
    nc = prog.build()
    allw = {"attn_w_in": lambda: f32(attn_w_in)[0], "attn_w_out": lambda: f32(attn_w_out)[0],
            "rec_w_in": lambda: f32(rec_w_in)[0], "rec_w_out": lambda: f32(rec_w_out)[0]}
    for l in range(2):
        for f in range(2):
            allw["ffn_w_in_%d%d" % (l, f)] = (lambda l=l, f=f: f32(np.asarray(ffn_w_in)[l, f]))
            allw["ffn_w_out_%d%d" % (l, f)] = (lambda l=l, f=f: f32(np.asarray(ffn_w_out)[l, f]))
    shared = {"cf": cf, "cb": cb, "vecs": vecs}
    for name in prog.wd:
        shared[name] = allw[name]()
    in_maps = []
    for b in range(8):
        src = x[b] if _xT is None else f32(_xT[b])
        xT = np.ascontiguousarray(src.T.reshape(8, 128, S).transpose(1, 0, 2))
        in_maps.append({"xT": xT, **shared})
    res = run_bass_kernel_spmd(nc, in_maps, core_ids=list(range(8)))
    out = np.empty((8, S, D), np.float32)
    for b in range(8):
        oT = np.asarray(res.results[b]["outT"])
        out[b] = oT.transpose(2, 1, 0).reshape(S, D)
    return out
```

```python
import os
import numpy as np
import ml_dtypes
import concourse.bass as bass
import concourse.mybir as mybir
from concourse.bass_utils import run_bass_kernel_spmd
from contextlib import ExitStack

F32 = mybir.dt.float32
BF16 = mybir.dt.bfloat16
ALU = mybir.AluOpType
AF = mybir.ActivationFunctionType

S = 2048
D = 1024
KT = 8
NT = 4
NB = 16
DFF = 2816
NJ = 22
EPS = 1e-6


class View:
    __slots__ = ("ap", "tid", "rects")

    def __init__(self, ap, tid, rects):
        self.ap = ap
        self.tid = tid
        self.rects = rects

    def with_ap(self, ap):
        return View(ap, self.tid, self.rects)


class Tile:
    def __init__(self, handle, shape, tid, base=0, isz=4, p0=0, gran=None):
        self.h = handle
        self.shape = tuple(shape)
        self.tid = tid
        self.base = base
        self.isz = isz
        self.p0 = p0
        self.gran = gran

    def __getitem__(self, idx):
        if not isinstance(idx, tuple):
            idx = (idx,)
        idx = idx + (slice(None),) * (len(self.shape) - len(idx))
        ap = self.h[idx]
        if self.tid is None:
            return View(ap, None, [])
        rng = []
        for ix, n in zip(idx, self.shape):
            if isinstance(ix, int):
                rng.append((ix, ix + 1))
            else:
                s, e, st = ix.indices(n)
                if st != 1:
                    e = s + ((e - s + st - 1) // st - 1) * st + 1
                rng.append((s, e))
        ivs = [(0, 1)]
        size = 1
        for (s, e), n in zip(reversed(rng[1:]), reversed(self.shape[1:])):
            if len(ivs) == 1 and ivs[0] == (0, size):
                ivs = [(s * size, e * size)]
            else:
                ivs = [(a * size + lo, a * size + hi) for a in range(s, e) for (lo, hi) in ivs]
            size *= n
        b, z = self.base, self.isz
        rects = [(self.p0 + rng[0][0], self.p0 + rng[0][1], b + lo * z, b + hi * z) for lo, hi in ivs]
        if self.gran is not None:
            gb, gp = self.gran
            rr = set()
            for (pa, pb, lo, hi) in rects:
                rr.add((pa // gp * gp, (pb + gp - 1) // gp * gp, lo // gb * gb, (hi + gb - 1) // gb * gb))
            rects = sorted(rr)
        return View(ap, self.tid, rects)


class Arena:
    def __init__(self, k, handle, nbytes, gran=None):
        self.h = handle
        self.nbytes = nbytes
        self.tid = k._new_tid()
        self.top = 0
        self.peak = 0
        self.gran = gran

    def mark(self):
        return self.top

    def release(self, m):
        self.top = m

    def tile(self, shape, dtype, off=None):
        isz = 4 if dtype == F32 else 2
        n = 1
        for d in shape[1:]:
            n *= d
        nb = (n * isz + 31) // 32 * 32
        if off is None:
            off = self.top
            self.top += nb
            self.peak = max(self.peak, self.top)
            assert self.top <= self.nbytes, ("arena overflow", self.top, self.nbytes)
        ap = self.h[:, off // 4:(off + nb) // 4]
        if dtype != F32:
            ap = ap.bitcast(dtype)
        ap = ap[0:shape[0], 0:n]
        if len(shape) == 3:
            ap = ap.rearrange("p (a b) -> p a b", a=shape[1], b=shape[2])
        elif len(shape) == 4:
            ap = ap.rearrange("p (a b c) -> p a b c", a=shape[1], b=shape[2], c=shape[3])
        return Tile(ap, shape, self.tid, base=off, isz=isz, gran=self.gran)


class Ins:
    __slots__ = ("eng", "fn", "deps", "signal", "count", "is_dma", "sem", "_barred", "epoch", "idx")

    def __init__(self, eng, fn, is_dma=False):
        self.epoch = 0
        self.idx = 0
        self.eng = eng
        self.fn = fn
        self.deps = []
        self.signal = False
        self.count = 0
        self.is_dma = is_dma
        self.sem = None
        self._barred = False


ENGS = ("pe", "act", "dve", "pool", "sp")


class K:
    def __init__(self, nc):
        self.nc = nc
        self.streams = {e: [] for e in ENGS}
        self.acc = {}
        self._tid = 0
        self.dma_keys = {}
        self.n_ins = 0
        self.psum_tid = None
        self.epoch = 0

    def _new_tid(self):
        self._tid += 1
        return self._tid

    @staticmethod
    def _ov(a, b):
        return a[0] < b[1] and b[0] < a[1] and a[2] < b[3] and b[2] < a[3]

    @staticmethod
    def _cover(a, b):
        return a[0] <= b[0] and a[1] >= b[1] and a[2] <= b[2] and a[3] >= b[3]

    def _track(self, ins, reads, writes):
        deps = {}
        ov = self._ov
        for v in reads:
            if v is None or v.tid is None:
                continue
            lst = self.acc.setdefault(v.tid, [])
            for r in v.rects:
                for (rect, j, kind) in lst:
                    if kind == "w" and ov(rect, r):
                        deps[id(j)] = j
        for v in writes:
            if v is None or v.tid is None:
                continue
            lst = self.acc.setdefault(v.tid, [])
            for r in v.rects:
                for (rect, j, kind) in lst:
                    if ov(rect, r):
                        deps[id(j)] = j
        for v in reads:
            if v is None or v.tid is None:
                continue
            lst = self.acc[v.tid]
            for r in v.rects:
                if not ins.is_dma:
                    lst[:] = [e for e in lst if not (e[2] == "r" and e[0] == r and e[1].eng == ins.eng and not e[1].is_dma)]
                lst.append((r, ins, "r"))
        for v in writes:
            if v is None or v.tid is None:
                continue
            lst = self.acc[v.tid]
            for r in v.rects:
                lst[:] = [e for e in lst if not self._cover(r, e[0])]
                lst.append((r, ins, "w"))
        deps.pop(id(ins), None)
        for j in deps.values():
            if j.eng == "pe" and ins.eng == "pe" and not j.is_dma and not ins.is_dma:
                continue
            ins.deps.append(j)
            j.signal = True

    def new_epoch(self):
        self.epoch += 1

    def op(self, eng, fn, reads, writes):
        ins = Ins(eng, fn)
        ins.epoch = self.epoch
        self.n_ins += 1
        if self.psum_tid is not None:
            pr = [v for v in reads if v is not None and v.tid == self.psum_tid]
            if pr:
                reads = [v for v in reads if v is None or v.tid != self.psum_tid]
                writes = list(writes) + pr
        self._track(ins, reads, writes)
        self.streams[eng].append(ins)
        return ins

    def dma(self, queue, out, in_, key, **kw):
        ins = Ins(queue, None, is_dma=True)
        self.n_ins += 1
        kk = self.dma_keys.setdefault(key, [0])
        kk[0] += 16
        ins.sem = key
        ins.count = kk[0]
        oa, ia = out.ap, in_.ap
        ins.fn = lambda e: e.dma_start(out=oa, in_=ia, **kw)
        self._track(ins, [in_], [out])
        self.streams[queue].append(ins)
        return ins

    def barrier(self):
        lasts = []
        for e in ENGS:
            comp = [i for i in self.streams[e] if not i.is_dma and i.fn is not None]
            if comp:
                lasts.append(comp[-1])
            for i in self.streams[e]:
                if i.is_dma and not i._barred:
                    i._barred = True
                    lasts.append(i)
        for e in ENGS:
            ins = Ins(e, None)
            ins.epoch = self.epoch
            for j in lasts:
                if j.eng == e and not j.is_dma and e == "pe":
                    continue
                ins.deps.append(j)
                j.signal = True
            self.streams[e].append(ins)
        self.acc = {}

    def mm(self, out, lhsT, rhs, start=True, stop=True, **kw):
        o, l, r = out.ap, lhsT.ap, rhs.ap
        return self.op("pe", lambda e: e.matmul(o, l, r, start=start, stop=stop, **kw), [lhsT, rhs], [out])

    def transpose(self, out, in_, ident):
        o, i, d = out.ap, in_.ap, ident.ap
        return self.op("pe", lambda e: e.transpose(o, i, d), [in_, ident], [out])

    def act(self, out, in_, func, bias=None, scale=1.0, accum_out=None):
        o, i = out.ap, in_.ap
        kw = {}
        rd = [in_]
        wr = [out]
        if bias is not None:
            if isinstance(bias, View):
                kw["bias"] = bias.ap
                rd.append(bias)
            else:
                kw["bias"] = bias
        if isinstance(scale, View):
            kw["scale"] = scale.ap
            rd.append(scale)
        else:
            kw["scale"] = scale
        if accum_out is not None:
            kw["accum_out"] = accum_out.ap
            wr.append(accum_out)
        return self.op("act", lambda e: e.activation(o, i, func, **kw), rd, wr)

    def tt(self, eng, out, in0, in1, op):
        o, a, b = out.ap, in0.ap, in1.ap
        return self.op(eng, lambda e: e.tensor_tensor(o, a, b, op), [in0, in1], [out])

    def ts(self, eng, out, in0, s1, s2=None, op0=ALU.mult, op1=None):
        o, a = out.ap, in0.ap
        rd = [in0]
        if isinstance(s1, View):
            rd.append(s1)
            s1 = s1.ap
        if isinstance(s2, View):
            rd.append(s2)
            s2 = s2.ap
        kw = {}
        if op1 is not None:
            kw["op1"] = op1
        return self.op(eng, lambda e: e.tensor_scalar(o, a, s1, s2, op0, **kw), rd, [out])

    def stt(self, out, in0, scalar, in1, op0, op1):
        o, a, b = out.ap, in0.ap, in1.ap
        rd = [in0, in1]
        if isinstance(scalar, View):
            rd.append(scalar)
            scalar = scalar.ap
        return self.op("dve", lambda e: e.scalar_tensor_tensor(o, a, scalar, b, op0, op1), rd, [out])

    def copy(self, eng, out, in_):
        o, i = out.ap, in_.ap
        if eng == "act":
            return self.op("act", lambda e: e.copy(o, i), [in_], [out])
        return self.op(eng, lambda e: e.tensor_copy(o, i), [in_], [out])

    def scan(self, out, d0, d1, initial, op0, op1):
        o, a, b = out.ap, d0.ap, d1.ap
        rd = [d0, d1]
        if isinstance(initial, View):
            rd.append(initial)
            initial = initial.ap
        return self.op("dve", lambda e: e.tensor_tensor_scan(o, a, b, initial, op0, op1), rd, [out])

    def recip(self, out, in_):
        o, i = out.ap, in_.ap
        return self.op("dve", lambda e: e.reciprocal(o, i), [in_], [out])

    def memset(self, eng, out, val):
        o = out.ap
        return self.op(eng, lambda e: e.memset(o, val), [], [out])

    def emit(self):
        nc = self.nc
        with ExitStack() as st:
            esem = {}
            dsem = {}
            for key in self.dma_keys:
                dsem[key] = st.enter_context(nc.semaphore("d_%d" % len(dsem)))
            self.max_count = 0
            for e in ENGS:
                for n_, ins in enumerate(self.streams[e]):
                    ins.idx = n_
                    ins.signal = False
            for e in ENGS:
                for ins in self.streams[e]:
                    best = {}
                    keep = []
                    for j in ins.deps:
                        if j.is_dma:
                            keep.append(j)
                            continue
                        kk_ = (j.eng, j.epoch)
                        if kk_ not in best or best[kk_].idx < j.idx:
                            best[kk_] = j
                    for j in best.values():
                        j.signal = True
                        keep.append(j)
                    ins.deps = keep
            for e in ENGS:
                c = {}
                for ins in self.streams[e]:
                    if ins.is_dma:
                        continue
                    if ins.signal:
                        ek = (e, ins.epoch)
                        if ek not in esem:
                            esem[ek] = st.enter_context(nc.semaphore("s_%s_%d" % ek))
                        c[ek] = c.get(ek, 0) + 1
                        ins.count = c[ek]
                        self.max_count = max(self.max_count, ins.count)
            block = st.enter_context(nc.Block())
            engobj = {"pe": block.tensor, "act": block.scalar, "dve": block.vector, "pool": block.gpsimd, "sp": block.sync}

            def mk(ename):
                def body(eng):
                    waited = {}
                    for ins in self.streams[ename]:
                        need = {}
                        for j in ins.deps:
                            if j.is_dma:
                                s, c, wk = dsem[j.sem], j.count, ("d", j.sem)
                            else:
                                s, c, wk = esem[(j.eng, j.epoch)], j.count, ("e", j.eng, j.epoch)
                            if c > need.get(wk, (None, 0))[1]:
                                need[wk] = (s, c)
                        for wk, (s, c) in need.items():
                            if waited.get(wk, 0) >= c:
                                continue
                            waited[wk] = c
                            eng.wait_ge(s, c)
                        if ins.fn is None:
                            continue
                        r = ins.fn(eng)
                        if ins.is_dma:
                            r.then_inc(dsem[ins.sem], 16)
                        elif ins.signal:
                            r.then_inc(esem[(ename, ins.epoch)], 1)
                return body

            for e in ENGS:
                engobj[e](mk(e))


CF_E = 0
CF_SBM = 2048
CF_HGM = 2176
CF_RST = 2240
CF_N = 2752


def host_consts():
    cf = np.zeros((128, CF_N), np.float32)
    k = np.arange(128)[:, None].astype(np.float64)
    q = np.arange(128)[None, :].astype(np.float64)
    for h in range(8):
        slope = 2.0 ** (-(h + 1))
        cur = np.where(q >= k, np.exp(-slope * (q - k)), 0.0)
        prev = np.where(k > q, np.exp(-slope * (q + 128 - k)), 0.0)
        cf[:, CF_E + h * 256:CF_E + h * 256 + 128] = cur
        cf[:, CF_E + h * 256 + 128:CF_E + h * 256 + 256] = prev
    t = np.arange(128)[:, None]
    j = np.arange(128)[None, :]
    cf[:, CF_SBM:CF_SBM + 128] = (j < t)
    s = (np.arange(128) % 64)[:, None]
    tt = np.arange(64)[None, :]
    cf[:, CF_HGM:CF_HGM + 64] = (tt >= s)
    cf[:, CF_RST:CF_RST + 512] = (np.arange(512) % 64 != 0)[None, :]
    cb = np.zeros((128, 384), np.float32)
    cb[:, 0:128] = np.eye(128)
    cb[:, 128:256] = 1.0
    cb[:, 256:384] = np.where(j < t, 0.0, -30000.0)
    return cf, cb.astype(ml_dtypes.bfloat16)


VC_NG = 0
VC_FG = 48
VC_HN = 56
VC_CW = 57
VC_LB = 69
VC_SK = 77
VC_N = 88


def host_vecs(norm_g, final_g, hgrn_norm_g, conv_w, hgrn_lb_logits, attn_sinks):
    v = np.zeros((128, VC_N), np.float32)
    v[:, VC_NG:VC_NG + 48] = norm_g.reshape(6, 8, 128).transpose(2, 0, 1).reshape(128, 48)
    v[:, VC_FG:VC_FG + 8] = final_g.reshape(8, 128).T
    v[:, VC_HN] = hgrn_norm_g.reshape(128)
    v[:, VC_CW:VC_CW + 12] = conv_w.reshape(3, 4, 128).transpose(2, 1, 0).reshape(128, 12)
    v[:, VC_LB:VC_LB + 8] = hgrn_lb_logits.reshape(2, 4, 128).transpose(2, 1, 0).reshape(128, 8)
    v[:, VC_SK:VC_SK + 8] = attn_sinks.reshape(1, 8)
    return v


ALL_STAGES = ("ffn00", "attn", "ffn01", "ffn10", "rec", "ffn11", "final")
W_SHAPES = {"attn_w_in": [D, 2304], "attn_w_out": [D, D], "rec_w_in": [D, 3584], "rec_w_out": [D, D]}
for _l in range(2):
    for _f in range(2):
        W_SHAPES["ffn_w_in_%d%d" % (_l, _f)] = [D, 2 * DFF]
        W_SHAPES["ffn_w_out_%d%d" % (_l, _f)] = [DFF, D]


class Prog:
    def __init__(self, stages=ALL_STAGES):
        self.stages = tuple(stages)
        nc = self.nc = bass.Bass("TRN2", target_bir_lowering=False)
        k = self.k = K(nc)

        def din(name, shape, dt=F32):
            return Tile(nc.dram_tensor(name, list(shape), dt, kind="ExternalInput"), shape, None)

        self.xT = din("xT", [128, 8, S])
        self.cf = din("cf", [128, CF_N])
        self.cb = din("cb", [128, 384], BF16)
        self.vecs = din("vecs", [128, VC_N])
        self.wd = {}
        outh = nc.dram_tensor("outT", [128, 8, S], F32, kind="ExternalOutput")
        self.outT = Tile(outh, [128, 8, S], k._new_tid())

    def w(self, name):
        if name not in self.wd:
            self.wd[name] = self.nc.dram_tensor(name, W_SHAPES[name], F32, kind="ExternalInput")
        return self.wd[name]

    def build(self):
        nc, k = self.nc, self.k
        with ExitStack() as st:
            SB_BYTES = 212480
            ah = st.enter_context(nc.sbuf_tensor("arena", [128, SB_BYTES // 4], F32))
            ph = st.enter_context(nc.psum_tensor("psum", [128, 4096], F32))
            self.A = A = Arena(k, ah, SB_BYTES)
            self.PS = Arena(k, ph, 16384, gran=(2048, 32))
            k.psum_tid = self.PS.tid
            self.hT = A.tile([128, 8, S], F32)
            self.cfT = A.tile([128, CF_N], F32)
            self.cbT = A.tile([128, 384], BF16)
            self.vc = A.tile([128, VC_N], F32)
            self.epsT = A.tile([128, 2], F32)
            self.wring = [(A.tile([128, 8, 256], BF16), A.tile([128, 8, 256], BF16)) for _ in range(3)]
            self.ident = Tile(self.cbT.h[:, 0:128], [128, 128], self.cbT.tid, self.cbT.base, 2)
            self.ones = Tile(self.cbT.h[:, 128:256], [128, 128], self.cbT.tid, self.cbT.base + 256, 2)
            self.mbT = Tile(self.cbT.h[:, 256:384], [128, 128], self.cbT.tid, self.cbT.base + 512, 2)
            self.wslab_issued = 0

            for t in range(NT):
                k.dma("sp", self.hT[:, :, t * 512:(t + 1) * 512], self.xT[:, :, t * 512:(t + 1) * 512], ("x", t))
            k.dma("sp", self.cfT[:], self.cf[:], "const_cf")
            k.dma("sp", self.cbT[:], self.cb[:], "const_cb")
            k.dma("sp", self.vc[:], self.vecs[:], "const_vc")
            k.memset("pool", self.epsT[:, 0:1], EPS)
            k.memset("pool", self.epsT[:, 1:2], 1.0)

            self.body()

            k.barrier()
            k.emit()
        return nc

    def psb(self, b, shape=(128, 512), dtype=F32):
        return self.PS.tile(list(shape), dtype, off=b * 2048)

    def body(self):
        st = self.stages
        self.ffn_seq = [x for x in st if x.startswith("ffn")]
        for i, name in enumerate(st):
            self.k.new_epoch()
            if name.startswith("ffn"):
                fidx = self.ffn_seq.index(name)
                nxt_is_ffn = (i + 1 < len(st) and st[i + 1].startswith("ffn"))
                self.ffn_issue(limit=fidx * 11 + 3)
                self.ffn(int(name[3]), int(name[4]), fidx, next_limit=fidx * 11 + (14 if nxt_is_ffn else 11))
            elif name == "attn":
                self.attn()
            elif name == "rec":
                self.rec()
            elif name == "final":
                self.final_norm()
        if "final" not in st:
            self.dump_h()

    def dump_h(self):
        k = self.k
        for t in range(NT):
            k.dma("sp", self.outT[:, :, t * 512:(t + 1) * 512], self.hT[:, :, t * 512:(t + 1) * 512], ("out", t))

    def rms_norm_T(self, gcol0, xnT):
        k, A = self.k, self.A
        m = A.mark()
        sq = A.tile([128, 8, 512], BF16)
        rs = [A.tile([128, 512], F32) for _ in range(2)]
        ssq = self.psb(6)
        for t in range(NT):
            ts_ = slice(t * 512, (t + 1) * 512)
            k.act(sq[:], self.hT[:, :, ts_], AF.Square)
            for kt in range(KT):
                k.mm(ssq[:], self.ones[:], sq[:, kt, :], start=(kt == 0), stop=(kt == KT - 1))
            r = rs[t % 2]
            k.act(r[:], ssq[:], AF.Sqrt, bias=self.epsT[:, 0:1], scale=1.0 / D)
            k.recip(r[:], r[:])
            for kt in range(KT):
                k.stt(xnT[:, kt, ts_], self.hT[:, kt, ts_], self.vc[:, gcol0 + kt:gcol0 + kt + 1], r[:], ALU.mult, ALU.mult)
        A.release(m)

    def ffn_issue(self, limit):
        k = self.k
        while self.wslab_issued < limit and self.wslab_issued < 11 * len(self.ffn_seq):
            gi = self.wslab_issued
            fi, s = divmod(gi, 11)
            slot = self.wring[gi % 3]
            w = self.w("ffn_w_in_" + self.ffn_seq[fi][3:5]).ap().rearrange("(kt p) c -> p kt c", p=128)
            k.dma("pool", slot[0][:], View(w[:, :, s * 256:(s + 1) * 256], None, []), ("wr", gi % 3, 0))
            k.dma("pool", slot[1][:], View(w[:, :, DFF + s * 256:DFF + (s + 1) * 256], None, []), ("wr", gi % 3, 1))
            self.wslab_issued += 1

    def ffn(self, l, f, fi, next_limit):
        k, A = self.k, self.A
        m = A.mark()
        xnT = A.tile([128, 8, S], BF16)
        parts = [6, 6, 6, 4]
        actT = A.tile([128, 6, S], BF16)
        wout = A.tile([128, 6, D], BF16)
        sg = [A.tile([128, 512], F32) for _ in range(3)]
        self.rms_norm_T(VC_NG + (l * 3 + (0 if f == 0 else 2)) * 8, xnT)
        wo_dram = self.w("ffn_w_out_%d%d" % (l, f)).ap().rearrange("(j p) c -> p j c", p=128)
        pg = [self.psb(0), self.psb(1), self.psb(2)]
        pu = [self.psb(3), self.psb(4), self.psb(5)]
        po = [self.psb(0), self.psb(1), self.psb(2), self.psb(3)]
        j0 = 0
        cnt = 0
        cnt2 = 0
        for nj in parts:
            k.dma("pool", wout[:, 0:nj, :], View(wo_dram[:, j0:j0 + nj, :], None, []), "wout")
            for sl in range(nj // 2):
                s = j0 // 2 + sl
                gi = fi * 11 + s
                slot = self.wring[gi % 3]
                for jj in range(2):
                    jl = sl * 2 + jj
                    for t in range(NT):
                        ts_ = slice(t * 512, (t + 1) * 512)
                        g_ps, u_ps, sgt = pg[cnt % 3], pu[cnt % 3], sg[cnt % 3]
                        cnt += 1
                        for kt in range(KT):
                            k.mm(g_ps[:], slot[0][:, kt, jj * 128:(jj + 1) * 128], xnT[:, kt, ts_], start=(kt == 0), stop=(kt == KT - 1))
                        for kt in range(KT):
                            k.mm(u_ps[:], slot[1][:, kt, jj * 128:(jj + 1) * 128], xnT[:, kt, ts_], start=(kt == 0), stop=(kt == KT - 1))
                        k.act(sgt[:], g_ps[:], AF.Silu)
                        k.tt("dve", actT[:, jl, ts_], sgt[:], u_ps[:], ALU.mult)
                self.ffn_issue(min(gi + 4, next_limit))
            for mo in range(KT):
                for t in range(NT):
                    ts_ = slice(t * 512, (t + 1) * 512)
                    o_ps = po[cnt2 % 4]
                    cnt2 += 1
                    for jl in range(nj):
                        k.mm(o_ps[:], wout[:, jl, mo * 128:(mo + 1) * 128], actT[:, jl, ts_], start=(jl == 0), stop=(jl == nj - 1))
                    k.stt(self.hT[:, mo, ts_], o_ps[:], 0.5, self.hT[:, mo, ts_], ALU.mult, ALU.add)
            j0 += nj
        A.release(m)

    def final_norm(self):
        k, A = self.k, self.A
        m = A.mark()
        sq = A.tile([128, 8, 512], BF16)
        rs = [A.tile([128, 512], F32) for _ in range(2)]
        yo = [A.tile([128, 8, 512], F32) for _ in range(2)]
        ssq = self.psb(6)
        for t in range(NT):
            ts_ = slice(t * 512, (t + 1) * 512)
            k.act(sq[:], self.hT[:, :, ts_], AF.Square)
            for kt in range(KT):
                k.mm(ssq[:], self.ones[:], sq[:, kt, :], start=(kt == 0), stop=(kt == KT - 1))
            r = rs[t % 2]
            k.act(r[:], ssq[:], AF.Sqrt, bias=self.epsT[:, 0:1], scale=1.0 / D)
            k.recip(r[:], r[:])
            y = yo[t % 2]
            for kt in range(KT):
                k.stt(y[:, kt, :], self.hT[:, kt, ts_], self.vc[:, VC_FG + kt:VC_FG + kt + 1], r[:], ALU.mult, ALU.mult)
            k.dma("sp", self.outT[:, :, ts_], y[:], ("out", t % 2))
        A.release(m)

    def sub(self, t, c0, c1, shape):
        ap = t.h[:, c0:c1]
        if len(shape) == 3:
            ap = ap.rearrange("p (a b) -> p a b", a=shape[1], b=shape[2])
        return Tile(ap, shape, t.tid, base=t.base + c0 * t.isz, isz=t.isz)

    def proj_fm(self, wt, c0, xnT, dest, ci, scale=None):
        k = self.k
        for t in range(NT):
            ts_ = slice(t * 512, (t + 1) * 512)
            ps = self.psb(self.pcnt % 4)
            self.pcnt += 1
            for kt in range(KT):
                k.mm(ps[:], wt[:, kt, c0:c0 + 128], xnT[:, kt, ts_], start=(kt == 0), stop=(kt == KT - 1))
            if self.pcnt % 2 == 0:
                k.act(dest[:, ci, ts_], ps[:], AF.Copy, scale=(1.0 if scale is None else scale))
            else:
                k.ts("dve", dest[:, ci, ts_], ps[:], (1.0 if scale is None else scale), None, op0=ALU.mult)

    def proj_tm(self, wt, c0, n, xnT, dest, d0):
        k = self.k
        for blk in range(NB):
            ps = self.PS.tile([128, n], F32, off=(self.pcnt % 4) * 2048)
            self.pcnt += 1
            for kt in range(KT):
                k.mm(ps[:], xnT[:, kt, blk * 128:(blk + 1) * 128], wt[:, kt, c0:c0 + n], start=(kt == 0), stop=(kt == KT - 1))
            if self.pcnt % 2 == 0:
                k.copy("act", dest[:, blk, d0:d0 + n], ps[:])
            else:
                k.copy("dve", dest[:, blk, d0:d0 + n], ps[:])

    def wslab(self, wname, ncols_total, c0, hs, keyid):
        w = self.w(wname).ap().rearrange("(kt p) c -> p kt c", p=128)
        self.k.dma("pool", hs[:], View(w[:, :, c0:c0 + 256], None, []), ("wr", keyid // 2, keyid % 2))

    def attn(self):
        k, A = self.k, self.A
        base = A.mark()
        self.pcnt = 0
        hs = [self.wring[i // 2][i % 2] for i in range(6)]
        W0 = self.wring[0][0].base
        xnT = A.tile([128, 8, S], BF16, off=base)
        oTa = A.tile([128, 4, S], BF16, off=base + 32768)
        oTb = A.tile([128, 4, S], BF16, off=base + 0)
        R1 = base + 49152
        A.top = R1 + 53312
        A.peak = max(A.peak, A.top)
        assert A.top <= A.nbytes, A.top
        qaT = A.tile([128, 4, S], BF16, off=R1)
        kkT = A.tile([128, 2, S], BF16, off=R1 + 16384)
        va = A.tile([128, NB, 128], BF16, off=R1 + 24576)
        PT = [A.tile([128, 8, 256], BF16, off=R1 + 28672 + i * 4096) for i in range(3)]
        eS = [A.tile([128, 4, 256], F32, off=R1 + 40960 + i * 4096) for i in range(2)]
        den = [A.tile([128, 4, 128], F32, off=R1 + 49152 + i * 2048) for i in range(2)]
        es = A.tile([128, 8], F32, off=R1 + 53248)
        esk2 = A.tile([128, 4], F32, off=R1 + 53280)
        qbT = A.tile([128, 4, S], BF16, off=R1)
        kbT = A.tile([128, 4, S], BF16, off=R1 + 16384)
        vb = A.tile([128, NB, 512], BF16, off=R1 + 32768)
        Etab = self.sub(self.cfT, CF_E, CF_E + 2048, [128, 8, 256])
        sbm = self.sub(self.cfT, CF_SBM, CF_SBM + 128, [128, 128])

        top_save = A.top
        A.top = R1 + 28672
        self.rms_norm_T(VC_NG + 1 * 8, xnT)
        A.top = top_save

        wname = "attn_w_in"
        wv = self.w(wname).ap().rearrange("(kt p) c -> p kt c", p=128)
        self.wslab(wname, 2304, 0, hs[0], 0)
        self.wslab(wname, 2304, 256, hs[1], 1)
        self.wslab(wname, 2304, 512, hs[2], 2)
        for i, (src, dst) in enumerate([(512, 0), (512, 64), (576, 128), (576, 192)]):
            k.dma("pool", hs[3][:, :, dst:dst + 64], View(wv[:, :, src:src + 64], None, []), ("kk", i))
        for ci in range(4):
            self.proj_fm(hs[ci // 2], (ci % 2) * 128, xnT, qaT, ci, scale=0.125)
        self.wslab(wname, 2304, 768, hs[4], 4)
        self.wslab(wname, 2304, 1024, hs[5], 5)
        self.proj_fm(hs[3], 0, xnT, kkT, 0)
        self.proj_fm(hs[3], 128, xnT, kkT, 1)
        self.proj_tm(hs[2], 128, 128, xnT, va, 0)
        self.wslab(wname, 2304, 1280, hs[0], 0)
        self.wslab(wname, 2304, 1536, hs[1], 1)
        self.wslab(wname, 2304, 1792, hs[2], 2)
        self.wslab(wname, 2304, 2048, hs[3], 3)

        cut = int(os.environ.get('ATTN_CUT', '99'))
        if cut <= 1:
            A.release(base)
            return
        k.act(es[:], self.vc[:, VC_SK:VC_SK + 8], AF.Exp)
        k.copy("dve", esk2[0:64, :], es[0:64, 0:8:2])
        k.copy("dve", esk2[64:128, :], es[64:128, 1:8:2])
        esb = View(esk2.h[:, :].unsqueeze(2).to_broadcast([128, 4, 128]), esk2.tid, esk2[:].rects)
        def swa_scores(kb):
            nq = 256 if kb < NB - 1 else 128
            pt = PT[kb % 3]
            for hg in range(2):
                Sps = self.PS.tile([128, 4, 256], F32, off=(hg * 2) * 2048)
                for i in range(4):
                    h = 4 * hg + i
                    c, po = h // 2, (h % 2) * 64
                    j = (i % 2) * 2 + (i // 2)
                    k.mm(Sps[:, j, 0:nq], kkT[po:po + 64, hg, kb * 128:(kb + 1) * 128],
                         qaT[po:po + 64, c, kb * 128:kb * 128 + nq])
                e = eS[hg]
                k.act(e[:, :, 0:nq], Sps[:, :, 0:nq], AF.Exp)
                for p in range(2):
                    hsel = slice(4 * hg + p, 4 * hg + p + 3, 2)
                    k.tt("dve", pt[:, hsel, 0:nq], e[:, 2 * p:2 * p + 2, 0:nq], Etab[:, hsel, 0:nq], ALU.mult)

        def swa_out(kb):
            pt = PT[kb % 3]
            O = self.PS.tile([128, 4, 128], F32, off=(4 + 2 * (kb % 2)) * 2048)
            Dn = self.PS.tile([128, 4, 128], F32, off=(5 + 2 * (kb % 2)) * 2048)
            ptp = PT[(kb - 1) % 3]
            for h in range(8):
                g, c, po = h // 4, h // 2, (h % 2) * 64
                k.mm(O[po:po + 64, c, :], va[:, kb, g * 64:(g + 1) * 64], pt[:, h, 0:128], start=True, stop=(kb == 0), tile_position=(0, po))
                if kb > 0:
                    k.mm(O[po:po + 64, c, :], va[:, kb - 1, g * 64:(g + 1) * 64], ptp[:, h, 128:256], start=False, stop=True, tile_position=(0, po))
                k.mm(Dn[po:po + 64, c, :], self.ones[:, 0:64], pt[:, h, 0:128], start=True, stop=(kb == 0), tile_position=(0, po))
                if kb > 0:
                    k.mm(Dn[po:po + 64, c, :], self.ones[:, 0:64], ptp[:, h, 128:256], start=False, stop=True, tile_position=(0, po))
            dn = den[kb % 2]
            k.tt("dve", dn[:], Dn[:], esb, ALU.add)
            k.recip(dn[:], dn[:])
            k.tt("dve", oTa[:, :, kb * 128:(kb + 1) * 128], O[:], dn[:], ALU.mult)

        swa_scores(0)
        for kb in range(NB):
            if kb + 1 < NB:
                swa_scores(kb + 1)
            swa_out(kb)

        if cut <= 2:
            A.release(base)
            return
        self.proj_fm(hs[4], 0, xnT, qbT, 0, scale=0.125)
        self.proj_fm(hs[4], 128, xnT, qbT, 1, scale=0.125)
        self.proj_fm(hs[5], 0, xnT, qbT, 2, scale=0.125)
        self.proj_fm(hs[5], 128, xnT, qbT, 3, scale=0.125)
        self.proj_fm(hs[0], 0, xnT, kbT, 0)
        self.proj_fm(hs[0], 128, xnT, kbT, 1)
        self.proj_fm(hs[1], 0, xnT, kbT, 2)
        self.proj_fm(hs[1], 128, xnT, kbT, 3)
        self.proj_tm(hs[2], 0, 256, xnT, vb, 0)
        self.proj_tm(hs[3], 0, 256, xnT, vb, 256)

        if cut <= 3:
            A.release(base)
            return
        k.new_epoch()
        X1 = base + 16384

        class Bf:
            pass
        bA, bB = Bf(), Bf()
        bB.e32 = A.tile([128, 2048], F32, off=W0)
        bB.spP = A.tile([128, 2049], F32, off=W0 + 8192)
        bB.xw = A.tile([128, 2048], BF16, off=W0 + 16416)
        bA.xw = A.tile([128, 1024], BF16, off=W0 + 20512)
        bA.npt = A.tile([128, 1], F32, off=W0 + 22560)
        bB.npt = A.tile([128, 1], F32, off=W0 + 22592)
        bB.WT = A.tile([128, 16, 128], BF16, off=X1)
        bA.e32 = A.tile([128, 1024], F32, off=X1 + 4096)
        bA.spP = A.tile([128, 1025], F32, off=X1 + 8192)
        bA.WT = A.tile([128, 8, 128], BF16, off=X1 + 12320)
        k.memset("pool", bA.spP[:, 0:1], 0.0)
        k.memset("pool", bB.spP[:, 0:1], 0.0)
        Zt = self.PS.tile([128, 2048], F32, off=0)
        TPb = [self.PS.tile([128, 8, 128], BF16, off=(4 + i) * 2048) for i in range(2)]
        Ob = [self.PS.tile([128, 128], F32, off=(6 + i) * 2048) for i in range(2)]
        order = []
        for i in range(8):
            order += [i, 15 - i]
        blocks = [(h, qb) for h in range(8) for qb in order]
        nblk = len(blocks)
        st = {"o": 0}

        def geo(i):
            h, qb = blocks[i]
            return h, qb, h // 2, (h % 2) * 64, (bA if qb < 8 else bB), (qb + 1) * 128, slice(qb * 128, (qb + 1) * 128)

        def s_qk(i):
            h, qb, c, po, B, Lk, qs = geo(i)
            nkc = (Lk + 511) // 512
            for kc in range(nkc):
                w = min(512, Lk - kc * 512)
                k.mm(Zt[:, kc * 512:kc * 512 + w], qbT[po:po + 64, c, qs], kbT[po:po + 64, c, kc * 512:kc * 512 + w],
                     start=True, stop=(kc < nkc - 1))
            k.mm(Zt[:, qb * 128:Lk], self.ident[:], self.mbT[:], start=False, stop=True)

        def s_exp(i):
            h, qb, c, po, B, Lk, qs = geo(i)
            k.act(B.e32[:, 0:Lk], Zt[:, 0:Lk], AF.Exp)

        def s_ln(i):
            h, qb, c, po, B, Lk, qs = geo(i)
            k.act(B.spP[:, 1:Lk + 1], B.e32[:, 0:Lk], AF.Ln, bias=self.epsT[:, 1:2])

        def s_scan(i):
            h, qb, c, po, B, Lk, qs = geo(i)
            k.scan(B.spP[:, 1:Lk + 1], B.spP[:, 1:Lk + 1], B.spP[:, 1:Lk + 1], 0.0, ALU.add, ALU.max)
            k.ts("dve", B.npt[:], B.spP[:, Lk:Lk + 1], -1.0, None, op0=ALU.mult)

        def s_exp2(i):
            h, qb, c, po, B, Lk, qs = geo(i)
            k.act(B.xw[:, 0:Lk], B.spP[:, 0:Lk], AF.Exp, bias=B.npt[:])

        def s_mult(i):
            h, qb, c, po, B, Lk, qs = geo(i)
            k.tt("dve", B.xw[:, 0:Lk], B.e32[:, 0:Lk], B.xw[:, 0:Lk], ALU.mult)

        def s_tr(i):
            h, qb, c, po, B, Lk, qs = geo(i)
            for g, sb0 in enumerate(range(0, qb + 1, 8)):
                n = min(8, qb + 1 - sb0)
                for ii in range(n):
                    k.transpose(TPb[g][:, ii, :], B.xw[:, (sb0 + ii) * 128:(sb0 + ii + 1) * 128], self.ident[:])

        def s_evac(i, eng_sel):
            h, qb, c, po, B, Lk, qs = geo(i)
            for g, sb0 in enumerate(range(0, qb + 1, 8)):
                n = min(8, qb + 1 - sb0)
                eng = "dve" if g == 0 else "act"
                if eng == eng_sel:
                    k.copy(eng, B.WT[:, sb0:sb0 + n, :], TPb[g][:, 0:n, :])

        def s_pv(i):
            h, qb, c, po, B, Lk, qs = geo(i)
            o = Ob[st["o"] % 2]
            st["o"] += 1
            for sb in range(qb + 1):
                k.mm(o[po:po + 64, :], vb[:, sb, c * 128 + po:c * 128 + po + 64], B.WT[:, sb, :],
                     start=(sb == 0), stop=(sb == qb), tile_position=(0, po))
            k.copy("act", oTb[po:po + 64, c, qs], o[po:po + 64, :])

        ok = lambda j: 0 <= j < nblk
        s_qk(0)
        for i in range(-1, nblk + 1):
            if ok(i + 1):
                s_exp(i + 1)
            if ok(i + 2):
                s_qk(i + 2)
            if ok(i):
                s_scan(i)
            if ok(i - 1):
                s_evac(i - 1, "act")
            if ok(i + 1):
                s_ln(i + 1)
            if ok(i):
                s_exp2(i)
            if ok(i - 1):
                s_evac(i - 1, "dve")
                s_pv(i - 1)
            if ok(i):
                s_mult(i)
                s_tr(i)

        if cut <= 4:
            A.release(base)
            return
        self.out_proj("attn_w_out", [(oTa, c) for c in range(4)] + [(oTb, c) for c in range(4)])
        A.release(base)

    def out_proj(self, wname, srcs):
        k = self.k
        hs = [self.wring[i // 2][i % 2] for i in range(6)]
        wo = self.w(wname).ap().rearrange("(c p) m -> p c m", p=128)
        for i in range(4):
            k.dma("pool", hs[i][:], View(wo[:, :, i * 256:(i + 1) * 256], None, []), ("wr", i // 2, i % 2))
        for mo in range(KT):
            hsl, c0 = hs[mo // 2], (mo % 2) * 128
            for t in range(NT):
                ts_ = slice(t * 512, (t + 1) * 512)
                ps = self.psb(self.pcnt % 4)
                self.pcnt += 1
                for c in range(8):
                    src, ci = srcs[c]
                    k.mm(ps[:], hsl[:, c, c0:c0 + 128], src[:, ci, ts_], start=(c == 0), stop=(c == 7))
                k.tt("dve", self.hT[:, mo, ts_], ps[:], self.hT[:, mo, ts_], ALU.add)

    def rec(self):
        k, A = self.k, self.A
        base = A.mark()
        self.pcnt = 0
        xnT = A.tile([128, 8, S], BF16)
        oT = A.tile([128, 8, S], BF16)
        self.rms_norm_T(VC_NG + (3 + 1) * 8, xnT)
        W0 = self.wring[0][0].base
        slots = [A.tile([128, 8, 512], BF16, off=W0 + i * 8192) for i in range(3)]
        slots.append(A.tile([128, 8, 512], BF16))
        wv = self.w("rec_w_in").ap().rearrange("(kt p) c -> p kt c", p=128)

        def load(slot_i, j, c0):
            k.dma("pool", slots[slot_i][:, :, j * 128:(j + 1) * 128], View(wv[:, :, c0:c0 + 128], None, []), ("wq", slot_i, j))

        sm = A.tile([128, 48], F32)
        L0 = self.vc[:, VC_LB:VC_LB + 8:2]
        L1 = self.vc[:, VC_LB + 1:VC_LB + 8:2]
        mx, d0, d1, ssum, lb, oml = [sm[:, i * 4:(i + 1) * 4] for i in range(6)]
        k.tt("dve", mx, L0, L1, ALU.max)
        k.tt("dve", d0, L0, mx, ALU.subtract)
        k.tt("dve", d1, L1, mx, ALU.subtract)
        k.act(d0, d0, AF.Exp)
        k.act(d1, d1, AF.Exp)
        k.tt("dve", ssum, d0, d1, ALU.add)
        k.recip(ssum, ssum)
        k.tt("dve", d0, d0, ssum, ALU.mult)
        k.tt("dve", d1, d1, ssum, ALU.mult)
        k.tt("dve", d1, d0, d1, ALU.add)
        k.tt("dve", lb, d1, d0, ALU.subtract)
        k.ts("dve", oml, lb, -1.0, 1.0, op0=ALU.mult, op1=ALU.add)

        mconv = A.mark()
        xbuf = A.tile([128, S + 2], F32)
        gbs = A.tile([128, S], F32)
        ybuf = A.tile([128, S], F32)
        usb = [A.tile([128, 512], F32) for _ in range(2)]
        k.memset("pool", xbuf[:, 0:2], 0.0)
        for c in range(3):
            for j in range(3):
                load(c, j, 2048 + j * 512 + c * 128)
        bcnt = 0
        for c in range(4):
            sl = slots[c % 4 if c < 3 else 3]
            if c == 3:
                for j in range(3):
                    load(3, j, 2048 + j * 512 + c * 128)
            for t in range(NT):
                ts_ = slice(t * 512, (t + 1) * 512)
                pss = []
                for j in range(3):
                    ps = self.psb(bcnt % 8)
                    bcnt += 1
                    for kt in range(KT):
                        k.mm(ps[:], sl[:, kt, j * 128:(j + 1) * 128], xnT[:, kt, ts_], start=(kt == 0), stop=(kt == KT - 1))
                    pss.append(ps)
                u = usb[t % 2]
                k.copy("act", u[:], pss[2][:])
                k.tt("dve", xbuf[:, 2 + t * 512:2 + (t + 1) * 512], pss[1][:], u[:], ALU.mult)
                k.copy("act", gbs[:, ts_], pss[0][:])
            cw = VC_CW + c * 3
            k.ts("dve", ybuf[:], xbuf[:, 2:S + 2], self.vc[:, cw + 2:cw + 3], None, op0=ALU.mult)
            k.stt(ybuf[:], xbuf[:, 1:S + 1], self.vc[:, cw + 1:cw + 2], ybuf[:], ALU.mult, ALU.add)
            k.stt(ybuf[:], xbuf[:, 0:S], self.vc[:, cw:cw + 1], ybuf[:], ALU.mult, ALU.add)
            k.tt("dve", oT[:, 4 + c, :], gbs[:], ybuf[:], ALU.mult)
        A.release(mconv)

        for h in range(4):
            for j in range(4):
                load(h, j, j * 512 + h * 128)
        f32t = lambda: A.tile([128, 512], F32)
        T_sq, T_b1, T_b2, T_b3 = f32t(), f32t(), f32t(), f32t()
        rs2 = T_sq
        T_kh = A.tile([128, 512], BF16)
        T_sq2 = T_kh

        class Hd:
            pass
        hd = []
        for h in range(4):
            H = Hd()
            H.qt = A.tile([128, 512], BF16)
            H.kt = A.tile([128, 512], BF16)
            H.khT = A.tile([128, 4, 128], BF16)
            H.vt = A.tile([128, 4, 128], BF16)
            H.AT = A.tile([128, 4, 64], BF16)
            H.sgl = A.tile([128, 512], BF16)
            H.eGl = A.tile([128, 8], F32)
            H.sf = A.tile([128, 128], F32)
            H.sb = [A.tile([128, 128], BF16) for _ in range(2)]
            k.memset("pool", H.sf[:], 0.0)
            hd.append(H)
        rst = self.sub(self.cfT, CF_RST, CF_RST + 512, [128, 512])
        hgm = self.sub(self.cfT, CF_HGM, CF_HGM + 64, [128, 64])
        hgmb = View(hgm.h[:, :].unsqueeze(1).to_broadcast([128, 4, 64]), hgm.tid, hgm[:].rects)
        ops = [self.psb(h) for h in range(4)]
        ups = [self.PS.tile([128, 128], F32, off=(4 + i) * 2048) for i in range(2)]
        pw = 0
        ucnt = 0

        def pwbank(shape=(128, 512), dtype=F32):
            nonlocal pw
            pw += 1
            return self.PS.tile(list(shape), dtype, off=(6 + pw % 2) * 2048)

        for t in range(NT):
            ts_ = slice(t * 512, (t + 1) * 512)
            for h in range(4):
                H, sl = hd[h], slots[h]

                def proj(j):
                    ps = pwbank()
                    for kt in range(KT):
                        k.mm(ps[:], sl[:, kt, j * 128:(j + 1) * 128], xnT[:, kt, ts_], start=(kt == 0), stop=(kt == KT - 1))
                    return ps
                psf = proj(1)
                k.act(T_b1[:], psf[:], AF.Exp, scale=-1.0)
                k.ts("dve", T_b1[:], T_b1[:], 1.0, None, op0=ALU.add)
                k.recip(T_b1[:], T_b1[:])
                k.ts("dve", T_b1[:], T_b1[:], sm[:, 20 + h:21 + h], sm[:, 16 + h:17 + h], op0=ALU.mult, op1=ALU.add)
                k.ts("dve", T_b2[:], T_b1[:], -1.0, 1.0, op0=ALU.mult, op1=ALU.add)
                k.act(T_b1[:], T_b1[:], AF.Ln)
                k.scan(T_b3[:], rst[:], T_b1[:], 0.0, ALU.mult, ALU.add)
                k.act(T_b1[:], T_b3[:], AF.Exp)
                k.act(T_b3[:], T_b3[:], AF.Exp, scale=-1.0)
                psq = proj(0)
                k.act(T_sq[:], psq[:], AF.Silu)
                psg = proj(3)
                k.act(H.sgl[:], psg[:], AF.Silu)
                k.copy("dve", H.eGl[:], T_b1[:, 63:512:64])
                k.tt("dve", H.qt[:], T_sq[:], T_b1[:], ALU.mult)
                k.tt("dve", T_b2[:], T_b2[:], T_b3[:], ALU.mult)
                k.copy("dve", H.kt[:], T_b2[:])
                eb = View(H.eGl.h[:, :].unsqueeze(2).to_broadcast([128, 8, 64]), H.eGl.tid, H.eGl[:].rects)
                khv = View(T_kh.h[:, :].rearrange("p (a b) -> p a b", a=8, b=64), T_kh.tid, T_kh[:].rects)
                b2v = View(T_b2.h[:, :].rearrange("p (a b) -> p a b", a=8, b=64), T_b2.tid, T_b2[:].rects)
                k.tt("dve", khv, b2v, eb, ALU.mult)
                for blk in range(4):
                    ps = pwbank((128, 128))
                    tok = slice(t * 512 + blk * 128, t * 512 + (blk + 1) * 128)
                    for kt in range(KT):
                        k.mm(ps[:], xnT[:, kt, tok], sl[:, kt, 256:384], start=(kt == 0), stop=(kt == KT - 1))
                    k.copy("act", H.vt[:, blk, :], ps[:])
                tp = pwbank((128, 4, 128), BF16)
                for blk in range(4):
                    k.transpose(tp[:, blk, :], T_kh[:, blk * 128:(blk + 1) * 128], self.ident[:])
                k.copy("dve", H.khT[:], tp[:])
                ap_ = pwbank((128, 4, 64))
                for blk in range(4):
                    for half in range(2):
                        t0 = (2 * blk + half) * 64
                        k.mm(ap_[half * 64:(half + 1) * 64, blk, :], H.kt[:, t0:t0 + 64], H.qt[:, t0:t0 + 64], tile_position=(0, half * 64))
                k.tt("dve", H.AT[:], ap_[:], hgmb, ALU.mult)
            for cc in range(8):
                blk, half = cc // 2, cc % 2
                p0 = half * 64
                t0 = cc * 64
                gc = t * 8 + cc
                for h in range(4):
                    H = hd[h]
                    k.mm(ops[h][:, t0:t0 + 64], H.vt[p0:p0 + 64, blk, :], H.AT[p0:p0 + 64, blk, :], start=True, stop=(gc == 0))
                    if gc > 0:
                        k.mm(ops[h][:, t0:t0 + 64], H.sb[gc % 2][:], H.qt[:, t0:t0 + 64], start=False, stop=True)
                    if gc < 31:
                        u = ups[ucnt % 2]
                        ucnt += 1
                        k.mm(u[:], H.khT[p0:p0 + 64, blk, :], H.vt[p0:p0 + 64, blk, :])
                        k.stt(H.sf[:], H.sf[:], H.eGl[:, cc:cc + 1], u[:], ALU.mult, ALU.add)
                        k.copy("act", H.sb[(gc + 1) % 2][:], H.sf[:])
            for h in range(4):
                H = hd[h]
                k.act(T_sq2[:], ops[h][:], AF.Square)
                ssq = pwbank()
                k.mm(ssq[:], self.ones[:], T_sq2[:])
                k.act(rs2[:], ssq[:], AF.Ln, bias=self.epsT[:, 0:1], scale=1.0 / 128)
                k.act(rs2[:], rs2[:], AF.Exp, scale=-0.5)
                k.stt(rs2[:], ops[h][:], self.vc[:, VC_HN:VC_HN + 1], rs2[:], ALU.mult, ALU.mult)
                k.tt("dve", oT[:, h, ts_], rs2[:], H.sgl[:], ALU.mult)

        self.out_proj("rec_w_out", [(oT, c) for c in range(8)])
        A.release(base)


_CONSTS = None


def kernel(x, norm_g, ffn_w_in, ffn_w_out, attn_w_in, attn_sinks, attn_w_out, rec_w_in,
           hgrn_lb_logits, hgrn_norm_g, conv_w, rec_w_out, final_g, _stages=ALL_STAGES, _xT=None):
    global _CONSTS
    if _CONSTS is None:
        _CONSTS = host_consts()
    cf, cb = _CONSTS
    f32 = lambda a: np.ascontiguousarray(np.asarray(a, dtype=np.float32))
    x = f32(x)
    vecs = host_vecs(f32(norm_g), f32(final_g), f32(hgrn_norm_g), f32(conv_w), f32(hgrn_lb_logits), f32(attn_sinks))
    prog = Prog(stages=_stages)
    nc = prog.build()
    allw = {"attn_w_in": lambda: f32(attn_w_in)[0], "attn_w_out": lambda: f32(attn_w_out)[0],
            "rec_w_in": lambda: f32(rec_w_in)[0], "rec_w_out": lambda: f32(rec_w_out)[0]}
    for l in range(2):
        for f in range(2):
            allw["ffn_w_in_%d%d" % (l, f)] = (lambda l=l, f=f: f32(np.asarray(ffn_w_in)[l, f]))
            allw["ffn_w_out_%d%d" % (l, f)] = (lambda l=l, f=f: f32(np.asarray(ffn_w_out)[l, f]))
    shared = {"cf": cf, "cb": cb, "vecs": vecs}
    for name in prog.wd:
        shared[name] = allw[name]()
    in_maps = []
    for b in range(8):
        src = x[b] if _xT is None else f32(_xT[b])
        xT = np.ascontiguousarray(src.T.reshape(8, 128, S).transpose(1, 0, 2))
        in_maps.append({"xT": xT, **shared})
    res = run_bass_kernel_spmd(nc, in_maps, core_ids=list(range(8)))
    out = np.empty((8, S, D), np.float32)
    for b in range(8):
        oT = np.asarray(res.results[b]["outT"])
        out[b] = oT.transpose(2, 1, 0).reshape(S, D)
    return out
```

```python
import os
import numpy as np
import ml_dtypes
import concourse.bass as bass
import concourse.mybir as mybir
from concourse.bass_utils import run_bass_kernel_spmd
from contextlib import ExitStack

F32 = mybir.dt.float32
BF16 = mybir.dt.bfloat16
ALU = mybir.AluOpType
AF = mybir.ActivationFunctionType

S = 2048
D = 1024
KT = 8
NT = 4
NB = 16
DFF = 2816
NJ = 22
EPS = 1e-6


class View:
    __slots__ = ("ap", "tid", "rects")

    def __init__(self, ap, tid, rects):
        self.ap = ap
        self.tid = tid
        self.rects = rects

    def with_ap(self, ap):
        return View(ap, self.tid, self.rects)


class Tile:
    def __init__(self, handle, shape, tid, base=0, isz=4, p0=0, gran=None):
        self.h = handle
        self.shape = tuple(shape)
        self.tid = tid
        self.base = base
        self.isz = isz
        self.p0 = p0
        self.gran = gran

    def __getitem__(self, idx):
        if not isinstance(idx, tuple):
            idx = (idx,)
        idx = idx + (slice(None),) * (len(self.shape) - len(idx))
        ap = self.h[idx]
        if self.tid is None:
            return View(ap, None, [])
        rng = []
        for ix, n in zip(idx, self.shape):
            if isinstance(ix, int):
                rng.append((ix, ix + 1))
            else:
                s, e, st = ix.indices(n)
                if st != 1:
                    e = s + ((e - s + st - 1) // st - 1) * st + 1
                rng.append((s, e))
        ivs = [(0, 1)]
        size = 1
        for (s, e), n in zip(reversed(rng[1:]), reversed(self.shape[1:])):
            if len(ivs) == 1 and ivs[0] == (0, size):
                ivs = [(s * size, e * size)]
            else:
                ivs = [(a * size + lo, a * size + hi) for a in range(s, e) for (lo, hi) in ivs]
            size *= n
        b, z = self.base, self.isz
        rects = [(self.p0 + rng[0][0], self.p0 + rng[0][1], b + lo * z, b + hi * z) for lo, hi in ivs]
        if self.gran is not None:
            gb, gp = self.gran
            rr = set()
            for (pa, pb, lo, hi) in rects:
                rr.add((pa // gp * gp, (pb + gp - 1) // gp * gp, lo // gb * gb, (hi + gb - 1) // gb * gb))
            rects = sorted(rr)
        return View(ap, self.tid, rects)


class Arena:
    def __init__(self, k, handle, nbytes, gran=None):
        self.h = handle
        self.nbytes = nbytes
        self.tid = k._new_tid()
        self.top = 0
        self.peak = 0
        self.gran = gran

    def mark(self):
        return self.top

    def release(self, m):
        self.top = m

    def tile(self, shape, dtype, off=None):
        isz = 4 if dtype == F32 else 2
        n = 1
        for d in shape[1:]:
            n *= d
        nb = (n * isz + 31) // 32 * 32
        if off is None:
            off = self.top
            self.top += nb
            self.peak = max(self.peak, self.top)
            assert self.top <= self.nbytes, ("arena overflow", self.top, self.nbytes)
        ap = self.h[:, off // 4:(off + nb) // 4]
        if dtype != F32:
            ap = ap.bitcast(dtype)
        ap = ap[0:shape[0], 0:n]
        if len(shape) == 3:
            ap = ap.rearrange("p (a b) -> p a b", a=shape[1], b=shape[2])
        elif len(shape) == 4:
            ap = ap.rearrange("p (a b c) -> p a b c", a=shape[1], b=shape[2], c=shape[3])
        return Tile(ap, shape, self.tid, base=off, isz=isz, gran=self.gran)


class Ins:
    __slots__ = ("eng", "fn", "deps", "signal", "count", "is_dma", "sem", "_barred", "epoch", "idx")

    def __init__(self, eng, fn, is_dma=False):
        self.epoch = 0
        self.idx = 0
        self.eng = eng
        self.fn = fn
        self.deps = []
        self.signal = False
        self.count = 0
        self.is_dma = is_dma
        self.sem = None
        self._barred = False


ENGS = ("pe", "act", "dve", "pool", "sp")


class K:
    def __init__(self, nc):
        self.nc = nc
        self.streams = {e: [] for e in ENGS}
        self.acc = {}
        self._tid = 0
        self.dma_keys = {}
        self.n_ins = 0
        self.psum_tid = None
        self.epoch = 0

    def _new_tid(self):
        self._tid += 1
        return self._tid

    @staticmethod
    def _ov(a, b):
        return a[0] < b[1] and b[0] < a[1] and a[2] < b[3] and b[2] < a[3]

    @staticmethod
    def _cover(a, b):
        return a[0] <= b[0] and a[1] >= b[1] and a[2] <= b[2] and a[3] >= b[3]

    def _track(self, ins, reads, writes):
        deps = {}
        ov = self._ov
        for v in reads:
            if v is None or v.tid is None:
                continue
            lst = self.acc.setdefault(v.tid, [])
            for r in v.rects:
                for (rect, j, kind) in lst:
                    if kind == "w" and ov(rect, r):
                        deps[id(j)] = j
        for v in writes:
            if v is None or v.tid is None:
                continue
            lst = self.acc.setdefault(v.tid, [])
            for r in v.rects:
                for (rect, j, kind) in lst:
                    if ov(rect, r):
                        deps[id(j)] = j
        for v in reads:
            if v is None or v.tid is None:
                continue
            lst = self.acc[v.tid]
            for r in v.rects:
                if not ins.is_dma:
                    lst[:] = [e for e in lst if not (e[2] == "r" and e[0] == r and e[1].eng == ins.eng and not e[1].is_dma)]
                lst.append((r, ins, "r"))
        for v in writes:
            if v is None or v.tid is None:
                continue
            lst = self.acc[v.tid]
            for r in v.rects:
                lst[:] = [e for e in lst if not self._cover(r, e[0])]
                lst.append((r, ins, "w"))
        deps.pop(id(ins), None)
        for j in deps.values():
            if j.eng == "pe" and ins.eng == "pe" and not j.is_dma and not ins.is_dma:
                continue
            ins.deps.append(j)
            j.signal = True

    def new_epoch(self):
        self.epoch += 1

    def op(self, eng, fn, reads, writes):
        ins = Ins(eng, fn)
        ins.epoch = self.epoch
        self.n_ins += 1
        if self.psum_tid is not None:
            pr = [v for v in reads if v is not None and v.tid == self.psum_tid]
            if pr:
                reads = [v for v in reads if v is None or v.tid != self.psum_tid]
                writes = list(writes) + pr
        self._track(ins, reads, writes)
        self.streams[eng].append(ins)
        return ins

    def dma(self, queue, out, in_, key, **kw):
        ins = Ins(queue, None, is_dma=True)
        self.n_ins += 1
        kk = self.dma_keys.setdefault(key, [0])
        kk[0] += 16
        ins.sem = key
        ins.count = kk[0]
        oa, ia = out.ap, in_.ap
        ins.fn = lambda e: e.dma_start(out=oa, in_=ia, **kw)
        self._track(ins, [in_], [out])
        self.streams[queue].append(ins)
        return ins

    def barrier(self):
        lasts = []
        for e in ENGS:
            comp = [i for i in self.streams[e] if not i.is_dma and i.fn is not None]
            if comp:
                lasts.append(comp[-1])
            for i in self.streams[e]:
                if i.is_dma and not i._barred:
                    i._barred = True
                    lasts.append(i)
        for e in ENGS:
            ins = Ins(e, None)
            ins.epoch = self.epoch
            for j in lasts:
                if j.eng == e and not j.is_dma and e == "pe":
                    continue
                ins.deps.append(j)
                j.signal = True
            self.streams[e].append(ins)
        self.acc = {}

    def mm(self, out, lhsT, rhs, start=True, stop=True, **kw):
        o, l, r = out.ap, lhsT.ap, rhs.ap
        return self.op("pe", lambda e: e.matmul(o, l, r, start=start, stop=stop, **kw), [lhsT, rhs], [out])

    def transpose(self, out, in_, ident):
        o, i, d = out.ap, in_.ap, ident.ap
        return self.op("pe", lambda e: e.transpose(o, i, d), [in_, ident], [out])

    def act(self, out, in_, func, bias=None, scale=1.0, accum_out=None):
        o, i = out.ap, in_.ap
        kw = {}
        rd = [in_]
        wr = [out]
        if bias is not None:
            if isinstance(bias, View):
                kw["bias"] = bias.ap
                rd.append(bias)
            else:
                kw["bias"] = bias
        if isinstance(scale, View):
            kw["scale"] = scale.ap
            rd.append(scale)
        else:
            kw["scale"] = scale
        if accum_out is not None:
            kw["accum_out"] = accum_out.ap
            wr.append(accum_out)
        return self.op("act", lambda e: e.activation(o, i, func, **kw), rd, wr)

    def tt(self, eng, out, in0, in1, op):
        o, a, b = out.ap, in0.ap, in1.ap
        return self.op(eng, lambda e: e.tensor_tensor(o, a, b, op), [in0, in1], [out])

    def ts(self, eng, out, in0, s1, s2=None, op0=ALU.mult, op1=None):
        o, a = out.ap, in0.ap
        rd = [in0]
        if isinstance(s1, View):
            rd.append(s1)
            s1 = s1.ap
        if isinstance(s2, View):
            rd.append(s2)
            s2 = s2.ap
        kw = {}
        if op1 is not None:
            kw["op1"] = op1
        return self.op(eng, lambda e: e.tensor_scalar(o, a, s1, s2, op0, **kw), rd, [out])

    def stt(self, out, in0, scalar, in1, op0, op1):
        o, a, b = out.ap, in0.ap, in1.ap
        rd = [in0, in1]
        if isinstance(scalar, View):
            rd.append(scalar)
            scalar = scalar.ap
        return self.op("dve", lambda e: e.scalar_tensor_tensor(o, a, scalar, b, op0, op1), rd, [out])

    def copy(self, eng, out, in_):
        o, i = out.ap, in_.ap
        if eng == "act":
            return self.op("act", lambda e: e.copy(o, i), [in_], [out])
        return self.op(eng, lambda e: e.tensor_copy(o, i), [in_], [out])

    def scan(self, out, d0, d1, initial, op0, op1):
        o, a, b = out.ap, d0.ap, d1.ap
        rd = [d0, d1]
        if isinstance(initial, View):
            rd.append(initial)
            initial = initial.ap
        return self.op("dve", lambda e: e.tensor_tensor_scan(o, a, b, initial, op0, op1), rd, [out])

    def recip(self, out, in_):
        o, i = out.ap, in_.ap
        return self.op("dve", lambda e: e.reciprocal(o, i), [in_], [out])

    def memset(self, eng, out, val):
        o = out.ap
        return self.op(eng, lambda e: e.memset(o, val), [], [out])

    def emit(self):
        nc = self.nc
        with ExitStack() as st:
            esem = {}
            dsem = {}
            for key in self.dma_keys:
                dsem[key] = st.enter_context(nc.semaphore("d_%d" % len(dsem)))
            self.max_count = 0
            for e in ENGS:
                for n_, ins in enumerate(self.streams[e]):
                    ins.idx = n_
                    ins.signal = False
            for e in ENGS:
                for ins in self.streams[e]:
                    best = {}
                    keep = []
                    for j in ins.deps:
                        if j.is_dma:
                            keep.append(j)
                            continue
                        kk_ = (j.eng, j.epoch)
                        if kk_ not in best or best[kk_].idx < j.idx:
                            best[kk_] = j
                    for j in best.values():
                        j.signal = True
                        keep.append(j)
                    ins.deps = keep
            for e in ENGS:
                c = {}
                for ins in self.streams[e]:
                    if ins.is_dma:
                        continue
                    if ins.signal:
                        ek = (e, ins.epoch)
                        if ek not in esem:
                            esem[ek] = st.enter_context(nc.semaphore("s_%s_%d" % ek))
                        c[ek] = c.get(ek, 0) + 1
                        ins.count = c[ek]
                        self.max_count = max(self.max_count, ins.count)
            block = st.enter_context(nc.Block())
            engobj = {"pe": block.tensor, "act": block.scalar, "dve": block.vector, "pool": block.gpsimd, "sp": block.sync}

            def mk(ename):
                def body(eng):
                    waited = {}
                    for ins in self.streams[ename]:
                        need = {}
                        for j in ins.deps:
                            if j.is_dma:
                                s, c, wk = dsem[j.sem], j.count, ("d", j.sem)
                            else:
                                s, c, wk = esem[(j.eng, j.epoch)], j.count, ("e", j.eng, j.epoch)
                            if c > need.get(wk, (None, 0))[1]:
                                need[wk] = (s, c)
                        for wk, (s, c) in need.items():
                            if waited.get(wk, 0) >= c:
                                continue
                            waited[wk] = c
                            eng.wait_ge(s, c)
                        if ins.fn is None:
                            continue
                        r = ins.fn(eng)
                        if ins.is_dma:
                            r.then_inc(dsem[ins.sem], 16)
                        elif ins.signal:
                            r.then_inc(esem[(ename, ins.epoch)], 1)
                return body

            for e in ENGS:
                engobj[e](mk(e))


CF_E = 0
CF_SBM = 2048
CF_HGM = 2176
CF_RST = 2240
CF_N = 2752


def host_consts():
    cf = np.zeros((128, CF_N), np.float32)
    k = np.arange(128)[:, None].astype(np.float64)
    q = np.arange(128)[None, :].astype(np.float64)
    for h in range(8):
        slope = 2.0 ** (-(h + 1))
        cur = np.where(q >= k, np.exp(-slope * (q - k)), 0.0)
        prev = np.where(k > q, np.exp(-slope * (q + 128 - k)), 0.0)
        cf[:, CF_E + h * 256:CF_E + h * 256 + 128] = cur
        cf[:, CF_E + h * 256 + 128:CF_E + h * 256 + 256] = prev
    t = np.arange(128)[:, None]
    j = np.arange(128)[None, :]
    cf[:, CF_SBM:CF_SBM + 128] = (j < t)
    s = (np.arange(128) % 64)[:, None]
    tt = np.arange(64)[None, :]
    cf[:, CF_HGM:CF_HGM + 64] = (tt >= s)
    cf[:, CF_RST:CF_RST + 512] = (np.arange(512) % 64 != 0)[None, :]
    cb = np.zeros((128, 384), np.float32)
    cb[:, 0:128] = np.eye(128)
    cb[:, 128:256] = 1.0
    cb[:, 256:384] = np.where(j < t, 0.0, -30000.0)
    return cf, cb.astype(ml_dtypes.bfloat16)


VC_NG = 0
VC_FG = 48
VC_HN = 56
VC_CW = 57
VC_LB = 69
VC_SK = 77
VC_N = 88


def host_vecs(norm_g, final_g, hgrn_norm_g, conv_w, hgrn_lb_logits, attn_sinks):
    v = np.zeros((128, VC_N), np.float32)
    v[:, VC_NG:VC_NG + 48] = norm_g.reshape(6, 8, 128).transpose(2, 0, 1).reshape(128, 48)
    v[:, VC_FG:VC_FG + 8] = final_g.reshape(8, 128).T
    v[:, VC_HN] = hgrn_norm_g.reshape(128)
    v[:, VC_CW:VC_CW + 12] = conv_w.reshape(3, 4, 128).transpose(2, 1, 0).reshape(128, 12)
    v[:, VC_LB:VC_LB + 8] = hgrn_lb_logits.reshape(2, 4, 128).transpose(2, 1, 0).reshape(128, 8)
    v[:, VC_SK:VC_SK + 8] = attn_sinks.reshape(1, 8)
    return v


ALL_STAGES = ("ffn00", "attn", "ffn01", "ffn10", "rec", "ffn11", "final")
W_SHAPES = {"attn_w_in": [D, 2304], "attn_w_out": [D, D], "rec_w_in": [D, 3584], "rec_w_out": [D, D]}
for _l in range(2):
    for _f in range(2):
        W_SHAPES["ffn_w_in_%d%d" % (_l, _f)] = [D, 2 * DFF]
        W_SHAPES["ffn_w_out_%d%d" % (_l, _f)] = [DFF, D]


class Prog:
    def __init__(self, stages=ALL_STAGES):
        self.stages = tuple(stages)
        nc = self.nc = bass.Bass("TRN2", target_bir_lowering=False)
        k = self.k = K(nc)

        def din(name, shape, dt=F32):
            return Tile(nc.dram_tensor(name, list(shape), dt, kind="ExternalInput"), shape, None)

        self.xT = din("xT", [128, 8, S])
        self.cf = din("cf", [128, CF_N])
        self.cb = din("cb", [128, 384], BF16)
        self.vecs = din("vecs", [128, VC_N])
        self.wd = {}
        outh = nc.dram_tensor("outT", [128, 8, S], F32, kind="ExternalOutput")
        self.outT = Tile(outh, [128, 8, S], k._new_tid())

    def w(self, name):
        if name not in self.wd:
            self.wd[name] = self.nc.dram_tensor(name, W_SHAPES[name], F32, kind="ExternalInput")
        return self.wd[name]

    def build(self):
        nc, k = self.nc, self.k
        with ExitStack() as st:
            SB_BYTES = 212480
            ah = st.enter_context(nc.sbuf_tensor("arena", [128, SB_BYTES // 4], F32))
            ph = st.enter_context(nc.psum_tensor("psum", [128, 4096], F32))
            self.A = A = Arena(k, ah, SB_BYTES)
            self.PS = Arena(k, ph, 16384, gran=(2048, 32))
            k.psum_tid = self.PS.tid
            self.hT = A.tile([128, 8, S], F32)
            self.cfT = A.tile([128, CF_N], F32)
            self.cbT = A.tile([128, 384], BF16)
            self.vc = A.tile([128, VC_N], F32)
            self.epsT = A.tile([128, 2], F32)
            self.wring = [(A.tile([128, 8, 256], BF16), A.tile([128, 8, 256], BF16)) for _ in range(3)]
            self.ident = Tile(self.cbT.h[:, 0:128], [128, 128], self.cbT.tid, self.cbT.base, 2)
            self.ones = Tile(self.cbT.h[:, 128:256], [128, 128], self.cbT.tid, self.cbT.base + 256, 2)
            self.mbT = Tile(self.cbT.h[:, 256:384], [128, 128], self.cbT.tid, self.cbT.base + 512, 2)
            self.wslab_issued = 0

            for t in range(NT):
                k.dma("sp", self.hT[:, :, t * 512:(t + 1) * 512], self.xT[:, :, t * 512:(t + 1) * 512], ("x", t))
            k.dma("sp", self.cfT[:], self.cf[:], "const_cf")
            k.dma("sp", self.cbT[:], self.cb[:], "const_cb")
            k.dma("sp", self.vc[:], self.vecs[:], "const_vc")
            k.memset("pool", self.epsT[:, 0:1], EPS)
            k.memset("pool", self.epsT[:, 1:2], 1.0)

            self.body()

            k.barrier()
            k.emit()
        return nc

    def psb(self, b, shape=(128, 512), dtype=F32):
        return self.PS.tile(list(shape), dtype, off=b * 2048)

    def body(self):
        st = self.stages
        self.ffn_seq = [x for x in st if x.startswith("ffn")]
        for i, name in enumerate(st):
            self.k.new_epoch()
            if name.startswith("ffn"):
                fidx = self.ffn_seq.index(name)
                nxt_is_ffn = (i + 1 < len(st) and st[i + 1].startswith("ffn"))
                self.ffn_issue(limit=fidx * 11 + 3)
                self.ffn(int(name[3]), int(name[4]), fidx, next_limit=fidx * 11 + (14 if nxt_is_ffn else 11))
            elif name == "attn":
                self.attn()
            elif name == "rec":
                self.rec()
            elif name == "final":
                self.final_norm()
        if "final" not in st:
            self.dump_h()

    def dump_h(self):
        k = self.k
        for t in range(NT):
            k.dma("sp", self.outT[:, :, t * 512:(t + 1) * 512], self.hT[:, :, t * 512:(t + 1) * 512], ("out", t))

    def rms_norm_T(self, gcol0, xnT):
        k, A = self.k, self.A
        m = A.mark()
        sq = A.tile([128, 8, 512], BF16)
        rs = [A.tile([128, 512], F32) for _ in range(2)]
        ssq = self.psb(6)
        for t in range(NT):
            ts_ = slice(t * 512, (t + 1) * 512)
            k.act(sq[:], self.hT[:, :, ts_], AF.Square)
            for kt in range(KT):
                k.mm(ssq[:], self.ones[:], sq[:, kt, :], start=(kt == 0), stop=(kt == KT - 1))
            r = rs[t % 2]
            k.act(r[:], ssq[:], AF.Sqrt, bias=self.epsT[:, 0:1], scale=1.0 / D)
            k.recip(r[:], r[:])
            for kt in range(KT):
                k.stt(xnT[:, kt, ts_], self.hT[:, kt, ts_], self.vc[:, gcol0 + kt:gcol0 + kt + 1], r[:], ALU.mult, ALU.mult)
        A.release(m)

    def ffn_issue(self, limit):
        k = self.k
        while self.wslab_issued < limit and self.wslab_issued < 11 * len(self.ffn_seq):
            gi = self.wslab_issued
            fi, s = divmod(gi, 11)
            slot = self.wring[gi % 3]
            w = self.w("ffn_w_in_" + self.ffn_seq[fi][3:5]).ap().rearrange("(kt p) c -> p kt c", p=128)
            k.dma("pool", slot[0][:], View(w[:, :, s * 256:(s + 1) * 256], None, []), ("wr", gi % 3, 0))
            k.dma("pool", slot[1][:], View(w[:, :, DFF + s * 256:DFF + (s + 1) * 256], None, []), ("wr", gi % 3, 1))
            self.wslab_issued += 1

    def ffn(self, l, f, fi, next_limit):
        k, A = self.k, self.A
        m = A.mark()
        xnT = A.tile([128, 8, S], BF16)
        parts = [6, 6, 6, 4]
        actT = A.tile([128, 6, S], BF16)
        wout = A.tile([128, 6, D], BF16)
        sg = [A.tile([128, 512], F32) for _ in range(3)]
        self.rms_norm_T(VC_NG + (l * 3 + (0 if f == 0 else 2)) * 8, xnT)
        wo_dram = self.w("ffn_w_out_%d%d" % (l, f)).ap().rearrange("(j p) c -> p j c", p=128)
        pg = [self.psb(0), self.psb(1), self.psb(2)]
        pu = [self.psb(3), self.psb(4), self.psb(5)]
        po = [self.psb(0), self.psb(1), self.psb(2), self.psb(3)]
        j0 = 0
        cnt = 0
        cnt2 = 0
        for nj in parts:
            k.dma("pool", wout[:, 0:nj, :], View(wo_dram[:, j0:j0 + nj, :], None, []), "wout")
            for sl in range(nj // 2):
                s = j0 // 2 + sl
                gi = fi * 11 + s
                slot = self.wring[gi % 3]
                for jj in range(2):
                    jl = sl * 2 + jj
                    for t in range(NT):
                        ts_ = slice(t * 512, (t + 1) * 512)
                        g_ps, u_ps, sgt = pg[cnt % 3], pu[cnt % 3], sg[cnt % 3]
                        cnt += 1
                        for kt in range(KT):
                            k.mm(g_ps[:], slot[0][:, kt, jj * 128:(jj + 1) * 128], xnT[:, kt, ts_], start=(kt == 0), stop=(kt == KT - 1))
                        for kt in range(KT):
                            k.mm(u_ps[:], slot[1][:, kt, jj * 128:(jj + 1) * 128], xnT[:, kt, ts_], start=(kt == 0), stop=(kt == KT - 1))
                        k.act(sgt[:], g_ps[:], AF.Silu)
                        k.tt("dve", actT[:, jl, ts_], sgt[:], u_ps[:], ALU.mult)
                self.ffn_issue(min(gi + 4, next_limit))
            for mo in range(KT):
                for t in range(NT):
                    ts_ = slice(t * 512, (t + 1) * 512)
                    o_ps = po[cnt2 % 4]
                    cnt2 += 1
                    for jl in range(nj):
                        k.mm(o_ps[:], wout[:, jl, mo * 128:(mo + 1) * 128], actT[:, jl, ts_], start=(jl == 0), stop=(jl == nj - 1))
                    k.stt(self.hT[:, mo, ts_], o_ps[:], 0.5, self.hT[:, mo, ts_], ALU.mult, ALU.add)
            j0 += nj
        A.release(m)

    def final_norm(self):
        k, A = self.k, self.A
        m = A.mark()
        sq = A.tile([128, 8, 512], BF16)
        rs = [A.tile([128, 512], F32) for _ in range(2)]
        yo = [A.tile([128, 8, 512], F32) for _ in range(2)]
        ssq = self.psb(6)
        for t in range(NT):
            ts_ = slice(t * 512, (t + 1) * 512)
            k.act(sq[:], self.hT[:, :, ts_], AF.Square)
            for kt in range(KT):
                k.mm(ssq[:], self.ones[:], sq[:, kt, :], start=(kt == 0), stop=(kt == KT - 1))
            r = rs[t % 2]
            k.act(r[:], ssq[:], AF.Sqrt, bias=self.epsT[:, 0:1], scale=1.0 / D)
            k.recip(r[:], r[:])
            y = yo[t % 2]
            for kt in range(KT):
                k.stt(y[:, kt, :], self.hT[:, kt, ts_], self.vc[:, VC_FG + kt:VC_FG + kt + 1], r[:], ALU.mult, ALU.mult)
            k.dma("sp", self.outT[:, :, ts_], y[:], ("out", t % 2))
        A.release(m)

    def sub(self, t, c0, c1, shape):
        ap = t.h[:, c0:c1]
        if len(shape) == 3:
            ap = ap.rearrange("p (a b) -> p a b", a=shape[1], b=shape[2])
        return Tile(ap, shape, t.tid, base=t.base + c0 * t.isz, isz=t.isz)

    def proj_fm(self, wt, c0, xnT, dest, ci, scale=None):
        k = self.k
        for t in range(NT):
            ts_ = slice(t * 512, (t + 1) * 512)
            ps = self.psb(self.pcnt % 4)
            self.pcnt += 1
            for kt in range(KT):
                k.mm(ps[:], wt[:, kt, c0:c0 + 128], xnT[:, kt, ts_], start=(kt == 0), stop=(kt == KT - 1))
            if self.pcnt % 2 == 0:
                k.act(dest[:, ci, ts_], ps[:], AF.Copy, scale=(1.0 if scale is None else scale))
            else:
                k.ts("dve", dest[:, ci, ts_], ps[:], (1.0 if scale is None else scale), None, op0=ALU.mult)

    def proj_tm(self, wt, c0, n, xnT, dest, d0):
        k = self.k
        for blk in range(NB):
            ps = self.PS.tile([128, n], F32, off=(self.pcnt % 4) * 2048)
            self.pcnt += 1
            for kt in range(KT):
                k.mm(ps[:], xnT[:, kt, blk * 128:(blk + 1) * 128], wt[:, kt, c0:c0 + n], start=(kt == 0), stop=(kt == KT - 1))
            if self.pcnt % 2 == 0:
                k.copy("act", dest[:, blk, d0:d0 + n], ps[:])
            else:
                k.copy("dve", dest[:, blk, d0:d0 + n], ps[:])

    def wslab(self, wname, ncols_total, c0, hs, keyid):
        w = self.w(wname).ap().rearrange("(kt p) c -> p kt c", p=128)
        self.k.dma("pool", hs[:], View(w[:, :, c0:c0 + 256], None, []), ("wr", keyid // 2, keyid % 2))

    def attn(self):
        k, A = self.k, self.A
        base = A.mark()
        self.pcnt = 0
        hs = [self.wring[i // 2][i % 2] for i in range(6)]
        W0 = self.wring[0][0].base
        xnT = A.tile([128, 8, S], BF16, off=base)
        oTa = A.tile([128, 4, S], BF16, off=base + 32768)
        oTb = A.tile([128, 4, S], BF16, off=base + 0)
        R1 = base + 49152
        A.top = R1 + 53312
        A.peak = max(A.peak, A.top)
        assert A.top <= A.nbytes, A.top
        qaT = A.tile([128, 4, S], BF16, off=R1)
        kkT = A.tile([128, 2, S], BF16, off=R1 + 16384)
        va = A.tile([128, NB, 128], BF16, off=R1 + 24576)
        PT = [A.tile([128, 8, 256], BF16, off=R1 + 28672 + i * 4096) for i in range(3)]
        eS = [A.tile([128, 4, 256], F32, off=R1 + 40960 + i * 4096) for i in range(2)]
        den = [A.tile([128, 4, 128], F32, off=R1 + 49152 + i * 2048) for i in range(2)]
        es = A.tile([128, 8], F32, off=R1 + 53248)
        esk2 = A.tile([128, 4], F32, off=R1 + 53280)
        qbT = A.tile([128, 4, S], BF16, off=R1)
        kbT = A.tile([128, 4, S], BF16, off=R1 + 16384)
        vb = A.tile([128, NB, 512], BF16, off=R1 + 32768)
        Etab = self.sub(self.cfT, CF_E, CF_E + 2048, [128, 8, 256])
        sbm = self.sub(self.cfT, CF_SBM, CF_SBM + 128, [128, 128])

        top_save = A.top
        A.top = R1 + 28672
        self.rms_norm_T(VC_NG + 1 * 8, xnT)
        A.top = top_save

        wname = "attn_w_in"
        wv = self.w(wname).ap().rearrange("(kt p) c -> p kt c", p=128)
        self.wslab(wname, 2304, 0, hs[0], 0)
        self.wslab(wname, 2304, 256, hs[1], 1)
        self.wslab(wname, 2304, 512, hs[2], 2)
        for i, (src, dst) in enumerate([(512, 0), (512, 64), (576, 128), (576, 192)]):
            k.dma("pool", hs[3][:, :, dst:dst + 64], View(wv[:, :, src:src + 64], None, []), ("kk", i))
        for ci in range(4):
            self.proj_fm(hs[ci // 2], (ci % 2) * 128, xnT, qaT, ci, scale=0.125)
        self.wslab(wname, 2304, 768, hs[4], 4)
        self.wslab(wname, 2304, 1024, hs[5], 5)
        self.proj_fm(hs[3], 0, xnT, kkT, 0)
        self.proj_fm(hs[3], 128, xnT, kkT, 1)
        self.proj_tm(hs[2], 128, 128, xnT, va, 0)
        self.wslab(wname, 2304, 1280, hs[0], 0)
        self.wslab(wname, 2304, 1536, hs[1], 1)
        self.wslab(wname, 2304, 1792, hs[2], 2)
        self.wslab(wname, 2304, 2048, hs[3], 3)

        cut = int(os.environ.get('ATTN_CUT', '99'))
        if cut <= 1:
            A.release(base)
            return
        k.act(es[:], self.vc[:, VC_SK:VC_SK + 8], AF.Exp)
        k.copy("dve", esk2[0:64, :], es[0:64, 0:8:2])
        k.copy("dve", esk2[64:128, :], es[64:128, 1:8:2])
        esb = View(esk2.h[:, :].unsqueeze(2).to_broadcast([128, 4, 128]), esk2.tid, esk2[:].rects)
        def swa_scores(kb):
            nq = 256 if kb < NB - 1 else 128
            pt = PT[kb % 3]
            for hg in range(2):
                Sps = self.PS.tile([128, 4, 256], F32, off=(hg * 2) * 2048)
                for i in range(4):
                    h = 4 * hg + i
                    c, po = h // 2, (h % 2) * 64
                    j = (i % 2) * 2 + (i // 2)
                    k.mm(Sps[:, j, 0:nq], kkT[po:po + 64, hg, kb * 128:(kb + 1) * 128],
                         qaT[po:po + 64, c, kb * 128:kb * 128 + nq])
                e = eS[hg]
                k.act(e[:, :, 0:nq], Sps[:, :, 0:nq], AF.Exp)
                for p in range(2):
                    hsel = slice(4 * hg + p, 4 * hg + p + 3, 2)
                    k.tt("dve", pt[:, hsel, 0:nq], e[:, 2 * p:2 * p + 2, 0:nq], Etab[:, hsel, 0:nq], ALU.mult)

        def swa_out(kb):
            pt = PT[kb % 3]
            O = self.PS.tile([128, 4, 128], F32, off=(4 + 2 * (kb % 2)) * 2048)
            Dn = self.PS.tile([128, 4, 128], F32, off=(5 + 2 * (kb % 2)) * 2048)
            ptp = PT[(kb - 1) % 3]
            for h in range(8):
                g, c, po = h // 4, h // 2, (h % 2) * 64
                k.mm(O[po:po + 64, c, :], va[:, kb, g * 64:(g + 1) * 64], pt[:, h, 0:128], start=True, stop=(kb == 0), tile_position=(0, po))
                if kb > 0:
                    k.mm(O[po:po + 64, c, :], va[:, kb - 1, g * 64:(g + 1) * 64], ptp[:, h, 128:256], start=False, stop=True, tile_position=(0, po))
                k.mm(Dn[po:po + 64, c, :], self.ones[:, 0:64], pt[:, h, 0:128], start=True, stop=(kb == 0), tile_position=(0, po))
                if kb > 0:
                    k.mm(Dn[po:po + 64, c, :], self.ones[:, 0:64], ptp[:, h, 128:256], start=False, stop=True, tile_position=(0, po))
            dn = den[kb % 2]
            k.tt("dve", dn[:], Dn[:], esb, ALU.add)
            k.recip(dn[:], dn[:])
            k.tt("dve", oTa[:, :, kb * 128:(kb + 1) * 128], O[:], dn[:], ALU.mult)

        swa_scores(0)
        for kb in range(NB):
            if kb + 1 < NB:
                swa_scores(kb + 1)
            swa_out(kb)

        if cut <= 2:
            A.release(base)
            return
        self.proj_fm(hs[4], 0, xnT, qbT, 0, scale=0.125)
        self.proj_fm(hs[4], 128, xnT, qbT, 1, scale=0.125)
        self.proj_fm(hs[5], 0, xnT, qbT, 2, scale=0.125)
        self.proj_fm(hs[5], 128, xnT, qbT, 3, scale=0.125)
        self.proj_fm(hs[0], 0, xnT, kbT, 0)
        self.proj_fm(hs[0], 128, xnT, kbT, 1)
        self.proj_fm(hs[1], 0, xnT, kbT, 2)
        self.proj_fm(hs[1], 128, xnT, kbT, 3)
        self.proj_tm(hs[2], 0, 256, xnT, vb, 0)
        self.proj_tm(hs[3], 0, 256, xnT, vb, 256)

        if cut <= 3:
            A.release(base)
            return
        k.new_epoch()
        X1 = base + 16384

        class Bf:
            pass
        bA, bB = Bf(), Bf()
        bB.e32 = A.tile([128, 2048], F32, off=W0)
        bB.spP = A.tile([128, 2049], F32, off=W0 + 8192)
        bB.xw = A.tile([128, 2048], BF16, off=W0 + 16416)
        bA.xw = A.tile([128, 1024], BF16, off=W0 + 20512)
        bA.npt = A.tile([128, 1], F32, off=W0 + 22560)
        bB.npt = A.tile([128, 1], F32, off=W0 + 22592)
        bB.WT = A.tile([128, 16, 128], BF16, off=X1)
        bA.e32 = A.tile([128, 1024], F32, off=X1 + 4096)
        bA.spP = A.tile([128, 1025], F32, off=X1 + 8192)
        bA.WT = A.tile([128, 8, 128], BF16, off=X1 + 12320)
        k.memset("pool", bA.spP[:, 0:1], 0.0)
        k.memset("pool", bB.spP[:, 0:1], 0.0)
        Zt = self.PS.tile([128, 2048], F32, off=0)
        TPb = [self.PS.tile([128, 8, 128], BF16, off=(4 + i) * 2048) for i in range(2)]
        Ob = [self.PS.tile([128, 128], F32, off=(6 + i) * 2048) for i in range(2)]
        order = []
        for i in range(8):
            order += [i, 15 - i]
        blocks = [(h, qb) for h in range(8) for qb in order]
        nblk = len(blocks)
        st = {"o": 0}

        def geo(i):
            h, qb = blocks[i]
            return h, qb, h // 2, (h % 2) * 64, (bA if qb < 8 else bB), (qb + 1) * 128, slice(qb * 128, (qb + 1) * 128)

        def s_qk(i):
            h, qb, c, po, B, Lk, qs = geo(i)
            nkc = (Lk + 511) // 512
            for kc in range(nkc):
                w = min(512, Lk - kc * 512)
                k.mm(Zt[:, kc * 512:kc * 512 + w], qbT[po:po + 64, c, qs], kbT[po:po + 64, c, kc * 512:kc * 512 + w],
                     start=True, stop=(kc < nkc - 1))
            k.mm(Zt[:, qb * 128:Lk], self.ident[:], self.mbT[:], start=False, stop=True)

        def s_exp(i):
            h, qb, c, po, B, Lk, qs = geo(i)
            k.act(B.e32[:, 0:Lk], Zt[:, 0:Lk], AF.Exp)

        def s_ln(i):
            h, qb, c, po, B, Lk, qs = geo(i)
            k.act(B.spP[:, 1:Lk + 1], B.e32[:, 0:Lk], AF.Ln, bias=self.epsT[:, 1:2])

        def s_scan(i):
            h, qb, c, po, B, Lk, qs = geo(i)
            k.scan(B.spP[:, 1:Lk + 1], B.spP[:, 1:Lk + 1], B.spP[:, 1:Lk + 1], 0.0, ALU.add, ALU.max)
            k.ts("dve", B.npt[:], B.spP[:, Lk:Lk + 1], -1.0, None, op0=ALU.mult)

        def s_exp2(i):
            h, qb, c, po, B, Lk, qs = geo(i)
            k.act(B.xw[:, 0:Lk], B.spP[:, 0:Lk], AF.Exp, bias=B.npt[:])

        def s_mult(i):
            h, qb, c, po, B, Lk, qs = geo(i)
            k.tt("dve", B.xw[:, 0:Lk], B.e32[:, 0:Lk], B.xw[:, 0:Lk], ALU.mult)

        def s_tr(i):
            h, qb, c, po, B, Lk, qs = geo(i)
            for g, sb0 in enumerate(range(0, qb + 1, 8)):
                n = min(8, qb + 1 - sb0)
                for ii in range(n):
                    k.transpose(TPb[g][:, ii, :], B.xw[:, (sb0 + ii) * 128:(sb0 + ii + 1) * 128], self.ident[:])

        def s_evac(i, eng_sel):
            h, qb, c, po, B, Lk, qs = geo(i)
            for g, sb0 in enumerate(range(0, qb + 1, 8)):
                n = min(8, qb + 1 - sb0)
                eng = "dve" if g == 0 else "act"
                if eng == eng_sel:
                    k.copy(eng, B.WT[:, sb0:sb0 + n, :], TPb[g][:, 0:n, :])

        def s_pv(i):
            h, qb, c, po, B, Lk, qs = geo(i)
            o = Ob[st["o"] % 2]
            st["o"] += 1
            for sb in range(qb + 1):
                k.mm(o[po:po + 64, :], vb[:, sb, c * 128 + po:c * 128 + po + 64], B.WT[:, sb, :],
                     start=(sb == 0), stop=(sb == qb), tile_position=(0, po))
            k.copy("act", oTb[po:po + 64, c, qs], o[po:po + 64, :])

        ok = lambda j: 0 <= j < nblk
        s_qk(0)
        for i in range(-1, nblk + 1):
            if ok(i + 1):
                s_exp(i + 1)
            if ok(i + 2):
                s_qk(i + 2)
            if ok(i):
                s_scan(i)
            if ok(i + 1):
                s_ln(i + 1)
            if ok(i):
                s_exp2(i)
            if ok(i - 1):
                s_evac(i - 1, "act")
                s_evac(i - 1, "dve")
                s_pv(i - 1)
            if ok(i):
                s_mult(i)
                s_tr(i)

        if cut <= 4:
            A.release(base)
            return
        self.out_proj("attn_w_out", [(oTa, c) for c in range(4)] + [(oTb, c) for c in range(4)])
        A.release(base)

    def out_proj(self, wname, srcs):
        k = self.k
        nslot = self.wslab_issued % 3
        use = [sl for sl in range(3) if sl != nslot][:2]
        hs = [self.wring[sl][hf] for sl in use for hf in range(2)]
        keys = [("wr", sl, hf) for sl in use for hf in range(2)]
        wo = self.w(wname).ap().rearrange("(c p) m -> p c m", p=128)
        for i in range(4):
            k.dma("pool", hs[i][:], View(wo[:, :, i * 256:(i + 1) * 256], None, []), keys[i])
        self.ffn_issue(self.wslab_issued + 1)
        for mo in range(KT):
            hsl, c0 = hs[mo // 2], (mo % 2) * 128
            for t in range(NT):
                ts_ = slice(t * 512, (t + 1) * 512)
                ps = self.psb(self.pcnt % 4)
                self.pcnt += 1
                for c in range(8):
                    src, ci = srcs[c]
                    k.mm(ps[:], hsl[:, c, c0:c0 + 128], src[:, ci, ts_], start=(c == 0), stop=(c == 7))
                k.tt("dve", self.hT[:, mo, ts_], ps[:], self.hT[:, mo, ts_], ALU.add)

    def rec(self):
        k, A = self.k, self.A
        base = A.mark()
        self.pcnt = 0
        xnT = A.tile([128, 8, S], BF16)
        oT = A.tile([128, 8, S], BF16)
        self.rms_norm_T(VC_NG + (3 + 1) * 8, xnT)
        W0 = self.wring[0][0].base
        slots = [A.tile([128, 8, 512], BF16, off=W0 + i * 8192) for i in range(3)]
        slots.append(A.tile([128, 8, 512], BF16))
        wv = self.w("rec_w_in").ap().rearrange("(kt p) c -> p kt c", p=128)

        def load(slot_i, j, c0):
            k.dma("pool", slots[slot_i][:, :, j * 128:(j + 1) * 128], View(wv[:, :, c0:c0 + 128], None, []), ("wq", slot_i, j))

        sm = A.tile([128, 48], F32)
        L0 = self.vc[:, VC_LB:VC_LB + 8:2]
        L1 = self.vc[:, VC_LB + 1:VC_LB + 8:2]
        mx, d0, d1, ssum, lb, oml = [sm[:, i * 4:(i + 1) * 4] for i in range(6)]
        k.tt("dve", mx, L0, L1, ALU.max)
        k.tt("dve", d0, L0, mx, ALU.subtract)
        k.tt("dve", d1, L1, mx, ALU.subtract)
        k.act(d0, d0, AF.Exp)
        k.act(d1, d1, AF.Exp)
        k.tt("dve", ssum, d0, d1, ALU.add)
        k.recip(ssum, ssum)
        k.tt("dve", d0, d0, ssum, ALU.mult)
        k.tt("dve", d1, d1, ssum, ALU.mult)
        k.tt("dve", d1, d0, d1, ALU.add)
        k.tt("dve", lb, d1, d0, ALU.subtract)
        k.ts("dve", oml, lb, -1.0, 1.0, op0=ALU.mult, op1=ALU.add)

        mconv = A.mark()
        xbuf = A.tile([128, S + 2], F32)
        gbs = A.tile([128, S], F32)
        ybuf = A.tile([128, S], F32)
        usb = [A.tile([128, 512], F32) for _ in range(2)]
        k.memset("pool", xbuf[:, 0:2], 0.0)
        for c in range(3):
            for j in range(3):
                load(c, j, 2048 + j * 512 + c * 128)
        bcnt = 0
        for c in range(4):
            sl = slots[c % 4 if c < 3 else 3]
            if c == 3:
                for j in range(3):
                    load(3, j, 2048 + j * 512 + c * 128)
            for t in range(NT):
                ts_ = slice(t * 512, (t + 1) * 512)
                pss = []
                for j in range(3):
                    ps = self.psb(bcnt % 8)
                    bcnt += 1
                    for kt in range(KT):
                        k.mm(ps[:], sl[:, kt, j * 128:(j + 1) * 128], xnT[:, kt, ts_], start=(kt == 0), stop=(kt == KT - 1))
                    pss.append(ps)
                u = usb[t % 2]
                k.copy("act", u[:], pss[2][:])
                k.tt("dve", xbuf[:, 2 + t * 512:2 + (t + 1) * 512], pss[1][:], u[:], ALU.mult)
                k.copy("act", gbs[:, ts_], pss[0][:])
            cw = VC_CW + c * 3
            k.ts("dve", ybuf[:], xbuf[:, 2:S + 2], self.vc[:, cw + 2:cw + 3], None, op0=ALU.mult)
            k.stt(ybuf[:], xbuf[:, 1:S + 1], self.vc[:, cw + 1:cw + 2], ybuf[:], ALU.mult, ALU.add)
            k.stt(ybuf[:], xbuf[:, 0:S], self.vc[:, cw:cw + 1], ybuf[:], ALU.mult, ALU.add)
            k.tt("dve", oT[:, 4 + c, :], gbs[:], ybuf[:], ALU.mult)
        A.release(mconv)

        for h in range(4):
            for j in range(4):
                load(h, j, j * 512 + h * 128)
        f32t = lambda: A.tile([128, 512], F32)
        T_sq, T_b1, T_b2, T_b3 = f32t(), f32t(), f32t(), f32t()
        rs2 = T_sq
        T_kh = A.tile([128, 512], BF16)
        T_sq2 = T_kh

        class Hd:
            pass
        hd = []
        for h in range(4):
            H = Hd()
            H.qt = A.tile([128, 512], BF16)
            H.kt = A.tile([128, 512], BF16)
            H.khT = A.tile([128, 4, 128], BF16)
            H.vt = A.tile([128, 4, 128], BF16)
            H.AT = A.tile([128, 4, 64], BF16)
            H.sgl = A.tile([128, 512], BF16)
            H.eGl = A.tile([128, 8], F32)
            H.sf = A.tile([128, 128], F32)
            H.sb = [A.tile([128, 128], BF16) for _ in range(2)]
            k.memset("pool", H.sf[:], 0.0)
            hd.append(H)
        rst = self.sub(self.cfT, CF_RST, CF_RST + 512, [128, 512])
        hgm = self.sub(self.cfT, CF_HGM, CF_HGM + 64, [128, 64])
        hgmb = View(hgm.h[:, :].unsqueeze(1).to_broadcast([128, 4, 64]), hgm.tid, hgm[:].rects)
        ops = [self.psb(h) for h in range(4)]
        ups = [self.PS.tile([128, 128], F32, off=(4 + i) * 2048) for i in range(2)]
        pw = 0
        ucnt = 0

        def pwbank(shape=(128, 512), dtype=F32):
            nonlocal pw
            pw += 1
            return self.PS.tile(list(shape), dtype, off=(6 + pw % 2) * 2048)

        for t in range(NT):
            ts_ = slice(t * 512, (t + 1) * 512)
            for h in range(4):
                H, sl = hd[h], slots[h]

                def proj(j):
                    ps = pwbank()
                    for kt in range(KT):
                        k.mm(ps[:], sl[:, kt, j * 128:(j + 1) * 128], xnT[:, kt, ts_], start=(kt == 0), stop=(kt == KT - 1))
                    return ps
                psf = proj(1)
                k.act(T_b1[:], psf[:], AF.Exp, scale=-1.0)
                k.ts("dve", T_b1[:], T_b1[:], 1.0, None, op0=ALU.add)
                k.recip(T_b1[:], T_b1[:])
                k.ts("dve", T_b1[:], T_b1[:], sm[:, 20 + h:21 + h], sm[:, 16 + h:17 + h], op0=ALU.mult, op1=ALU.add)
                k.ts("dve", T_b2[:], T_b1[:], -1.0, 1.0, op0=ALU.mult, op1=ALU.add)
                k.act(T_b1[:], T_b1[:], AF.Ln)
                k.scan(T_b3[:], rst[:], T_b1[:], 0.0, ALU.mult, ALU.add)
                k.act(T_b1[:], T_b3[:], AF.Exp)
                k.act(T_b3[:], T_b3[:], AF.Exp, scale=-1.0)
                psq = proj(0)
                k.act(T_sq[:], psq[:], AF.Silu)
                psg = proj(3)
                k.act(H.sgl[:], psg[:], AF.Silu)
                k.copy("dve", H.eGl[:], T_b1[:, 63:512:64])
                k.tt("dve", H.qt[:], T_sq[:], T_b1[:], ALU.mult)
                k.tt("dve", T_b2[:], T_b2[:], T_b3[:], ALU.mult)
                k.copy("dve", H.kt[:], T_b2[:])
                eb = View(H.eGl.h[:, :].unsqueeze(2).to_broadcast([128, 8, 64]), H.eGl.tid, H.eGl[:].rects)
                khv = View(T_kh.h[:, :].rearrange("p (a b) -> p a b", a=8, b=64), T_kh.tid, T_kh[:].rects)
                b2v = View(T_b2.h[:, :].rearrange("p (a b) -> p a b", a=8, b=64), T_b2.tid, T_b2[:].rects)
                k.tt("dve", khv, b2v, eb, ALU.mult)
                for blk in range(4):
                    ps = pwbank((128, 128))
                    tok = slice(t * 512 + blk * 128, t * 512 + (blk + 1) * 128)
                    for kt in range(KT):
                        k.mm(ps[:], xnT[:, kt, tok], sl[:, kt, 256:384], start=(kt == 0), stop=(kt == KT - 1))
                    k.copy("act", H.vt[:, blk, :], ps[:])
                tp = pwbank((128, 4, 128), BF16)
                for blk in range(4):
                    k.transpose(tp[:, blk, :], T_kh[:, blk * 128:(blk + 1) * 128], self.ident[:])
                k.copy("dve", H.khT[:], tp[:])
                ap_ = pwbank((128, 4, 64))
                for blk in range(4):
                    for half in range(2):
                        t0 = (2 * blk + half) * 64
                        k.mm(ap_[half * 64:(half + 1) * 64, blk, :], H.kt[:, t0:t0 + 64], H.qt[:, t0:t0 + 64], tile_position=(0, half * 64))
                k.tt("dve", H.AT[:], ap_[:], hgmb, ALU.mult)
            for cc in range(8):
                blk, half = cc // 2, cc % 2
                p0 = half * 64
                t0 = cc * 64
                gc = t * 8 + cc
                for h in range(4):
                    H = hd[h]
                    k.mm(ops[h][:, t0:t0 + 64], H.vt[p0:p0 + 64, blk, :], H.AT[p0:p0 + 64, blk, :], start=True, stop=(gc == 0))
                    if gc > 0:
                        k.mm(ops[h][:, t0:t0 + 64], H.sb[gc % 2][:], H.qt[:, t0:t0 + 64], start=False, stop=True)
                    if gc < 31:
                        u = ups[ucnt % 2]
                        ucnt += 1
                        k.mm(u[:], H.khT[p0:p0 + 64, blk, :], H.vt[p0:p0 + 64, blk, :])
                        k.stt(H.sf[:], H.sf[:], H.eGl[:, cc:cc + 1], u[:], ALU.mult, ALU.add)
                        k.copy("act", H.sb[(gc + 1) % 2][:], H.sf[:])
            for h in range(4):
                H = hd[h]
                k.act(T_sq2[:], ops[h][:], AF.Square)
                ssq = pwbank()
                k.mm(ssq[:], self.ones[:], T_sq2[:])
                k.act(rs2[:], ssq[:], AF.Ln, bias=self.epsT[:, 0:1], scale=1.0 / 128)
                k.act(rs2[:], rs2[:], AF.Exp, scale=-0.5)
                k.stt(rs2[:], ops[h][:], self.vc[:, VC_HN:VC_HN + 1], rs2[:], ALU.mult, ALU.mult)
                k.tt("dve", oT[:, h, ts_], rs2[:], H.sgl[:], ALU.mult)

        self.out_proj("rec_w_out", [(oT, c) for c in range(8)])
        A.release(base)


_CONSTS = None


def kernel(x, norm_g, ffn_w_in, ffn_w_out, attn_w_in, attn_sinks, attn_w_out, rec_w_in,
           hgrn_lb_logits, hgrn_norm_g, conv_w, rec_w_out, final_g, _stages=ALL_STAGES, _xT=None):
    global _CONSTS
    if _CONSTS is None:
        _CONSTS = host_consts()
    cf, cb = _CONSTS
    f32 = lambda a: np.ascontiguousarray(np.asarray(a, dtype=np.float32))
    x = f32(x)
    vecs = host_vecs(f32(norm_g), f32(final_g), f32(hgrn_norm_g), f32(conv_w), f32(hgrn_lb_logits), f32(attn_sinks))
    prog = Prog(stages=_stages)
    nc = prog.build()
    allw = {"attn_w_in": lambda: f32(attn_w_in)[0], "attn_w_out": lambda: f32(attn_w_out)[0],
            "rec_w_in": lambda: f32(rec_w_in)[0], "rec_w_out": lambda: f32(rec_w_out)[0]}
    for l in range(2):
        for f in range(2):
            allw["ffn_w_in_%d%d" % (l, f)] = (lambda l=l, f=f: f32(np.asarray(ffn_w_in)[l, f]))
            allw["ffn_w_out_%d%d" % (l, f)] = (lambda l=l, f=f: f32(np.asarray(ffn_w_out)[l, f]))
    shared = {"cf": cf, "cb": cb, "vecs": vecs}
    for name in prog.wd:
        shared[name] = allw[name]()
    in_maps = []
    for b in range(8):
        src = x[b] if _xT is None else f32(_xT[b])
        xT = np.ascontiguousarray(src.T.reshape(8, 128, S).transpose(1, 0, 2))
        in_maps.append({"xT": xT, **shared})
    res = run_bass_kernel_spmd(nc, in_maps, core_ids=list(range(8)))
    out = np.empty((8, S, D), np.float32)
    for b in range(8):
        oT = np.asarray(res.results[b]["outT"])
        out[b] = oT.transpose(2, 1, 0).reshape(S, D)
    return out
```
